# Optimizing a Trainium2 kernel written in Bass

```python
import math
import jax, jax.numpy as jnp
from jax import lax
import numpy as np

D_MODEL = 2048
BATCH = 4
SEQ = 2048
DEPTH = 4
DEC_BATCH = 128
DEC_SEQ = 1
PAST_LEN = 16384
PAGE_SIZE = 128

BRANCH_WIDTH = D_MODEL // 2
N_BRANCH = 3
HG_HEADS = 8
HG_DK = BRANCH_WIDTH // HG_HEADS
HG_DV = BRANCH_WIDTH // HG_HEADS
HG_CHUNK = 64
HG_LB_FLOOR = 1e-30
RET_HEADS = 4
RET_DK = BRANCH_WIDTH // RET_HEADS
RET_DV = BRANCH_WIDTH // RET_HEADS
RET_CHUNK = 128
ROPE_BASE = 10000.0
ML_HEADS = 4
ML_DK = BRANCH_WIDTH // ML_HEADS
ML_DV = BRANCH_WIDTH // ML_HEADS
ML_CHUNK = 128
ML_FGATE_BIAS_LO = 3.0
ML_FGATE_BIAS_HI = 6.0
NEG_LARGE = -1e30
NORM_EPS = 1e-6
LN_EPS = 1e-5
DEEPNORM_ALPHA = (2 * DEPTH) ** 0.25
DEEPNORM_BETA = (8 * DEPTH) ** -0.25
COL_SIZES = (BRANCH_WIDTH, BRANCH_WIDTH, BRANCH_WIDTH, BRANCH_WIDTH,
             BRANCH_WIDTH, BRANCH_WIDTH, BRANCH_WIDTH, BRANCH_WIDTH,
             BRANCH_WIDTH, BRANCH_WIDTH, BRANCH_WIDTH, BRANCH_WIDTH, BRANCH_WIDTH,
             ML_HEADS, ML_HEADS,
             N_BRANCH * D_MODEL)
D_IN = sum(COL_SIZES)
SPLIT_POINTS = tuple(int(v) for v in np.cumsum(COL_SIZES)[:-1])

kernel_name = 'hybrid_hgrn2_retention_mlstm_step'

F32 = jnp.float32


def _to_chunks(a, c):
    b, l = a.shape[:2]
    return jnp.moveaxis(a.reshape((b, l // c, c) + a.shape[2:]), 1, 0)


def _from_chunks(a):
    a = jnp.moveaxis(a, 0, 1)
    return a.reshape((a.shape[0], a.shape[1] * a.shape[2]) + a.shape[3:])


def _causal_mask(c):
    return jnp.tril(jnp.ones((c, c), dtype=bool))


def _masked_exp(mask, a):
    return jnp.where(mask, jnp.exp(jnp.where(mask, a, 0.0)), 0.0)


def _hgrn2_scan(q, k, v, log_f, s0):
    c = math.gcd(q.shape[1], HG_CHUNK)
    mask = _causal_mask(c)[None, :, :, None, None]

    def step(s, inp):
        qc, kc, vc, gc = inp
        bc = jnp.cumsum(gc, axis=1)
        o = jnp.einsum('bthk,bhkv->bthv', qc * jnp.exp(bc), s)
        dec = _masked_exp(mask, bc[:, :, None] - bc[:, None])
        a = jnp.einsum('bthk,bshk,btshk->btsh', qc, kc, dec)
        o = o + jnp.einsum('btsh,bshv->bthv', a, vc)
        last = bc[:, -1]
        k_dec = kc * jnp.exp(last[:, None] - bc)
        s_new = jnp.exp(last)[..., None] * s + jnp.einsum('bshk,bshv->bhkv', k_dec, vc)
        return s_new, o

    s_fin, o = lax.scan(step, s0, tuple(_to_chunks(t, c) for t in (q, k, v, log_f)))
    return _from_chunks(o), s_fin


def _retention_scan(q, k, v, s0):
    c = math.gcd(q.shape[1], RET_CHUNK)
    log_gamma = jnp.log(1.0 - 2.0 ** (-5.0 - jnp.arange(RET_HEADS, dtype=F32)))
    t_idx = jnp.arange(c, dtype=F32)
    rel = t_idx[:, None] - t_idx[None, :]
    intra = _masked_exp(_causal_mask(c)[..., None], rel[..., None] * log_gamma)
    inter = jnp.exp((t_idx[:, None] + 1.0) * log_gamma)
    to_end = jnp.exp((c - 1.0 - t_idx[:, None]) * log_gamma)
    chunk_dec = jnp.exp(c * log_gamma)

    def step(s, inp):
        qc, kc, vc = inp
        o = jnp.einsum('bthk,bhkv->bthv', qc * inter[None, :, :, None], s)
        a = jnp.einsum('bthk,bshk->btsh', qc, kc) * intra[None]
        o = o + jnp.einsum('btsh,bshv->bthv', a, vc)
        s_new = chunk_dec[:, None, None] * s + jnp.einsum('bshk,bshv->bhkv', kc * to_end[None, :, :, None], vc)
        return s_new, o

    s_fin, o = lax.scan(step, s0, tuple(_to_chunks(t, c) for t in (q, k, v)))
    return _from_chunks(o), s_fin


def _mlstm_scan(q, k, v, i_pre, log_f, c0, n0, m0):
    c = math.gcd(q.shape[1], ML_CHUNK)
    mask = _causal_mask(c)[None, :, :, None]

    def step(carry, inp):
        cs, ns, ms = carry
        qc, kc, vc, ic, fc = inp
        b = jnp.cumsum(fc, axis=1)
        log_w = jnp.where(mask, b[:, :, None] - b[:, None] + ic[:, None], NEG_LARGE)
        log_inter = ms[:, None] + b
        m = jnp.maximum(log_inter, jnp.max(log_w, axis=2))
        w = _masked_exp(mask, log_w - m[:, :, None])
        sc = jnp.exp(log_inter - m)
        qk = jnp.einsum('bthk,bshk->btsh', qc, kc) * w
        num = jnp.einsum('btsh,bshv->bthv', qk, vc) + sc[..., None] * jnp.einsum('bthk,bhkv->bthv', qc, cs)
        den = jnp.sum(qk, axis=2) + sc * jnp.einsum('bthk,bhk->bth', qc, ns)
        h = num / jnp.maximum(jnp.abs(den), jnp.exp(-m))[..., None]
        m_last = m[:, -1]
        dec_state = jnp.exp(ms + b[:, -1] - m_last)
        kw = kc * jnp.exp(b[:, -1:] - b + ic - m_last[:, None])[..., None]
        cs_new = dec_state[..., None, None] * cs + jnp.einsum('bshk,bshv->bhkv', kw, vc)
        ns_new = dec_state[..., None] * ns + jnp.sum(kw, axis=1)
        return (cs_new, ns_new, m_last), h

    (c_fin, n_fin, m_fin), h = lax.scan(step, (c0, n0, m0), tuple(_to_chunks(t, c) for t in (q, k, v, i_pre, log_f)))
    return _from_chunks(h), c_fin, n_fin, m_fin


def _rotary(x, pos):
    d = x.shape[-1]
    theta = 1.0 / (ROPE_BASE ** jnp.linspace(0.0, 1.0, d // 2, dtype=F32))
    ang = pos.astype(F32)[:, None] * theta[None]
    cos = jnp.cos(ang)[None, :, None]
    sin = jnp.sin(ang)[None, :, None]
    x1 = x[..., 0::2]
    x2 = x[..., 1::2]
    return jnp.stack([x1 * cos - x2 * sin, x1 * sin + x2 * cos], axis=-1).reshape(x.shape)


def _head_rmsnorm(o, gain):
    b, l, h, d = o.shape
    y = o * lax.rsqrt(jnp.mean(o * o, axis=-1, keepdims=True) + NORM_EPS)
    return y.reshape(b, l, h * d) * gain.astype(F32)


def _head_layernorm(o, gain):
    b, l, h, d = o.shape
    mu = jnp.mean(o, axis=-1, keepdims=True)
    var = jnp.mean(jnp.square(o - mu), axis=-1, keepdims=True)
    y = (o - mu) * lax.rsqrt(var + NORM_EPS)
    return y.reshape(b, l, h * d) * gain.astype(F32)


def _layer(x, pos, state, params, lower_bound):
    hg_s, ret_s, ml_c, ml_n, ml_m = state
    w_in, hg_gain, ret_gain, ml_gain, b_i, b_f, w_br, w_o, ln_g, ln_b = params
    bsz, seq_len, _ = x.shape
    (hq, hf, hi, hz, rq, rk, rv, rz, mq, mk, mv, mz, mo, mi, mf, gates) = jnp.split(x @ w_in, SPLIT_POINTS, axis=-1)

    def heads(a, n):
        return a.astype(F32).reshape(bsz, seq_len, n, -1)

    lb = lower_bound.astype(F32)
    log_f = jnp.logaddexp(jnp.log(jnp.maximum(lb, HG_LB_FLOOR)),
                          jnp.log1p(-lb) + jax.nn.log_sigmoid(hf.astype(F32)))
    o_a, hg_new = _hgrn2_scan(heads(jax.nn.silu(hq.astype(F32)), HG_HEADS), heads(-jnp.expm1(log_f), HG_HEADS),
                              heads(hi, HG_HEADS), heads(log_f, HG_HEADS), hg_s.astype(F32))
    y_a = _head_rmsnorm(o_a, hg_gain) * jax.nn.silu(hz.astype(F32))

    q_r = _rotary(heads(rq, RET_HEADS), pos)
    k_r = _rotary(heads(rk, RET_HEADS), pos) * (RET_DK ** -0.5)
    o_b, ret_new = _retention_scan(q_r, k_r, heads(rv, RET_HEADS), ret_s.astype(F32))
    y_b = _head_layernorm(o_b, ret_gain) * jax.nn.silu(rz.astype(F32))

    i_pre = mi.astype(F32) + b_i.astype(F32)
    log_fg = jax.nn.log_sigmoid(mf.astype(F32) + b_f.astype(F32))
    h_c, mc_new, mn_new, mm_new = _mlstm_scan(heads(mq, ML_HEADS), heads(mk, ML_HEADS) * (ML_DK ** -0.5),
                                              heads(mv, ML_HEADS), i_pre, log_fg,
                                              ml_c.astype(F32), ml_n.astype(F32), ml_m.astype(F32))
    h_c = jax.nn.sigmoid(heads(mo, ML_HEADS)) * h_c
    y_c = _head_layernorm(h_c, ml_gain) * jax.nn.silu(mz.astype(F32))

    ys = jnp.stack([y_a, y_b, y_c], axis=2).astype(x.dtype)
    branch = jnp.einsum('blnc,ncd->blnd', ys, w_br)
    merged = jnp.sum(jax.nn.sigmoid(gates.reshape(bsz, seq_len, N_BRANCH, D_MODEL)) * branch, axis=2)
    out = merged @ w_o

    h = DEEPNORM_ALPHA * x.astype(F32) + out.astype(F32)
    mu = jnp.mean(h, axis=-1, keepdims=True)
    var = jnp.mean(jnp.square(h - mu), axis=-1, keepdims=True)
    x_new = (h - mu) * lax.rsqrt(var + LN_EPS) * ln_g.astype(F32) + ln_b.astype(F32)
    return x_new.astype(x.dtype), (hg_new, ret_new, mc_new, mn_new, mm_new)


def _run_group(x, pos, states, lower_bounds, params):
    new_states = []
    for layer in range(DEPTH):
        layer_params = tuple(p[layer] for p in params)
        layer_state = tuple(s[layer] for s in states)
        x, st = _layer(x, pos, layer_state, layer_params, lower_bounds[layer])
        new_states.append(st)
    stacked = tuple(jnp.stack(group) for group in zip(*new_states))
    return x, stacked


def setup_inputs(seed: int = 0) -> dict:
    key = jax.random.key(seed)
    ks = jax.random.split(key, 20)
    nrm = jax.random.normal
    x_prompt = nrm(ks[0], (BATCH, SEQ, D_MODEL), F32)
    x_sample = nrm(ks[1], (DEC_BATCH, DEC_SEQ, D_MODEL), F32)
    state_hgrn = 0.5 * nrm(ks[2], (DEPTH, DEC_BATCH, HG_HEADS, HG_DK, HG_DV), F32)
    state_ret = 0.5 * nrm(ks[3], (DEPTH, DEC_BATCH, RET_HEADS, RET_DK, RET_DV), F32)
    state_mlstm_c = 0.5 * nrm(ks[4], (DEPTH, DEC_BATCH, ML_HEADS, ML_DK, ML_DV), F32)
    state_mlstm_n = 0.5 * nrm(ks[5], (DEPTH, DEC_BATCH, ML_HEADS, ML_DK), F32)
    state_mlstm_m = 4.0 * jax.random.uniform(ks[6], (DEPTH, DEC_BATCH, ML_HEADS), F32)
    w_in = nrm(ks[7], (DEPTH, D_MODEL, D_IN), F32) * (D_MODEL ** -0.5)
    hgrn_lb_logits = 1.0 + 0.1 * nrm(ks[8], (DEPTH, BRANCH_WIDTH), F32)
    hgrn_norm = 1.0 + 0.02 * nrm(ks[9], (DEPTH, BRANCH_WIDTH), F32)
    ret_norm = 1.0 + 0.02 * nrm(ks[10], (DEPTH, BRANCH_WIDTH), F32)
    mlstm_norm = 1.0 + 0.02 * nrm(ks[11], (DEPTH, BRANCH_WIDTH), F32)
    mlstm_b_i = 0.1 * nrm(ks[12], (DEPTH, ML_HEADS), F32)
    mlstm_b_f = (jnp.linspace(ML_FGATE_BIAS_LO, ML_FGATE_BIAS_HI, ML_HEADS, dtype=F32)[None]
                 + 0.1 * nrm(ks[13], (DEPTH, ML_HEADS), F32))
    w_branch = nrm(ks[14], (DEPTH, N_BRANCH, BRANCH_WIDTH, D_MODEL), F32) * (BRANCH_WIDTH ** -0.5) * DEEPNORM_BETA
    w_out = nrm(ks[15], (DEPTH, D_MODEL, D_MODEL), F32) * (D_MODEL ** -0.5) * DEEPNORM_BETA
    ln_g = 1.0 + 0.02 * nrm(ks[16], (DEPTH, D_MODEL), F32)
    ln_b = 0.02 * nrm(ks[17], (DEPTH, D_MODEL), F32)
    return {'x_prompt': x_prompt, 'x_sample': x_sample,
            'state_hgrn': state_hgrn, 'state_ret': state_ret, 'state_mlstm_c': state_mlstm_c,
            'state_mlstm_n': state_mlstm_n, 'state_mlstm_m': state_mlstm_m,
            'w_in': w_in, 'hgrn_lb_logits': hgrn_lb_logits, 'hgrn_norm': hgrn_norm, 'ret_norm': ret_norm,
            'mlstm_norm': mlstm_norm, 'mlstm_b_i': mlstm_b_i, 'mlstm_b_f': mlstm_b_f,
            'w_branch': w_branch, 'w_out': w_out, 'ln_g': ln_g, 'ln_b': ln_b}


def reference(x_prompt, x_sample, state_hgrn, state_ret, state_mlstm_c, state_mlstm_n, state_mlstm_m,
              w_in, hgrn_lb_logits, hgrn_norm, ret_norm, mlstm_norm, mlstm_b_i, mlstm_b_f,
              w_branch, w_out, ln_g, ln_b):
    lb_sm = jax.nn.softmax(hgrn_lb_logits.astype(F32), axis=0)
    lower_bounds = jnp.cumsum(lb_sm, axis=0) - lb_sm[0]
    params = (w_in, hgrn_norm, ret_norm, mlstm_norm, mlstm_b_i, mlstm_b_f, w_branch, w_out, ln_g, ln_b)

    bp, lp = x_prompt.shape[0], x_prompt.shape[1]
    zero_states = (jnp.zeros((DEPTH, bp, HG_HEADS, HG_DK, HG_DV), F32),
                   jnp.zeros((DEPTH, bp, RET_HEADS, RET_DK, RET_DV), F32),
                   jnp.zeros((DEPTH, bp, ML_HEADS, ML_DK, ML_DV), F32),
                   jnp.zeros((DEPTH, bp, ML_HEADS, ML_DK), F32),
                   jnp.zeros((DEPTH, bp, ML_HEADS), F32))
    y_prompt, (hg_p, ret_p, mc_p, mn_p, mm_p) = _run_group(
        x_prompt, jnp.arange(lp), zero_states, lower_bounds, params)

    pos_sample = PAST_LEN + jnp.arange(x_sample.shape[1])
    y_sample, (hg_s, ret_s, mc_s, mn_s, mm_s) = _run_group(
        x_sample, pos_sample, (state_hgrn, state_ret, state_mlstm_c, state_mlstm_n, state_mlstm_m),
        lower_bounds, params)

    return (y_prompt, y_sample, hg_p, ret_p, mc_p, mn_p, mm_p, hg_s, ret_s, mc_s, mn_s, mm_s)
```

```python
import math
import numpy as np
from contextlib import ExitStack
import concourse.bass as bass
import concourse.mybir as mybir
from concourse.bass_utils import run_bass_kernel_spmd

F32 = mybir.dt.float32
BF16 = mybir.dt.bfloat16
ALU = mybir.AluOpType
AF = mybir.ActivationFunctionType
AX = mybir.AxisListType

D = 2048
DIN = 19464
BW = 1024
TP = 512
NS = 16
NW = TP + NS
PAST = 16384
GATE0 = 13320


class Sem:
    def __init__(self, h):
        self.h = h
        self.v = 0


class Res:
    __slots__ = ("w", "r")

    def __init__(self):
        self.w = None
        self.r = {}


class Eng:
    def __init__(self, name, eng, sem, inorder=False):
        self.name = name
        self.eng = eng
        self.sem = sem
        self.waited = {}
        self.inorder = inorder


class FW:
    def __init__(self, nc, ctx):
        self.nc = nc
        self.ctx = ctx
        self.allsems = []
        self.pe = Eng("pe", nc.tensor, self.sem("pe"), inorder=True)
        self.act = Eng("act", nc.scalar, self.sem("act"))
        self.dve = Eng("dve", nc.vector, self.sem("dve"))
        self.pool = Eng("pool", nc.gpsimd, self.sem("pool"))
        self.sp = Eng("sp", nc.sync, self.sem("sp"))
        self.res = {}
        self.nops = 0

    def sem(self, name):
        s = Sem(self.ctx.enter_context(self.nc.semaphore(name)))
        self.allsems.append(s)
        return s

    def sb(self, name, shape, dt):
        t = self.ctx.enter_context(self.nc.sbuf_tensor(name, list(shape), dt))
        self.res[name] = Res()
        return t

    def R(self, name):
        if name not in self.res:
            self.res[name] = Res()
        return self.res[name]

    def op(self, E, fn, reads=(), writes=(), sem=None):
        reads = [self.R(r) if isinstance(r, str) else r for r in reads]
        writes = [self.R(w) if isinstance(w, str) else w for w in writes]
        deps = []
        for r in reads:
            if r.w is not None:
                deps.append(r.w)
        for w in writes:
            if w.w is not None:
                deps.append(w.w)
            deps.extend(w.r.items())
        for (s, v) in deps:
            if s is E.sem and E.inorder:
                continue
            if E.waited.get(s, 0) >= v:
                continue
            E.eng.wait_ge(s.h, v)
            E.waited[s] = v
        ins = fn()
        self.nops += 1
        if sem is not None:
            S = sem
            S.v += 16
            ins.then_inc(S.h, 16)
        else:
            S = E.sem
            S.v += 1
            ins.then_inc(S.h, 1)
        for r in reads:
            if r.r.get(S, 0) < S.v:
                r.r[S] = S.v
        for w in writes:
            w.w = (S, S.v)
            w.r = {}
        return ins

    def fence(self, sem, names):
        for n in names:
            self.R(n).w = (sem, sem.v)


def build(L, NP):
    nc = bass.Bass("TRN2", target_bir_lowering=False)
    NTOK = NP * TP
    NT = NTOK + NS
    NCH = TP // 128
    ALPHA = float((2 * L) ** 0.25)

    def din(name, shape):
        return nc.dram_tensor(name, list(shape), F32, kind="ExternalInput").ap()

    def dout(name, shape):
        return nc.dram_tensor(name, list(shape), F32, kind="ExternalOutput").ap()

    xp = din("xp", [NTOK, D]); xs = din("xs", [NS, D])
    st_h = din("st_h", [L, NS, 8, 128, 128]); st_r = din("st_r", [L, NS, 4, 256, 256]); st_c = din("st_c", [L, NS, 4, 256, 256])
    st_n = din("st_n", [L, NS, 1024]); st_m = din("st_m", [L, NS, 4])
    w_in = din("w_in", [L, D, DIN]); w_br = din("w_br", [L, 3, BW, D]); w_o = din("w_o", [L, D, D])
    pvec = din("pvec", [L, 8192]); bif = din("bif", [4, 2, L])
    c_id = din("c_id", [128, 128]); c_perm = din("c_perm", [128, 128]); c_tri64 = din("c_tri64", [128, 64]); c_tri128 = din("c_tri128", [128, 128])
    c_rmask = din("c_rmask", [128, 4, 128]); c_rG = din("c_rG", [128, 4, 128]); c_rkd = din("c_rkd", [128, 4])
    c_reset = din("c_reset", [128, TP]); c_cos = din("c_cos", [128, 2, NT]); c_sin = din("c_sin", [128, 2, NT]); c_sel = din("c_sel", [4, 4, 128])
    c_i16 = din("c_i16", [16, 16])
    yp = dout("yp", [NTOK, D]); ys = dout("ys", [NS, D])
    o_hp = dout("o_hp", [L, 8, 128, 128]); o_rp = dout("o_rp", [L, 4, 256, 256]); o_cp = dout("o_cp", [L, 4, 256, 256])
    o_np = dout("o_np", [L, 4, 256]); o_mp = dout("o_mp", [L, 4])
    o_hs = dout("o_hs", [L, NS, 8, 128, 128]); o_rs = dout("o_rs", [L, NS, 4, 256, 256]); o_cs = dout("o_cs", [L, NS, 4, 256, 256])
    o_ns = dout("o_ns", [L, NS, 1024]); o_ms = dout("o_ms", [L, NS, 4])
    XF = nc.dram_tensor("xf", [2, 16, 128, NT], F32, kind="Internal").ap()
    GAM = [1.0 - 2.0 ** (-5.0 - h) for h in range(4)]

    with ExitStack() as ctx:
        fw = FW(nc, ctx)
        pe, act, dve, pool, sp = fw.pe, fw.act, fw.dve, fw.pool, fw.sp
        sb = fw.sb
        s_misc = fw.sem("misc")
        s_w = [fw.sem(f"w{i}") for i in range(2)]
        s_x = fw.sem("x"); s_xr = [fw.sem(f"xr{i}") for i in range(2)]; s_xw = [fw.sem(f"xw{i}") for i in range(2)]
        s_st = fw.sem("st"); s_tab = fw.sem("tab"); s_wg = fw.sem("wg"); s_wb = fw.sem("wb")
        osems = {}

        def osem(name):
            if name not in osems:
                osems[name] = fw.sem("o_" + name)
            return osems[name]

        def dma(E, out, in_, reads, writes, sem, slow=False):
            q = {"sp": nc.sync, "pool": nc.gpsimd, "act": nc.scalar}[E.name]
            if slow:
                return fw.op(E, lambda: q.dma_start(out=out, in_=in_, allow_slow_non_contiguous=True), reads=reads, writes=writes, sem=sem)
            return fw.op(E, lambda: q.dma_start(out=out, in_=in_), reads=reads, writes=writes, sem=sem)

        def MM(out, lhsT, rhs, start, stop, reads, writes, skip=False):
            if skip:
                return fw.op(pe, lambda: nc.tensor.matmul(out, lhsT=lhsT, rhs=rhs, start=start, stop=stop, skip_group_check=True), reads=reads, writes=writes)
            return fw.op(pe, lambda: nc.tensor.matmul(out, lhsT=lhsT, rhs=rhs, start=start, stop=stop), reads=reads, writes=writes)

        def TR(out, in_, ident, reads, writes):
            return fw.op(pe, lambda: nc.tensor.transpose(out, in_, ident), reads=reads, writes=writes)

        def ACT(out, in_, func, reads, writes, scale=1.0, bias=None):
            if bias is None:
                return fw.op(act, lambda: nc.scalar.activation(out=out, in_=in_, func=func, scale=scale), reads=reads, writes=writes)
            return fw.op(act, lambda: nc.scalar.activation(out=out, in_=in_, func=func, scale=scale, bias=bias), reads=reads, writes=writes)

        def TT(E, out, in0, in1, op, reads, writes):
            e = nc.vector if E is dve else nc.gpsimd
            return fw.op(E, lambda: e.tensor_tensor(out=out, in0=in0, in1=in1, op=op), reads=reads, writes=writes)

        def TS(E, out, in0, s1, s2, op0, op1, reads, writes):
            e = nc.vector if E is dve else nc.gpsimd
            if op1 is None:
                return fw.op(E, lambda: e.tensor_scalar(out=out, in0=in0, scalar1=s1, scalar2=None, op0=op0), reads=reads, writes=writes)
            return fw.op(E, lambda: e.tensor_scalar(out=out, in0=in0, scalar1=s1, scalar2=s2, op0=op0, op1=op1), reads=reads, writes=writes)

        def STT(out, in0, scalar, in1, op0, op1, reads, writes):
            return fw.op(dve, lambda: nc.vector.scalar_tensor_tensor(out=out, in0=in0, scalar=scalar, in1=in1, op0=op0, op1=op1), reads=reads, writes=writes)

        def CP(E, out, in_, reads, writes):
            if E is act:
                return fw.op(act, lambda: nc.scalar.copy(out=out, in_=in_), reads=reads, writes=writes)
            e = nc.vector if E is dve else nc.gpsimd
            return fw.op(E, lambda: e.tensor_copy(out=out, in_=in_), reads=reads, writes=writes)

        def MSET(E, ap, val, writes):
            e = nc.vector if E is dve else nc.gpsimd
            return fw.op(E, lambda: e.memset(ap, val), writes=writes)

        def SCAN(out, d0, d1, init, op0, op1, reads, writes):
            return fw.op(dve, lambda: nc.vector.tensor_tensor_scan(out=out, data0=d0, data1=d1, initial=init, op0=op0, op1=op1), reads=reads, writes=writes)

        def RECIP(out, in_, reads, writes):
            return fw.op(dve, lambda: nc.vector.reciprocal(out=out, in_=in_), reads=reads, writes=writes)

        ident32 = sb("ident32", [128, 128], F32); ident16 = sb("ident16", [128, 128], BF16); perm32 = sb("perm32", [128, 128], F32)
        tri64 = sb("tri64", [128, 64], F32); tri128 = sb("tri128", [128, 128], F32)
        rmask = sb("rmask", [128, 4, 128], F32); rG = sb("rG", [128, 4, 128], F32); rkd = sb("rkd", [128, 4], F32)
        reset = sb("reset", [128, TP], F32); sel = sb("sel", [4, 4, 128], F32); i16 = sb("i16", [16, 16], F32)
        ones16 = sb("ones16", [128, 128], BF16); ones32 = sb("ones32", [128, 128], F32); onesrow = sb("onesrow", [4, TP], F32)
        epsn = sb("epsn", [128, 1], F32); epsl = sb("epsl", [128, 1], F32)
        BIF = sb("BIF", [4, 2, L], F32)
        PT32 = sb("PT32", [L, 1024], F32); PAR = sb("PAR", [128, 64, L], F32)
        lbv = sb("lbv", [128, 8, L], F32); oml = sb("oml", [128, 8, L], F32); lbe = sb("lbe", [128, 8, L], F32); lbs = sb("lbs", [128, 8], F32)
        cl = [(ident32, c_id), (perm32, c_perm), (tri64, c_tri64), (tri128, c_tri128), (rmask, c_rmask), (rG, c_rG), (rkd, c_rkd),
              (reset, c_reset), (sel, c_sel), (i16, c_i16), (BIF, bif)]
        for t, src in cl:
            dma(sp, t[:], src, [], [t.name], s_misc)
        dma(pool, ident16[:], c_id, [], ["ident16"], s_misc)
        fw.fence(s_misc, [t.name for t, _ in cl] + ["ident16"])
        MSET(dve, ones16[:], 1.0, ["ones16"]); MSET(dve, ones32[:], 1.0, ["ones32"]); MSET(dve, onesrow[:], 1.0, ["onesrow"])
        MSET(dve, epsn[:], 1e-6, ["epsn"]); MSET(dve, epsl[:], 1e-5, ["epsl"])

        PB = [ctx.enter_context(nc.psum_tensor(f"pb{i}", [128, 512], F32)) for i in range(8)]
        PBn = [f"pb{i}" for i in range(8)]

        for pc in range(8):
            dma(sp, PT32[:], pvec[:, pc * 1024:(pc + 1) * 1024], [], ["PT32"], osem("pt32"))
            for c8 in range(8):
                c = pc * 8 + c8
                TR(PB[0][:, c * L:(c + 1) * L], PT32[0:L, c8 * 128:(c8 + 1) * 128], ident32[0:L, 0:L], ["PT32", "ident32"], [PBn[0]])
        CP(dve, PAR[:].rearrange("p c l -> p (c l)"), PB[0][:, 0:64 * L], [PBn[0]], ["PAR"])
        ACT(lbe[:], PAR[:, 0:8, :], AF.Exp, ["PAR"], ["lbe"])
        fw.op(dve, lambda: nc.vector.tensor_reduce(out=lbs[:], in_=lbe[:], axis=AX.X, op=ALU.add), reads=["lbe"], writes=["lbs"])
        RECIP(lbs[:], lbs[:], ["lbs"], ["lbs"])
        TT(dve, lbe[:], lbe[:], lbs[:].unsqueeze(2).to_broadcast([128, 8, L]), ALU.mult, ["lbe", "lbs"], ["lbe"])
        MSET(dve, lbv[:], 0.0, ["lbv"])
        for l in range(1, L):
            TT(dve, lbv[:, :, l:l + 1], lbv[:, :, l - 1:l], lbe[:, :, l:l + 1], ALU.add, ["lbv", "lbe"], ["lbv"])
        TS(dve, oml[:], lbv[:], -1.0, 1.0, ALU.mult, ALU.add, ["lbv"], ["oml"])

        xT16 = sb("xT16", [128, 16, NW], BF16)
        U = sb("U", [128, 16 * NW], F32)
        yT = U[:].bitcast(BF16)[:, 0:24 * NW].rearrange("p (c t) -> p c t", t=NW)
        Hbuf = U[:].rearrange("p (c t) -> p c t", t=NW)
        S_h = sb("S_h", [128, 8, 128], F32); S_r = sb("S_r", [128, 8, 256], F32); S_c = sb("S_c", [128, 8, 257], F32)
        cosT = sb("cosT", [128, 2, NW], F32); sinT = sb("sinT", [128, 2, NW], F32)
        WT = [sb(f"WT{i}", [128, 16, 256], BF16) for i in range(2)]
        wcnt = [0]
        Ta = sb("Ta", [128, NW], F32); Tb = sb("Tb", [128, NW], F32); Tc = sb("Tc", [128, NW], F32); Td = sb("Td", [128, NW], F32)
        Te = sb("Te", [128, NW], F32); Tf = sb("Tf", [128, NW], F32); Tg = sb("Tg", [128, NW], F32)
        PH1 = sb("PH1", [128, 8 * NW], F32)
        Tz = PH1[:, 0:2 * NW].rearrange("p (c t) -> p c t", t=NW); To = PH1[:, 2 * NW:4 * NW].rearrange("p (c t) -> p c t", t=NW)
        H32 = PH1[:, 4 * NW:6 * NW].rearrange("p (c t) -> p c t", t=NW); SQ32 = PH1[:, 6 * NW:8 * NW].rearrange("p (c t) -> p c t", t=NW)
        mg16 = PH1[:].bitcast(BF16).rearrange("p (c t) -> p c t", t=NW)
        MGALL = ["mg16", "Tz", "To", "H32", "SQ32"]
        Q16 = sb("Q16", [128, 2, TP], BF16); K16 = sb("K16", [128, 2, TP], BF16); Q32 = sb("Q32", [128, 2, NW], F32); KP32 = sb("KP32", [128, 2, NS], F32)
        A32 = sb("A32", [128, NS], F32)
        KD16 = sb("KD16", [128, TP], BF16); KDT16 = sb("KDT16", [128, 4, 256], BF16); V16 = sb("V16", [128, 4, 257], BF16)
        AM16 = [sb(f"AM16_{i}", [128, 128], BF16) for i in range(2)]
        NREP = sb("NREP", [128, 2, 128], F32)
        Vs16 = sb("Vs16", [16, 256], BF16); BDs = sb("BD", [128, 16 * 256], BF16)
        BD = BDs[0:16, :].rearrange("p (s v) -> p s v", v=256)
        WBt = BDs[:, 0:8 * 3 * 128].rearrange("p (c n d) -> p c n d", n=3, d=128)
        Ssmp = sb("Ssmp", [128, 16, 256], F32); T1 = sb("T1", [128, 512], F32)
        Sflat = Ssmp[:].rearrange("p s v -> p (s v)")
        NS32 = Sflat[0:16, 0:1024]; NO32 = Sflat[0:16, 1024:2048]
        WG = Sflat.bitcast(BF16)[:, 0:16 * 3 * 128].rearrange("p (k n d) -> p k n d", n=3, d=128)
        GI = Ta[0:4, :]; GLF = Tb[0:4, :]; GB = Tc[0:4, 0:TP]; GU = Td[0:4, 0:TP]; GM = Te[0:4, 0:TP]
        GE1 = Tf[0:4, 0:TP]; GE2 = Tg[0:4, 0:TP]; GT = PH1[0:4, 0:TP]; GMS = sb("GMS", [4, NCH], F32); GDEC = sb("GDEC", [4, NCH], F32)
        sb_gt2 = sb("GT2", [4, TP], F32)
        Bc = sb("Bc", [4, 1], F32); Mc = sb("Mc", [4, 1], F32); Mfin = sb("Mfin", [4, 1], F32)
        ET = sb("ET", [128, NCH, 8], F32); THRB = sb("THRB", [128, TP], F32); DECB = sb("DECB", [128, 4, NCH], F32)
        SMS = sb("SMS", [4, NS], F32); SMm = sb("SMm", [4, NS], F32); SLI = sb("SLI", [4, NS], F32); SW = sb("SW", [4, 3, NS], F32); SWB = sb("SWB", [128, 4, 3, NS], F32)
        NT32 = sb("NT32", [128, 8, NS], F32)
        XIN = U[:, 0:D]; XS = U[:, D:2 * D].rearrange("p (k t) -> p k t", t=128)
        XR = [sb(f"XR{i}", [128, NW], F32) for i in range(2)]
        MS32 = Ta; MA32 = Tb; MB32 = Tc
        ST1 = sb("ST1", [128, NW], F32); ST2 = sb("ST2", [128, NW], F32); SQH = sb("SQH", [128, NW], F32)
        MSET(dve, V16[:, :, 256:257], 1.0, ["V16"])

        def xfres(par, kc):
            return f"xf{par}_{kc}"

        tiles = [(xp[i * 128:(i + 1) * 128, :], 128, i * 128) for i in range(NTOK // 128)] + [(xs, NS, NTOK)]
        for (src, nr, c0) in tiles:
            dma(sp, XIN[0:nr, :], src, [], ["U"], osem("xin"))
            for g in range(4):
                for j in range(4):
                    kc = g * 4 + j
                    TR(PB[g % 2][:, j * 128:j * 128 + nr], XIN[0:nr, kc * 128:(kc + 1) * 128], ident32[0:nr, 0:nr], ["U", "ident32"], [PBn[g % 2]])
                CP(act if g % 2 else dve, XS[:, g * 4:(g + 1) * 4, 0:nr], PB[g % 2][:].rearrange("p (j t) -> p j t", t=128)[:, :, 0:nr], [PBn[g % 2]], ["U"])
            dma(sp, XF[0, :, :, c0:c0 + nr].rearrange("k p t -> p k t"), XS[:, :, 0:nr], ["U"], [xfres(0, k) for k in range(16)], osem("xs"))

        def load_w(segs):
            i = wcnt[0] % 2
            wcnt[0] += 1
            off = 0
            for s in segs:
                ncol = s.shape[1]
                dma(pool, WT[i][:, :, off:off + ncol], s.rearrange("(k p) c -> p k c", p=128), [], [f"WT{i}"], s_w[i])
                off += ncol
            return WT[i], f"WT{i}"

        def proj_fm(bank, wt, wn, c0, ncol, blk):
            lc0, n = blk
            for kc in range(16):
                MM(PB[bank][0:ncol, 0:n], wt[:, kc, c0:c0 + ncol], xT16[:, kc, lc0:lc0 + n], kc == 0, kc == 15, [wn, "xT16"], [PBn[bank]])
            return PB[bank][0:ncol, 0:n]

        def proj_tm(bank, wt, wn, c0, ncol, lc0, nr):
            for kc in range(16):
                MM(PB[bank][0:nr, 0:ncol], xT16[:, kc, lc0:lc0 + nr], wt[:, kc, c0:c0 + ncol], kc == 0, kc == 15, [wn, "xT16"], [PBn[bank]])

        bankctr = [0]

        def nb():
            bankctr[0] ^= 1
            return bankctr[0]

        def chunk_scan(c, nkc, nvc, Sst, sname, h, mask_fn, a_fn, den):
            dv = nvc * 128
            ncol = dv + (1 if den else 0)
            for j in range(TP // c):
                i = (j * c) // 128
                base = (j * c) % 128
                rows = slice(base, base + c)
                cs = slice(j * c, (j + 1) * c)
                am = AM16[j % 2]; amn = f"AM16_{j % 2}"
                for kc in range(nkc):
                    MM(PB[4][rows, 0:c], K16[:, kc, cs], Q16[:, kc, cs], kc == 0, kc == nkc - 1, ["K16", "Q16"], [PBn[4]])
                mask_fn(j, rows, am, amn, c)
                for kc in range(nkc):
                    MM(PB[5 + kc][:, 0:ncol], KDT16[rows, i, kc * 128:(kc + 1) * 128], V16[rows, i, 0:ncol], True, True, ["KDT16", "V16"], [PBn[5 + kc]])
                for vc in range(nvc):
                    MM(PB[vc][:, cs], V16[rows, i, vc * 128:(vc + 1) * 128], am[rows, 0:c], True, False, ["V16", amn], [PBn[vc]])
                    for kc in range(nkc):
                        MM(PB[vc][:, cs], Sst[:, h * nkc + kc, vc * 128:(vc + 1) * 128], Q32[:, kc, cs], False, kc == nkc - 1, [sname, "Q32"], [PBn[vc]])
                if den:
                    for kc in range(nkc):
                        CP(pool, NREP[:, kc, :], Sst[:, h * nkc + kc, 256:257].to_broadcast([128, 128]), [sname], ["NREP"])
                    CP(pool, Tg[:, 0:c], am[:, 0:c], [amn], ["Tg"])
                    MM(PB[2][:, cs], ones32[:, :], Tg[:, 0:c], True, False, ["ones32", "Tg"], [PBn[2]])
                    for kc in range(nkc):
                        MM(PB[2][:, cs], NREP[:, kc, :], Q32[:, kc, cs], False, kc == nkc - 1, ["NREP", "Q32"], [PBn[2]])
                for kc in range(nkc):
                    STT(Sst[:, h * nkc + kc, 0:ncol], Sst[:, h * nkc + kc, 0:ncol], a_fn(j), PB[5 + kc][:, 0:ncol], ALU.mult, ALU.add,
                        [sname, PBn[5 + kc], "DECB", "Te"], [sname])

        def kd_transposes(nkc, srcs, scale_fn):
            for i in range(NCH):
                for kc in range(nkc):
                    ap, rn = srcs[kc]
                    pt = PB[4][:].bitcast(BF16)[:, 512 + kc * 128:512 + (kc + 1) * 128]
                    TR(pt, ap[:, i * 128:(i + 1) * 128], ident16[:], [rn, "ident16"], [PBn[4]])
                    sc = scale_fn(i)
                    if sc is None:
                        CP(act, KDT16[:, i, kc * 128:(kc + 1) * 128], pt, [PBn[4]], ["KDT16"])
                    else:
                        fw.op(act, lambda: nc.scalar.activation(out=KDT16[:, i, kc * 128:(kc + 1) * 128], in_=pt, func=AF.Copy, scale=sc),
                              reads=[PBn[4], "ET", "rkd"], writes=["KDT16"])

        def v_tokmajor(wt, wn, c0, dv, blocks):
            for (lc0, n), bt in blocks:
                if bt == "P":
                    for i in range(NCH):
                        b = nb()
                        proj_tm(b, wt, wn, c0, dv, lc0 + i * 128, 128)
                        CP(act, V16[:, i, 0:dv], PB[b][:, 0:dv], [PBn[b]], ["V16"])
                else:
                    b = nb()
                    proj_tm(b, wt, wn, c0, dv, lc0, NS)
                    CP(act, Vs16[:, 0:dv], PB[b][0:NS, 0:dv], [PBn[b]], ["Vs16"])
                    TT(dve, BD[:, :, 0:dv], Vs16[:, 0:dv].unsqueeze(1).to_broadcast([NS, NS, dv]), i16[:].unsqueeze(2).to_broadcast([NS, NS, dv]),
                       ALU.mult, ["Vs16", "i16"], ["BD"])

        def sample_update(l, dv, nkc, h, st_in, st_out, a_fn, kp_fn, nvc):
            per = 512 // dv
            for kc in range(nkc):
                dma(sp, Ssmp[:, :, 0:dv], st_in[l, :, h, kc * 128:(kc + 1) * 128, :].rearrange("s k v -> k s v"), [], ["Ssmp"], s_st)
                a_ap = a_fn(kc); kp_ap = kp_fn(kc)
                for g in range(NS // per):
                    ss = slice(g * per, (g + 1) * per)
                    MM(PB[3][:, 0:512], ones16[0:NS, :], BD[:, ss, 0:dv], True, True, ["ones16", "BD"], [PBn[3]])
                    TT(dve, T1[:].rearrange("p (s v) -> p s v", v=dv), PB[3][:, 0:512].rearrange("p (s v) -> p s v", v=dv),
                       kp_ap[:, ss].unsqueeze(2).to_broadcast([128, per, dv]), ALU.mult, [PBn[3], "KP32", "Td"], ["T1"])
                    TT(pool, Ssmp[:, ss, 0:dv], Ssmp[:, ss, 0:dv], a_ap[:, ss].unsqueeze(2).to_broadcast([128, per, dv]), ALU.mult,
                       ["Ssmp", "A32", "Tf", "SWB"], ["Ssmp"])
                    TT(dve, Ssmp[:, ss, 0:dv], Ssmp[:, ss, 0:dv], T1[:].rearrange("p (s v) -> p s v", v=dv), ALU.add, ["Ssmp", "T1"], ["Ssmp"])
                dma(sp, st_out[l, :, h, kc * 128:(kc + 1) * 128, :].rearrange("s k v -> k s v"), Ssmp[:, :, 0:dv], ["Ssmp"], [], osem("ssmp"))
                for s in range(NS):
                    for vc in range(nvc):
                        MM(PB[7][:, vc * 16 + s:vc * 16 + s + 1], Ssmp[:, s, vc * 128:(vc + 1) * 128], Q32[:, kc, TP + s:TP + s + 1],
                           kc == 0 and s == 0 and vc == 0, kc == nkc - 1, ["Ssmp", "Q32"], [PBn[7]], skip=True)

        def head_norm(kind, nvc, gain_chunk0, l, ycol0, n):
            for vc in range(nvc):
                ACT(SQ32[:, vc, 0:n], H32[:, vc, 0:n], AF.Square, ["H32"], ["SQ32"])
            pieces = [(0, 512, 3)] + ([(512, n - 512, 2)] if n > 512 else [])
            for (c0, nn, bk) in pieces:
                for vc in range(nvc):
                    MM(PB[bk][:, 0:nn], ones32[:], SQ32[:, vc, c0:c0 + nn], vc == 0, vc == nvc - 1, ["ones32", "SQ32"], [PBn[bk]])
                CP(act, ST2[:, c0:c0 + nn], PB[bk][:, 0:nn], [PBn[bk]], ["ST2"])
                if kind == "ln":
                    for vc in range(nvc):
                        MM(PB[bk][:, 0:nn], ones32[:], H32[:, vc, c0:c0 + nn], vc == 0, vc == nvc - 1, ["ones32", "H32"], [PBn[bk]])
                    CP(act, ST1[:, c0:c0 + nn], PB[bk][:, 0:nn], [PBn[bk]], ["ST1"])
            dvv = float(nvc * 128)
            if kind == "ln":
                TS(dve, ST1[:, 0:n], ST1[:, 0:n], 1.0 / dvv, None, ALU.mult, None, ["ST1"], ["ST1"])
                TT(dve, SQH[:, 0:n], ST1[:, 0:n], ST1[:, 0:n], ALU.mult, ["ST1"], ["SQH"])
                STT(ST2[:, 0:n], ST2[:, 0:n], 1.0 / dvv, SQH[:, 0:n], ALU.mult, ALU.subtract, ["ST2", "SQH"], ["ST2"])
                ACT(ST2[:, 0:n], ST2[:, 0:n], AF.Ln, ["ST2", "epsn"], ["ST2"], scale=1.0, bias=epsn[:])
            else:
                ACT(ST2[:, 0:n], ST2[:, 0:n], AF.Ln, ["ST2", "epsn"], ["ST2"], scale=1.0 / dvv, bias=epsn[:])
            ACT(ST2[:, 0:n], ST2[:, 0:n], AF.Exp, ["ST2"], ["ST2"], scale=-0.5)
            for vc in range(nvc):
                g_col = PAR[:, gain_chunk0 + vc, l:l + 1]
                if kind == "ln":
                    TT(dve, SQ32[:, vc, 0:n], H32[:, vc, 0:n], ST1[:, 0:n], ALU.subtract, ["H32", "ST1"], ["SQ32"])
                    STT(SQ32[:, vc, 0:n], SQ32[:, vc, 0:n], g_col, ST2[:, 0:n], ALU.mult, ALU.mult, ["SQ32", "ST2", "PAR"], ["SQ32"])
                else:
                    STT(SQ32[:, vc, 0:n], H32[:, vc, 0:n], g_col, ST2[:, 0:n], ALU.mult, ALU.mult, ["H32", "ST2", "PAR"], ["SQ32"])
                TT(pool, yT[:, ycol0 + vc, 0:n], SQ32[:, vc, 0:n], Tz[:, vc, 0:n], ALU.mult, ["SQ32", "Tz"], ["U"])

        def rotary(dst32, dname, kc, n):
            for (c0, nn) in [(0, 512)] + ([(512, n - 512)] if n > 512 else []):
                b = nb()
                MM(PB[b][:, 0:nn], perm32[:], Ta[:, c0:c0 + nn], True, True, ["perm32", "Ta"], [PBn[b]])
                TT(dve, Tc[:, c0:c0 + nn], PB[b][:, 0:nn], sinT[:, kc, c0:c0 + nn], ALU.mult, [PBn[b], "sinT"], ["Tc"])
            TT(pool, Tb[:, 0:n], Ta[:, 0:n], cosT[:, kc, 0:n], ALU.mult, ["Ta", "cosT"], ["Tb"])
            TT(dve, dst32, Tb[:, 0:n], Tc[:, 0:n], ALU.add, ["Tb", "Tc"], [dname])

        for l in range(L):
            par = l % 2
            for p in range(NP):
                first = (p == 0)
                last = (p == NP - 1)
                blocks = [((0, TP), "P")] + ([((TP, NS), "S")] if first else [])
                n_all = TP + (NS if first else 0)
                xres = [xfres(par, k) for k in range(16)]
                dma(pool, xT16[:, :, 0:TP], XF[par, :, :, p * TP:(p + 1) * TP].rearrange("k p t -> p k t"), xres, ["xT16"], s_x)
                dma(sp, cosT[:, :, 0:TP], c_cos[:, :, p * TP:(p + 1) * TP], [], ["cosT"], s_tab)
                dma(sp, sinT[:, :, 0:TP], c_sin[:, :, p * TP:(p + 1) * TP], [], ["sinT"], s_tab)
                if first:
                    dma(pool, xT16[:, :, TP:NW], XF[par, :, :, NTOK:NT].rearrange("k p t -> p k t"), xres, ["xT16"], s_x)
                    dma(sp, cosT[:, :, TP:NW], c_cos[:, :, NTOK:NT], [], ["cosT"], s_tab)
                    dma(sp, sinT[:, :, TP:NW], c_sin[:, :, NTOK:NT], [], ["sinT"], s_tab)
                    MSET(pool, S_h[:], 0.0, ["S_h"]); MSET(pool, S_r[:], 0.0, ["S_r"]); MSET(pool, S_c[:], 0.0, ["S_c"])
                    MSET(dve, Bc[:], 0.0, ["Bc"]); MSET(dve, Mc[:], 0.0, ["Mc"])
                fw.fence(s_tab, ["cosT", "sinT"])
                fw.fence(s_x, ["xT16"])

                for h in range(8):
                    wtA, wnA = load_w([w_in[l][:, sg * BW + h * 128: sg * BW + (h + 1) * 128] for sg in (0, 1)])
                    wtB, wnB = load_w([w_in[l][:, sg * BW + h * 128: sg * BW + (h + 1) * 128] for sg in (2, 3)])
                    for (blk, bt) in blocks:
                        lc0, n = blk
                        b = nb(); ps = proj_fm(b, wtA, wnA, 0, 128, blk)
                        ACT(Q32[:, 0, lc0:lc0 + n], ps, AF.Silu, [PBn[b]], ["Q32"])
                        b = nb(); ps = proj_fm(b, wtA, wnA, 128, 128, blk)
                        ACT(Tf[:, lc0:lc0 + n], ps, AF.Sigmoid, [PBn[b]], ["Tf"])
                        b = nb(); ps = proj_fm(b, wtB, wnB, 128, 128, blk)
                        ACT(Tz[:, 0, lc0:lc0 + n], ps, AF.Silu, [PBn[b]], ["Tz"])
                    v_tokmajor(wtB, wnB, 0, 128, blocks)
                    TS(dve, Tf[:, 0:n_all], Tf[:, 0:n_all], oml[:, h, l:l + 1], lbv[:, h, l:l + 1], ALU.mult, ALU.add, ["Tf", "oml", "lbv"], ["Tf"])
                    TS(dve, Td[:, 0:n_all], Tf[:, 0:n_all], -1.0, 1.0, ALU.mult, ALU.add, ["Tf"], ["Td"])
                    ACT(Tg[:, 0:TP], Tf[:, 0:TP], AF.Ln, ["Tf"], ["Tg"])
                    SCAN(Tb[:, 0:TP], reset[:], Tg[:, 0:TP], 0.0, ALU.mult, ALU.add, ["reset", "Tg"], ["Tb"])
                    ACT(Te[:, 0:TP], Tb[:, 0:TP], AF.Exp, ["Tb"], ["Te"])
                    ACT(Tc[:, 0:TP], Tb[:, 0:TP], AF.Exp, ["Tb"], ["Tc"], scale=-1.0)
                    TT(dve, Q32[:, 0, 0:TP], Q32[:, 0, 0:TP], Te[:, 0:TP], ALU.mult, ["Q32", "Te"], ["Q32"])
                    CP(pool, Q16[:, 0, :], Q32[:, 0, 0:TP], ["Q32"], ["Q16"])
                    TT(pool, K16[:, 0, :], Td[:, 0:TP], Tc[:, 0:TP], ALU.mult, ["Td", "Tc"], ["K16"])
                    TT(dve, KD16[:].rearrange("p (j t) -> p j t", t=64), K16[:, 0, :].rearrange("p (j t) -> p j t", t=64),
                       Te[:, 0:TP].rearrange("p (j t) -> p j t", t=64)[:, :, 63:64].to_broadcast([128, TP // 64, 64]), ALU.mult, ["K16", "Te"], ["KD16"])
                    kd_transposes(1, [(KD16, "KD16")], lambda i: None)

                    def hmask(j, rows, am, amn, c):
                        TT(dve, am[rows, 0:c], PB[4][rows, 0:c], tri64[rows, :], ALU.mult, [PBn[4], "tri64"], [amn])
                    chunk_scan(64, 1, 1, S_h, "S_h", h, hmask, lambda j: Te[:, 64 * j + 63:64 * j + 64], False)
                    CP(act, H32[:, 0, 0:TP], PB[0][:, 0:TP], [PBn[0]], ["H32"])
                    if first:
                        sample_update(l, 128, 1, h, st_h, o_hs, lambda kc: Tf[:, TP:NW], lambda kc: Td[:, TP:NW], 1)
                        CP(act, H32[:, 0, TP:NW], PB[7][:, 0:NS], [PBn[7]], ["H32"])
                    head_norm("rms", 1, 8 + h, l, h, n_all)
                if last:
                    dma(sp, o_hp[l].rearrange("h k v -> k h v"), S_h[:], ["S_h"], [], osem("S_h"))

                for h in range(4):
                    for sgi, sgn in ((4, "q"), (5, "k")):
                        wt, wn = load_w([w_in[l][:, sgi * BW + h * 256: sgi * BW + (h + 1) * 256]])
                        for kc in range(2):
                            for (blk, bt) in blocks:
                                lc0, n = blk
                                b = nb(); ps = proj_fm(b, wt, wn, kc * 128, 128, blk)
                                CP(act, Ta[:, lc0:lc0 + n], ps, [PBn[b]], ["Ta"])
                            if sgn == "q":
                                rotary(Q32[:, kc, 0:n_all], "Q32", kc, n_all)
                                CP(pool, Q16[:, kc, :], Q32[:, kc, 0:TP], ["Q32"], ["Q16"])
                            else:
                                rotary(Td[:, 0:n_all], "Td", kc, n_all)
                                CP(pool, K16[:, kc, :], Td[:, 0:TP], ["Td"], ["K16"])
                                if first:
                                    TS(dve, KP32[:, kc, :], Td[:, TP:NW], 1.0 / 16.0, None, ALU.mult, None, ["Td"], ["KP32"])
                    wt, wn = load_w([w_in[l][:, 6 * BW + h * 256: 6 * BW + (h + 1) * 256]])
                    v_tokmajor(wt, wn, 0, 256, blocks)
                    wt, wn = load_w([w_in[l][:, 7 * BW + h * 256: 7 * BW + (h + 1) * 256]])
                    for vc in range(2):
                        for (blk, bt) in blocks:
                            lc0, n = blk
                            b = nb(); ps = proj_fm(b, wt, wn, vc * 128, 128, blk)
                            ACT(Tz[:, vc, lc0:lc0 + n], ps, AF.Silu, [PBn[b]], ["Tz"])
                    kd_transposes(2, [(K16[:, 0, :], "K16"), (K16[:, 1, :], "K16")], lambda i: rkd[:, h:h + 1])

                    def rmaskf(j, rows, am, amn, c):
                        TT(dve, am[rows, 0:c], PB[4][rows, 0:c], rmask[:, h, :], ALU.mult, [PBn[4], "rmask"], [amn])
                    ga = float(GAM[h] ** 128)
                    chunk_scan(128, 2, 2, S_r, "S_r", h, rmaskf, lambda j: ga, False)
                    for vc in range(2):
                        TT(dve, H32[:, vc, 0:TP].rearrange("p (j t) -> p j t", t=128), PB[vc][:, 0:TP].rearrange("p (j t) -> p j t", t=128),
                           rG[:, h, :].unsqueeze(1).to_broadcast([128, NCH, 128]), ALU.mult, [PBn[vc], "rG"], ["H32"])
                    if first:
                        MSET(pool, A32[:], float(GAM[h]), ["A32"])
                        sample_update(l, 256, 2, h, st_r, o_rs, lambda kc: A32, lambda kc: KP32[:, kc, :], 2)
                        for vc in range(2):
                            CP(act, H32[:, vc, TP:NW], PB[7][:, vc * 16:vc * 16 + NS], [PBn[7]], ["H32"])
                    head_norm("ln", 2, 16 + 2 * h, l, 8 + 2 * h, n_all)
                if last:
                    dma(sp, o_rp[l].rearrange("h (kc k) v -> k (h kc) v", k=128), S_r[:], ["S_r"], [], osem("S_r"))

                wt, wn = load_w([w_in[l][:, 13312:13320]])
                for (blk, bt) in blocks:
                    lc0, n = blk
                    b = nb(); ps = proj_fm(b, wt, wn, 0, 4, blk)
                    ACT(GI[:, lc0:lc0 + n], ps, AF.Identity, [PBn[b], "BIF"], ["Ta"], bias=BIF[:, 0, l:l + 1])
                    b = nb(); ps = proj_fm(b, wt, wn, 4, 4, blk)
                    ACT(GLF[:, lc0:lc0 + n], ps, AF.Sigmoid, [PBn[b], "BIF"], ["Tb"], bias=BIF[:, 1, l:l + 1])
                ACT(GLF[:, 0:n_all], GLF[:, 0:n_all], AF.Ln, ["Tb"], ["Tb"])
                SCAN(GB[:], onesrow[:], GLF[:, 0:TP], Bc[:, 0:1], ALU.mult, ALU.add, ["onesrow", "Tb", "Bc"], ["Tc"])
                TT(dve, GU[:], GI[:, 0:TP], GB[:], ALU.subtract, ["Ta", "Tc"], ["Td"])
                SCAN(GM[:], onesrow[:], GU[:], Mc[:, 0:1], ALU.mult, ALU.max, ["onesrow", "Td", "Mc"], ["Te"])
                GMv = GM[:].rearrange("p (j t) -> p j t", t=128)
                CP(dve, GMS[:, 0:1], Mc[:], ["Mc"], ["GMS"])
                if NCH > 1:
                    CP(dve, GMS[:, 1:NCH].unsqueeze(2), GMv[:, 0:NCH - 1, 127:128], ["Te"], ["GMS"])
                TT(dve, GE1[:].rearrange("p (j t) -> p j t", t=128), GU[:].rearrange("p (j t) -> p j t", t=128),
                   GMS[:].unsqueeze(2).to_broadcast([4, NCH, 128]), ALU.subtract, ["Td", "GMS"], ["Tf"])
                ACT(GE1[:], GE1[:], AF.Exp, ["Tf"], ["Tf"])
                TT(dve, GE2[:].rearrange("p (j t) -> p j t", t=128), GU[:].rearrange("p (j t) -> p j t", t=128),
                   GMv[:, :, 127:128].to_broadcast([4, NCH, 128]), ALU.subtract, ["Td", "Te"], ["Tg"])
                ACT(GE2[:], GE2[:], AF.Exp, ["Tg"], ["Tg"])
                TS(dve, GE2[:], GE2[:], 1.0 / 16.0, None, ALU.mult, None, ["Tg"], ["Tg"])
                TT(dve, GT[:].rearrange("p (j t) -> p j t", t=128), GB[:].rearrange("p (j t) -> p j t", t=128),
                   GMS[:].unsqueeze(2).to_broadcast([4, NCH, 128]), ALU.add, ["Tc", "GMS"], ["Tz"])
                ACT(GT[:], GT[:], AF.Exp, ["Tz"], ["Tz"], scale=-1.0)
                TT(dve, GDEC[:].unsqueeze(2), GMS[:].unsqueeze(2), GMv[:, :, 127:128], ALU.subtract, ["GMS", "Te"], ["GDEC"])
                ACT(GDEC[:], GDEC[:], AF.Exp, ["GDEC"], ["GDEC"])
                CP(dve, Bc[:], GB[:, TP - 1:TP], ["Tc"], ["Bc"])
                CP(dve, Mc[:], GM[:, TP - 1:TP], ["Te"], ["Mc"])
                TT(dve, Mfin[:], GB[:, TP - 1:TP], GM[:, TP - 1:TP], ALU.add, ["Tc", "Te"], ["Mfin"])
                for j in range(NCH):
                    TR(PB[4][:, 0:4], GE1[:, j * 128:(j + 1) * 128], ident32[0:4, 0:4], ["Tf", "ident32"], [PBn[4]])
                    TR(PB[4][:, 4:8], GE2[:, j * 128:(j + 1) * 128], ident32[0:4, 0:4], ["Tg", "ident32"], [PBn[4]])
                    CP(act, ET[:, j, :], PB[4][:, 0:8], [PBn[4]], ["ET"])
                GT2 = sb_gt2
                CP(dve, GT2[:], GT[:], ["Tz"], ["GT2"])
                for h in range(4):
                    MM(PB[2][:, h * NCH:(h + 1) * NCH], sel[:, h, :], GDEC[:], True, True, ["sel", "GDEC"], [PBn[2]])
                CP(act, DECB[:].rearrange("p h j -> p (h j)"), PB[2][:, 0:4 * NCH], [PBn[2]], ["DECB"])
                if first:
                    dma(sp, SMS[:], st_m[l].rearrange("s h -> h s"), [], ["SMS"], osem("sms_in"), slow=True)
                    dma(sp, NS32[:], st_n[l], [], ["Ssmp"], osem("ns_in"))
                    TT(dve, SLI[:], SMS[:], GLF[:, TP:NW], ALU.add, ["SMS", "Tb"], ["SLI"])
                    TT(dve, SMm[:], SLI[:], GI[:, TP:NW], ALU.max, ["SLI", "Ta"], ["SMm"])
                    TT(dve, SW[:, 0, :], GI[:, TP:NW], SMm[:], ALU.subtract, ["Ta", "SMm"], ["SW"])
                    TT(dve, SW[:, 1, :], SLI[:], SMm[:], ALU.subtract, ["SLI", "SMm"], ["SW"])
                    TS(dve, SW[:, 2, :], SMm[:], -1.0, None, ALU.mult, None, ["SMm"], ["SW"])
                    ACT(SW[:], SW[:], AF.Exp, ["SW"], ["SW"])
                    TS(dve, SW[:, 0, :], SW[:, 0, :], 1.0 / 16.0, None, ALU.mult, None, ["SW"], ["SW"])
                    dma(sp, o_ms[l].rearrange("s h -> h s"), SMm[:], ["SMm"], [], osem("smm"), slow=True)
                    for h in range(4):
                        MM(PB[3][:, h * 48:(h + 1) * 48], sel[:, h, :], SW[:].rearrange("p a s -> p (a s)"), True, True, ["sel", "SW"], [PBn[3]])
                    CP(act, SWB[:].rearrange("p h a s -> p (h a s)"), PB[3][:, 0:192], [PBn[3]], ["SWB"])
                    for c in range(8):
                        TR(PB[4][:, 128 + c * NS:128 + (c + 1) * NS], NS32[0:NS, c * 128:(c + 1) * 128], ident32[0:NS, 0:NS], ["Ssmp", "ident32"], [PBn[4]])
                    CP(act, NT32[:].rearrange("p c s -> p (c s)"), PB[4][:, 128:128 + 8 * NS], [PBn[4]], ["NT32"])
                for h in range(4):
                    wt, wn = load_w([w_in[l][:, 8 * BW + h * 256: 8 * BW + (h + 1) * 256]])
                    for kc in range(2):
                        for (blk, bt) in blocks:
                            lc0, n = blk
                            b = nb(); ps = proj_fm(b, wt, wn, kc * 128, 128, blk)
                            CP(act, Q32[:, kc, lc0:lc0 + n], ps, [PBn[b]], ["Q32"])
                        CP(pool, Q16[:, kc, :], Q32[:, kc, 0:TP], ["Q32"], ["Q16"])
                    wt, wn = load_w([w_in[l][:, 9 * BW + h * 256: 9 * BW + (h + 1) * 256]])
                    for kc in range(2):
                        for (blk, bt) in blocks:
                            lc0, n = blk
                            b = nb(); ps = proj_fm(b, wt, wn, kc * 128, 128, blk)
                            if bt == "P":
                                CP(act, K16[:, kc, :], ps, [PBn[b]], ["K16"])
                            else:
                                TT(dve, KP32[:, kc, :], ps, SWB[:, h, 0, :], ALU.mult, [PBn[b], "SWB"], ["KP32"])
                    wt, wn = load_w([w_in[l][:, 10 * BW + h * 256: 10 * BW + (h + 1) * 256]])
                    v_tokmajor(wt, wn, 0, 256, blocks)
                    for sgi, dst, fn_, dn in ((11, Tz, AF.Silu, "Tz"), (12, To, AF.Sigmoid, "To")):
                        wt, wn = load_w([w_in[l][:, sgi * BW + h * 256: sgi * BW + (h + 1) * 256]])
                        for vc in range(2):
                            for (blk, bt) in blocks:
                                lc0, n = blk
                                b = nb(); ps = proj_fm(b, wt, wn, vc * 128, 128, blk)
                                ACT(dst[:, vc, lc0:lc0 + n], ps, fn_, [PBn[b]], [dn])
                    kd_transposes(2, [(K16[:, 0, :], "K16"), (K16[:, 1, :], "K16")], lambda i: ET[:, i, 4 + h:5 + h])

                    def mmask(j, rows, am, amn, c):
                        STT(am[rows, 0:c], PB[4][rows, 0:c], ET[:, j, h:h + 1], tri128[:], ALU.mult, ALU.mult, [PBn[4], "ET", "tri128"], [amn])
                    chunk_scan(128, 2, 2, S_c, "S_c", h, mmask, lambda j: DECB[:, h, j:j + 1], True)
                    MM(PB[3][:, 0:TP], sel[:, h, :], GT2[:], True, True, ["sel", "GT2"], [PBn[3]])
                    CP(act, THRB[:], PB[3][:, 0:TP], [PBn[3]], ["THRB"])
                    ACT(Te[:, 0:TP], PB[2][:, 0:TP], AF.Abs, [PBn[2]], ["Te"])
                    TT(dve, Te[:, 0:TP], Te[:, 0:TP], THRB[:], ALU.max, ["Te", "THRB"], ["Te"])
                    RECIP(Te[:, 0:TP], Te[:, 0:TP], ["Te"], ["Te"])
                    for vc in range(2):
                        TT(dve, H32[:, vc, 0:TP], PB[vc][:, 0:TP], Te[:, 0:TP], ALU.mult, [PBn[vc], "Te"], ["H32"])
                    if first:
                        for kc in range(2):
                            c8 = h * 2 + kc
                            TT(dve, NT32[:, c8, :], NT32[:, c8, :], SWB[:, h, 1, :], ALU.mult, ["NT32", "SWB"], ["NT32"])
                            TT(dve, NT32[:, c8, :], NT32[:, c8, :], KP32[:, kc, :], ALU.add, ["NT32", "KP32"], ["NT32"])
                            TT(dve, Ta[:, kc * NS:(kc + 1) * NS], Q32[:, kc, TP:NW], NT32[:, c8, :], ALU.mult, ["Q32", "NT32"], ["Ta"])
                        MM(PB[7][:, 32:48], ones32[:], Ta[:, 0:NS], True, False, ["ones32", "Ta"], [PBn[7]])
                        MM(PB[7][:, 32:48], ones32[:], Ta[:, NS:2 * NS], False, True, ["ones32", "Ta"], [PBn[7]])
                        sample_update(l, 256, 2, h, st_c, o_cs, lambda kc: SWB[:, h, 1, :], lambda kc: KP32[:, kc, :], 2)
                        ACT(Tc[:, 0:NS], PB[7][:, 32:48], AF.Abs, [PBn[7]], ["Tc"])
                        TT(dve, Tc[:, 0:NS], Tc[:, 0:NS], SWB[:, h, 2, :], ALU.max, ["Tc", "SWB"], ["Tc"])
                        RECIP(Tc[:, 0:NS], Tc[:, 0:NS], ["Tc"], ["Tc"])
                        for vc in range(2):
                            TT(dve, H32[:, vc, TP:NW], PB[7][:, vc * 16:vc * 16 + NS], Tc[:, 0:NS], ALU.mult, [PBn[7], "Tc"], ["H32"])
                    for vc in range(2):
                        TT(pool, H32[:, vc, 0:n_all], H32[:, vc, 0:n_all], To[:, vc, 0:n_all], ALU.mult, ["H32", "To"], ["H32"])
                    head_norm("ln", 2, 24 + 2 * h, l, 16 + 2 * h, n_all)
                if first:
                    for c in range(8):
                        TR(PB[5 + c // 4][0:NS, (c % 4) * 128:(c % 4 + 1) * 128], NT32[:, c, :], ident32[:], ["NT32", "ident32"], [PBn[5 + c // 4]])
                    CP(act, NO32[:, 0:512], PB[5][0:NS, :], [PBn[5]], ["Ssmp"])
                    CP(act, NO32[:, 512:1024], PB[6][0:NS, :], [PBn[6]], ["Ssmp"])
                    dma(sp, o_ns[l], NO32[:], ["Ssmp"], [], osem("no32"))
                if last:
                    dma(sp, o_cp[l].rearrange("h (kc k) v -> k (h kc) v", k=128), S_c[:, :, 0:256], ["S_c"], [], osem("S_c"))
                    dma(sp, o_np[l].rearrange("h (kc k) -> k (h kc)", k=128), S_c[:, :, 256], ["S_c"], [], osem("S_c"), slow=True)
                    dma(sp, o_mp[l].rearrange("(h o) -> h o", o=1), Mfin[:], ["Mfin"], [], osem("mfin"), slow=True)

                pieces = [(0, 512)] + ([(512, NS)] if first else [])
                for dc in range(16):
                    d0 = dc * 128
                    for nbr in range(3):
                        dma(pool, WG[:, :, nbr, :], w_in[l][:, GATE0 + nbr * D + d0: GATE0 + nbr * D + d0 + 128].rearrange("(k p) c -> p k c", p=128), [], ["Ssmp"], s_wg)
                        dma(pool, WBt[:, :, nbr, :], w_br[l, nbr][:, d0:d0 + 128].rearrange("(k p) c -> p k c", p=128), [], ["BD"], s_wb)
                    for (c0, nn) in pieces:
                        for nbr in range(3):
                            b = nb()
                            for kc in range(16):
                                MM(PB[b][:, 0:nn], WG[:, kc, nbr, :], xT16[:, kc, c0:c0 + nn], kc == 0, kc == 15, ["Ssmp", "xT16"], [PBn[b]])
                            ACT(MS32[:, c0:c0 + nn], PB[b][:, 0:nn], AF.Sigmoid, [PBn[b]], ["Ta"])
                            b = nb()
                            for cc in range(8):
                                MM(PB[b][:, 0:nn], WBt[:, cc, nbr, :], yT[:, nbr * 8 + cc, c0:c0 + nn], cc == 0, cc == 7, ["BD", "U"], [PBn[b]])
                            if nbr == 0:
                                TT(dve, MA32[:, c0:c0 + nn], MS32[:, c0:c0 + nn], PB[b][:, 0:nn], ALU.mult, ["Ta", PBn[b]], ["Tb"])
                            else:
                                TT(dve, MB32[:, c0:c0 + nn], MS32[:, c0:c0 + nn], PB[b][:, 0:nn], ALU.mult, ["Ta", PBn[b]], ["Tc"])
                                if nbr == 1:
                                    TT(pool, MA32[:, c0:c0 + nn], MA32[:, c0:c0 + nn], MB32[:, c0:c0 + nn], ALU.add, ["Tb", "Tc"], ["Tb"])
                                else:
                                    TT(pool, mg16[:, dc, c0:c0 + nn], MA32[:, c0:c0 + nn], MB32[:, c0:c0 + nn], ALU.add, ["Tb", "Tc"], MGALL)
                for dc in range(16):
                    d0 = dc * 128
                    wt, wn = load_w([w_o[l][:, d0:d0 + 128]])
                    xi = dc % 2
                    dma(sp, XR[xi][:, 0:TP], XF[par, dc, :, p * TP:(p + 1) * TP], [xfres(par, dc)], [f"XR{xi}"], s_xr[xi])
                    if first:
                        dma(sp, XR[xi][:, TP:NW], XF[par, dc, :, NTOK:NT], [xfres(par, dc)], [f"XR{xi}"], s_xr[xi])
                    for (c0, nn) in pieces:
                        b = nb()
                        for mc in range(16):
                            MM(PB[b][:, 0:nn], wt[:, mc, 0:128], mg16[:, mc, c0:c0 + nn], mc == 0, mc == 15, [wn] + MGALL, [PBn[b]])
                        STT(Hbuf[:, dc, c0:c0 + nn], XR[xi][:, c0:c0 + nn], ALPHA, PB[b][:, 0:nn], ALU.mult, ALU.add, [f"XR{xi}", PBn[b]], ["U"])
                    ACT(SQH[:, 0:n_all], Hbuf[:, dc, 0:n_all], AF.Square, ["U"], ["SQH"])
                    MM(PB[2][:, 0:512], ones32[:], Hbuf[:, dc, 0:512], dc == 0, dc == 15, ["ones32", "U"], [PBn[2]])
                    MM(PB[3][:, 0:512], ones32[:], SQH[:, 0:512], dc == 0, dc == 15, ["ones32", "SQH"], [PBn[3]])
                    if first:
                        MM(PB[6][:, 0:NS], ones32[:], Hbuf[:, dc, TP:NW], dc == 0, dc == 15, ["ones32", "U"], [PBn[6]])
                        MM(PB[7][:, 0:NS], ones32[:], SQH[:, TP:NW], dc == 0, dc == 15, ["ones32", "SQH"], [PBn[7]])
                CP(act, ST1[:, 0:512], PB[2][:, 0:512], [PBn[2]], ["ST1"])
                CP(act, ST2[:, 0:512], PB[3][:, 0:512], [PBn[3]], ["ST2"])
                if first:
                    CP(act, ST1[:, TP:NW], PB[6][:, 0:NS], [PBn[6]], ["ST1"])
                    CP(act, ST2[:, TP:NW], PB[7][:, 0:NS], [PBn[7]], ["ST2"])
                n = n_all
                TS(dve, ST1[:, 0:n], ST1[:, 0:n], 1.0 / D, None, ALU.mult, None, ["ST1"], ["ST1"])
                TT(dve, SQH[:, 0:n], ST1[:, 0:n], ST1[:, 0:n], ALU.mult, ["ST1"], ["SQH"])
                STT(ST2[:, 0:n], ST2[:, 0:n], 1.0 / D, SQH[:, 0:n], ALU.mult, ALU.subtract, ["ST2", "SQH"], ["ST2"])
                ACT(ST2[:, 0:n], ST2[:, 0:n], AF.Ln, ["ST2", "epsl"], ["ST2"], scale=1.0, bias=epsl[:])
                ACT(ST2[:, 0:n], ST2[:, 0:n], AF.Exp, ["ST2"], ["ST2"], scale=-0.5)
                for dc in range(16):
                    xi = dc % 2
                    TT(dve, MA32[:, 0:n], Hbuf[:, dc, 0:n], ST1[:, 0:n], ALU.subtract, ["U", "ST1"], ["Tb"])
                    TT(pool, MB32[:, 0:n], MA32[:, 0:n], ST2[:, 0:n], ALU.mult, ["Tb", "ST2"], ["Tc"])
                    TS(dve, XR[xi][:, 0:n], MB32[:, 0:n], PAR[:, 32 + dc, l:l + 1], PAR[:, 48 + dc, l:l + 1], ALU.mult, ALU.add, ["Tc", "PAR"], [f"XR{xi}"])
                    dma(sp, XF[1 - par, dc, :, p * TP:(p + 1) * TP], XR[xi][:, 0:TP], [f"XR{xi}"], [xfres(1 - par, dc)], s_xw[xi])
                    if first:
                        dma(sp, XF[1 - par, dc, :, NTOK:NT], XR[xi][:, TP:NW], [f"XR{xi}"], [xfres(1 - par, dc)], s_xw[xi])

        fp_ = L % 2
        otiles = [(yp[i * 128:(i + 1) * 128, :], 128, i * 128) for i in range(NTOK // 128)] + [(ys, NS, NTOK)]
        for (dst, nr, c0) in otiles:
            dma(sp, XS[:, :, 0:nr], XF[fp_, :, :, c0:c0 + nr].rearrange("k p t -> p k t"), [xfres(fp_, k) for k in range(16)], ["U"], osem("xs_in"))
            for g in range(4):
                for j in range(4):
                    kc = g * 4 + j
                    TR(PB[g % 2][0:nr, j * 128:(j + 1) * 128], XS[:, kc, 0:nr], ident32[:], ["U", "ident32"], [PBn[g % 2]])
                CP(act if g % 2 else dve, XIN[0:nr, g * 512:(g + 1) * 512], PB[g % 2][0:nr, :], [PBn[g % 2]], ["U"])
            dma(sp, dst, XIN[0:nr, :], ["U"], [], osem("xin_out"))
        for s in osems.values():
            nc.sync.wait_ge(s.h, s.v)
        for s in s_xw:
            nc.sync.wait_ge(s.h, s.v)
        print("ops", fw.nops)
    return nc


def make_consts(NP, seq_b_pos0=0):
    NTOK = NP * TP
    NT = NTOK + NS
    c = {}
    c["c_id"] = np.eye(128, dtype=np.float32)
    perm = np.zeros((128, 128), np.float32)
    for k in range(128):
        perm[k, k ^ 1] = 1.0
    c["c_perm"] = perm
    s = np.arange(128)
    c["c_tri64"] = ((s[:, None] % 64) <= np.arange(64)[None, :]).astype(np.float32)
    tri = (s[:, None] <= s[None, :]).astype(np.float32)
    c["c_tri128"] = (tri / 16.0).astype(np.float32)
    gam = np.array([1.0 - 2.0 ** (-5.0 - h) for h in range(4)], np.float64)
    rmask = np.zeros((128, 4, 128), np.float64); rG = np.zeros((128, 4, 128), np.float64); rkd = np.zeros((128, 4), np.float64)
    for h in range(4):
        rmask[:, h, :] = tri * (gam[h] ** (-(s[:, None] + 1.0))) / 16.0
        rG[:, h, :] = (gam[h] ** (s[None, :] + 1.0))
        rkd[:, h] = gam[h] ** (127.0 - s) / 16.0
    c["c_rmask"] = rmask.astype(np.float32); c["c_rG"] = rG.astype(np.float32); c["c_rkd"] = rkd.astype(np.float32)
    rs = np.ones((128, TP), np.float32); rs[:, 0::64] = 0.0
    c["c_reset"] = rs
    pos = np.concatenate([np.arange(NTOK, dtype=np.float32), np.full((NS,), float(PAST), np.float32)])
    theta = (1.0 / (10000.0 ** np.linspace(0.0, 1.0, 128, dtype=np.float32))).astype(np.float32)
    ang = (pos[:, None] * theta[None, :]).astype(np.float32)
    cosv = np.cos(ang).astype(np.float32); sinv = np.sin(ang).astype(np.float32)
    cos_t = np.zeros((128, 2, NT), np.float32); sin_t = np.zeros((128, 2, NT), np.float32)
    for kc in range(2):
        for f in range(128):
            jf = (kc * 128 + f) // 2
            cos_t[f, kc, :] = cosv[:, jf]
            sin_t[f, kc, :] = sinv[:, jf] * (-1.0 if f % 2 == 0 else 1.0)
    c["c_cos"] = cos_t; c["c_sin"] = sin_t
    sel = np.zeros((4, 4, 128), np.float32)
    for h in range(4):
        sel[h, h, :] = 1.0
    c["c_sel"] = sel
    c["c_i16"] = np.eye(16, dtype=np.float32)
    return c


_cache = {}


def prep(L, NP, NB, ncores, x_prompt, x_sample, state_hgrn, state_ret, state_mlstm_c, state_mlstm_n, state_mlstm_m,
         w_in, hgrn_lb_logits, hgrn_norm, ret_norm, mlstm_norm, mlstm_b_i, mlstm_b_f, w_branch, w_out, ln_g, ln_b):
    f = lambda a: np.ascontiguousarray(np.asarray(a, dtype=np.float32))
    consts = make_consts(NP)
    pvec = f(np.concatenate([hgrn_lb_logits, hgrn_norm, ret_norm, mlstm_norm, ln_g, ln_b], axis=1))
    bif = f(np.stack([np.asarray(mlstm_b_i).T, np.asarray(mlstm_b_f).T], axis=1))
    w_in = f(w_in); w_branch = f(w_branch); w_out = f(w_out)
    in_maps = []
    nsb = np.asarray(x_sample).shape[0]
    for c in range(ncores):
        b = c % NB
        s0 = (c * NS) % nsb
        sl = slice(s0, s0 + NS)
        m = dict(consts)
        m.update({
            "xp": f(x_prompt[b]), "xs": f(np.asarray(x_sample)[sl, 0, :]),
            "st_h": f(np.asarray(state_hgrn)[:, sl]), "st_r": f(np.asarray(state_ret)[:, sl]), "st_c": f(np.asarray(state_mlstm_c)[:, sl]),
            "st_n": f(np.asarray(state_mlstm_n)[:, sl].reshape(L, NS, 1024)), "st_m": f(np.asarray(state_mlstm_m)[:, sl]),
            "w_in": w_in, "w_br": w_branch, "w_o": w_out, "pvec": pvec, "bif": bif,
        })
        in_maps.append(m)
    return in_maps


def gather(R, L, NB, nsb):
    nsc = nsb // NS
    y_prompt = np.stack([R[b]["yp"] for b in range(NB)], axis=0)
    y_sample = np.concatenate([R[c]["ys"] for c in range(nsc)], axis=0)[:, None, :]
    hg_p = np.stack([R[b]["o_hp"] for b in range(NB)], axis=1)
    ret_p = np.stack([R[b]["o_rp"] for b in range(NB)], axis=1)
    mc_p = np.stack([R[b]["o_cp"] for b in range(NB)], axis=1)
    mn_p = np.stack([R[b]["o_np"] for b in range(NB)], axis=1)
    mm_p = np.stack([R[b]["o_mp"] for b in range(NB)], axis=1)
    hg_s = np.concatenate([R[c]["o_hs"] for c in range(nsc)], axis=1)
    ret_s = np.concatenate([R[c]["o_rs"] for c in range(nsc)], axis=1)
    mc_s = np.concatenate([R[c]["o_cs"] for c in range(nsc)], axis=1)
    mn_s = np.concatenate([R[c]["o_ns"] for c in range(nsc)], axis=1).reshape(L, nsb, 4, 256)
    mm_s = np.concatenate([R[c]["o_ms"] for c in range(nsc)], axis=1)
    return (y_prompt, y_sample, hg_p, ret_p, mc_p, mn_p, mm_p, hg_s, ret_s, mc_s, mn_s, mm_s)


def kernel(**inputs):
    L, NP, NB = 4, 4, 4
    key = (L, NP)
    if key not in _cache:
        _cache[key] = build(L, NP)
    in_maps = prep(L, NP, NB, 8, **inputs)
    res = run_bass_kernel_spmd(_cache[key], in_maps, core_ids=list(range(8)))
    return gather(res.results, L, NB, 128)
```

```python
import math
import numpy as np
from contextlib import ExitStack
import concourse.bass as bass
import concourse.mybir as mybir
from concourse.bass_utils import run_bass_kernel_spmd

F32 = mybir.dt.float32
BF16 = mybir.dt.bfloat16
ALU = mybir.AluOpType
AF = mybir.ActivationFunctionType
AX = mybir.AxisListType

D = 2048
DIN = 19464
BW = 1024
TP = 512
NS = 16
NW = TP + NS
PAST = 16384
GATE0 = 13320


class Sem:
    def __init__(self, h):
        self.h = h
        self.v = 0


class Res:
    __slots__ = ("w", "r")

    def __init__(self):
        self.w = None
        self.r = {}


class Eng:
    def __init__(self, name, eng, sem, inorder=False):
        self.name = name
        self.eng = eng
        self.sem = sem
        self.waited = {}
        self.inorder = inorder


class FW:
    def __init__(self, nc, ctx):
        self.nc = nc
        self.ctx = ctx
        self.allsems = []
        self.pe = Eng("pe", nc.tensor, self.sem("pe"), inorder=True)
        self.act = Eng("act", nc.scalar, self.sem("act"))
        self.dve = Eng("dve", nc.vector, self.sem("dve"))
        self.pool = Eng("pool", nc.gpsimd, self.sem("pool"))
        self.sp = Eng("sp", nc.sync, self.sem("sp"))
        self.res = {}
        self.nops = 0
        self.tag = ""
        self.petags = []

    def sem(self, name):
        s = Sem(self.ctx.enter_context(self.nc.semaphore(name)))
        self.allsems.append(s)
        return s

    def sb(self, name, shape, dt):
        t = self.ctx.enter_context(self.nc.sbuf_tensor(name, list(shape), dt))
        self.res[name] = Res()
        return t

    def R(self, name):
        if name not in self.res:
            self.res[name] = Res()
        return self.res[name]

    def op(self, E, fn, reads=(), writes=(), sem=None):
        reads = [self.R(r) if isinstance(r, str) else r for r in reads]
        writes = [self.R(w) if isinstance(w, str) else w for w in writes]
        deps = []
        for r in reads:
            if r.w is not None:
                deps.append(r.w)
        for w in writes:
            if w.w is not None:
                deps.append(w.w)
            deps.extend(w.r.items())
        for (s, v) in deps:
            if s is E.sem and E.inorder:
                continue
            if E.waited.get(s, 0) >= v:
                continue
            E.eng.wait_ge(s.h, v)
            E.waited[s] = v
        ins = fn()
        self.nops += 1
        if E.inorder:
            self.petags.append(self.tag)
        if sem is not None:
            S = sem
            S.v += 16
            ins.then_inc(S.h, 16)
        else:
            S = E.sem
            S.v += 1
            ins.then_inc(S.h, 1)
        for r in reads:
            if r.r.get(S, 0) < S.v:
                r.r[S] = S.v
        for w in writes:
            w.w = (S, S.v)
            w.r = {}
        return ins

    def fence(self, sem, names):
        for n in names:
            self.R(n).w = (sem, sem.v)


def build(L, NP):
    nc = bass.Bass("TRN2", target_bir_lowering=False)
    NTOK = NP * TP
    NT = NTOK + NS
    NCH = TP // 128
    ALPHA = float((2 * L) ** 0.25)

    def din(name, shape):
        return nc.dram_tensor(name, list(shape), F32, kind="ExternalInput").ap()

    def dout(name, shape):
        return nc.dram_tensor(name, list(shape), F32, kind="ExternalOutput").ap()

    xp = din("xp", [NTOK, D]); xs = din("xs", [NS, D])
    st_h = din("st_h", [L, NS, 8, 128, 128]); st_r = din("st_r", [L, NS, 4, 256, 256]); st_c = din("st_c", [L, NS, 4, 256, 256])
    st_n = din("st_n", [L, NS, 1024]); st_m = din("st_m", [L, NS, 4])
    w_in = din("w_in", [L, D, DIN]); w_br = din("w_br", [L, 3, BW, D]); w_o = din("w_o", [L, D, D])
    pvec = din("pvec", [L, 8192]); bif = din("bif", [4, 2, L])
    c_id = din("c_id", [128, 128]); c_perm = din("c_perm", [128, 128]); c_tri64 = din("c_tri64", [128, 64]); c_tri128 = din("c_tri128", [128, 128])
    c_rmask = din("c_rmask", [128, 4, 128]); c_rG = din("c_rG", [128, 4, 128]); c_rkd = din("c_rkd", [128, 4])
    c_reset = din("c_reset", [128, TP]); c_cos = din("c_cos", [128, 2, NT]); c_sin = din("c_sin", [128, 2, NT]); c_sel = din("c_sel", [4, 4, 128])
    c_i16 = din("c_i16", [16, 16])
    yp = dout("yp", [NTOK, D]); ys = dout("ys", [NS, D])
    o_hp = dout("o_hp", [L, 8, 128, 128]); o_rp = dout("o_rp", [L, 4, 256, 256]); o_cp = dout("o_cp", [L, 4, 256, 256])
    o_np = dout("o_np", [L, 4, 256]); o_mp = dout("o_mp", [L, 4])
    o_hs = dout("o_hs", [L, NS, 8, 128, 128]); o_rs = dout("o_rs", [L, NS, 4, 256, 256]); o_cs = dout("o_cs", [L, NS, 4, 256, 256])
    o_ns = dout("o_ns", [L, NS, 1024]); o_ms = dout("o_ms", [L, NS, 4])
    XF = nc.dram_tensor("xf", [2, 16, 128, NT], F32, kind="Internal").ap()
    GAM = [1.0 - 2.0 ** (-5.0 - h) for h in range(4)]

    with ExitStack() as ctx:
        fw = FW(nc, ctx)
        pe, act, dve, pool, sp = fw.pe, fw.act, fw.dve, fw.pool, fw.sp
        sb = fw.sb
        s_misc = fw.sem("misc")
        s_w = [fw.sem(f"w{i}") for i in range(3)]
        s_mu = [fw.sem(f"mu{i}") for i in range(3)]
        s_x = fw.sem("x"); s_xr = [fw.sem(f"xr{i}") for i in range(2)]; s_xw = [fw.sem(f"xw{i}") for i in range(2)]
        s_st = fw.sem("st"); s_tab = fw.sem("tab"); s_wg = fw.sem("wg"); s_wb = fw.sem("wb")
        osems = {}

        def osem(name):
            if name not in osems:
                osems[name] = fw.sem("o_" + name)
            return osems[name]

        pool_hist = []

        def dma(E, out, in_, reads, writes, sem, slow=False):
            q = {"sp": nc.sync, "pool": nc.gpsimd, "act": nc.scalar}[E.name]
            if E is pool:
                if len(pool_hist) >= 4:
                    s_, v_ = pool_hist[-4]
                    if E.waited.get(s_, 0) < v_:
                        E.eng.wait_ge(s_.h, v_)
                        E.waited[s_] = v_
                pool_hist.append((sem, sem.v + 16))
            if slow:
                return fw.op(E, lambda: q.dma_start(out=out, in_=in_, allow_slow_non_contiguous=True), reads=reads, writes=writes, sem=sem)
            return fw.op(E, lambda: q.dma_start(out=out, in_=in_), reads=reads, writes=writes, sem=sem)

        def MM(out, lhsT, rhs, start, stop, reads, writes, skip=False):
            if skip:
                return fw.op(pe, lambda: nc.tensor.matmul(out, lhsT=lhsT, rhs=rhs, start=start, stop=stop, skip_group_check=True), reads=reads, writes=writes)
            return fw.op(pe, lambda: nc.tensor.matmul(out, lhsT=lhsT, rhs=rhs, start=start, stop=stop), reads=reads, writes=writes)

        def TR(out, in_, ident, reads, writes):
            return fw.op(pe, lambda: nc.tensor.transpose(out, in_, ident), reads=reads, writes=writes)

        def ACT(out, in_, func, reads, writes, scale=1.0, bias=None):
            if bias is None:
                return fw.op(act, lambda: nc.scalar.activation(out=out, in_=in_, func=func, scale=scale), reads=reads, writes=writes)
            return fw.op(act, lambda: nc.scalar.activation(out=out, in_=in_, func=func, scale=scale, bias=bias), reads=reads, writes=writes)

        def TT(E, out, in0, in1, op, reads, writes):
            e = nc.vector if E is dve else nc.gpsimd
            return fw.op(E, lambda: e.tensor_tensor(out=out, in0=in0, in1=in1, op=op), reads=reads, writes=writes)

        def TS(E, out, in0, s1, s2, op0, op1, reads, writes):
            e = nc.vector if E is dve else nc.gpsimd
            if op1 is None:
                return fw.op(E, lambda: e.tensor_scalar(out=out, in0=in0, scalar1=s1, scalar2=None, op0=op0), reads=reads, writes=writes)
            return fw.op(E, lambda: e.tensor_scalar(out=out, in0=in0, scalar1=s1, scalar2=s2, op0=op0, op1=op1), reads=reads, writes=writes)

        def STT(out, in0, scalar, in1, op0, op1, reads, writes):
            return fw.op(dve, lambda: nc.vector.scalar_tensor_tensor(out=out, in0=in0, scalar=scalar, in1=in1, op0=op0, op1=op1), reads=reads, writes=writes)

        def CP(E, out, in_, reads, writes):
            if E is act:
                return fw.op(act, lambda: nc.scalar.copy(out=out, in_=in_), reads=reads, writes=writes)
            e = nc.vector if E is dve else nc.gpsimd
            return fw.op(E, lambda: e.tensor_copy(out=out, in_=in_), reads=reads, writes=writes)

        def MSET(E, ap, val, writes):
            e = nc.vector if E is dve else nc.gpsimd
            return fw.op(E, lambda: e.memset(ap, val), writes=writes)

        def SCAN(out, d0, d1, init, op0, op1, reads, writes):
            return fw.op(dve, lambda: nc.vector.tensor_tensor_scan(out=out, data0=d0, data1=d1, initial=init, op0=op0, op1=op1), reads=reads, writes=writes)

        def RECIP(out, in_, reads, writes):
            return fw.op(dve, lambda: nc.vector.reciprocal(out=out, in_=in_), reads=reads, writes=writes)

        ident32 = sb("ident32", [128, 128], F32); ident16 = sb("ident16", [128, 128], BF16); perm32 = sb("perm32", [128, 128], F32)
        tri64 = sb("tri64", [128, 64], F32); tri128 = sb("tri128", [128, 128], F32)
        rmask = sb("rmask", [128, 4, 128], F32); rG = sb("rG", [128, 4, 128], F32); rkd = sb("rkd", [128, 4], F32)
        reset = sb("reset", [128, TP], BF16); sel = sb("sel", [4, 4, 128], F32); i16 = sb("i16", [16, 16], F32)
        ones16 = sb("ones16", [128, 128], BF16); ones32 = sb("ones32", [128, 128], F32); onesrow = sb("onesrow", [4, TP], BF16)
        epsn = sb("epsn", [128, 1], F32); epsl = sb("epsl", [128, 1], F32)
        BIF = sb("BIF", [4, 2, L], F32)
        PT32 = sb("PT32", [L, 1024], F32); PAR = sb("PAR", [128, 64, L], F32)
        lbv = sb("lbv", [128, 8, L], F32); oml = sb("oml", [128, 8, L], F32); lbe = sb("lbe", [128, 8, L], F32); lbs = sb("lbs", [128, 8], F32)
        cl = [(ident32, c_id), (perm32, c_perm), (tri64, c_tri64), (tri128, c_tri128), (rmask, c_rmask), (rG, c_rG), (rkd, c_rkd),
              (sel, c_sel), (i16, c_i16), (BIF, bif)]
        for t, src in cl:
            dma(sp, t[:], src, [], [t.name], s_misc)
        dma(pool, ident16[:], c_id, [], ["ident16"], s_misc)
        dma(pool, reset[:], c_reset, [], ["reset"], s_misc)
        fw.fence(s_misc, [t.name for t, _ in cl] + ["ident16", "reset"])
        MSET(dve, ones16[:], 1.0, ["ones16"]); MSET(dve, ones32[:], 1.0, ["ones32"]); MSET(dve, onesrow[:], 1.0, ["onesrow"])
        MSET(dve, epsn[:], 1e-6, ["epsn"]); MSET(dve, epsl[:], 1e-5, ["epsl"])

        PB = [ctx.enter_context(nc.psum_tensor(f"pb{i}", [128, 512], F32)) for i in range(8)]
        PBn = [f"pb{i}" for i in range(8)]

        for pc in range(8):
            dma(sp, PT32[:], pvec[:, pc * 1024:(pc + 1) * 1024], [], ["PT32"], osem("pt32"))
            for c8 in range(8):
                c = pc * 8 + c8
                TR(PB[0][:, c * L:(c + 1) * L], PT32[0:L, c8 * 128:(c8 + 1) * 128], ident32[0:L, 0:L], ["PT32", "ident32"], [PBn[0]])
        CP(dve, PAR[:].rearrange("p c l -> p (c l)"), PB[0][:, 0:64 * L], [PBn[0]], ["PAR"])
        ACT(lbe[:], PAR[:, 0:8, :], AF.Exp, ["PAR"], ["lbe"])
        fw.op(dve, lambda: nc.vector.tensor_reduce(out=lbs[:], in_=lbe[:], axis=AX.X, op=ALU.add), reads=["lbe"], writes=["lbs"])
        RECIP(lbs[:], lbs[:], ["lbs"], ["lbs"])
        TT(dve, lbe[:], lbe[:], lbs[:].unsqueeze(2).to_broadcast([128, 8, L]), ALU.mult, ["lbe", "lbs"], ["lbe"])
        MSET(dve, lbv[:], 0.0, ["lbv"])
        for l in range(1, L):
            TT(dve, lbv[:, :, l:l + 1], lbv[:, :, l - 1:l], lbe[:, :, l:l + 1], ALU.add, ["lbv", "lbe"], ["lbv"])
        TS(dve, oml[:], lbv[:], -1.0, 1.0, ALU.mult, ALU.add, ["lbv"], ["oml"])

        xT16 = sb("xT16", [128, 16, NW], BF16)
        U = sb("U", [128, 16 * NW], F32)
        yT = U[:].bitcast(BF16)[:, 0:24 * NW].rearrange("p (c t) -> p c t", t=NW)
        Hbuf = U[:].rearrange("p (c t) -> p c t", t=NW)
        S_h = sb("S_h", [128, 8, 128], F32); S_r = sb("S_r", [128, 8, 256], F32); S_c = sb("S_c", [128, 8, 257], F32)
        cosT = sb("cosT", [128, 2, NW], F32); sinT = sb("sinT", [128, 2, NW], F32)
        WT = [sb(f"WT{i}", [128, 16, 256], BF16) for i in range(3)]
        wcnt = [0]
        Ta = sb("Ta", [128, NW], F32); Tb = sb("Tb", [128, NW], F32); Tc = sb("Tc", [128, NW], F32); Td = sb("Td", [128, NW], F32)
        Te = sb("Te", [128, NW], F32); Tf = sb("Tf", [128, NW], F32); Tg = sb("Tg", [128, NW], F32)
        PH1 = sb("PH1", [128, 8 * NW], F32)
        Tz = PH1[:, 0:2 * NW].rearrange("p (c t) -> p c t", t=NW); To = PH1[:, 2 * NW:4 * NW].rearrange("p (c t) -> p c t", t=NW)
        H32 = PH1[:, 4 * NW:6 * NW].rearrange("p (c t) -> p c t", t=NW); SQ32 = PH1[:, 6 * NW:8 * NW].rearrange("p (c t) -> p c t", t=NW)
        mg16 = PH1[:].bitcast(BF16).rearrange("p (c t) -> p c t", t=NW)
        MGALL = ["mg16", "Tz", "To", "H32", "SQ32"]
        Q16 = sb("Q16", [128, 2, TP], BF16); K16 = sb("K16", [128, 2, TP], BF16); Q32 = sb("Q32", [128, 2, NW], F32); KP32 = sb("KP32", [128, 2, NS], F32)
        A32 = sb("A32", [128, NS], F32)
        KD16 = sb("KD16", [128, TP], BF16); KDT16 = sb("KDT16", [128, 4, 256], BF16); V16 = sb("V16", [128, 4, 257], BF16)
        AM16 = [sb(f"AM16_{i}", [128, 128], BF16) for i in range(2)]
        NREP = sb("NREP", [128, 2, 128], F32)
        Vs16 = sb("Vs16", [16, 256], BF16); BDs = sb("BD", [128, 16 * 256], BF16)
        BD = BDs[0:16, :].rearrange("p (s v) -> p s v", v=256)
        WBt = BDs[:, 0:8 * 3 * 128].rearrange("p (c n d) -> p c n d", n=3, d=128)
        Ssmp = sb("Ssmp", [128, 16, 256], F32); T1 = sb("T1", [128, 512], F32)
        Sflat = Ssmp[:].rearrange("p s v -> p (s v)")
        NS32 = Sflat[0:16, 0:1024]; NO32 = Sflat[0:16, 1024:2048]
        WG = Sflat.bitcast(BF16)[:, 0:16 * 3 * 128].rearrange("p (k n d) -> p k n d", n=3, d=128)
        GI = Ta[0:4, :]; GLF = Tb[0:4, :]; GB = Tc[0:4, 0:TP]; GU = Td[0:4, 0:TP]; GM = Te[0:4, 0:TP]
        GE1 = Tf[0:4, 0:TP]; GE2 = Tg[0:4, 0:TP]; GT = PH1[0:4, 0:TP]; GMS = sb("GMS", [4, NCH], F32); GDEC = sb("GDEC", [4, NCH], F32)
        sb_gt2 = sb("GT2", [4, TP], F32)
        Bc = sb("Bc", [4, 1], F32); Mc = sb("Mc", [4, 1], F32); Mfin = sb("Mfin", [4, 1], F32)
        ET = sb("ET", [128, NCH, 8], F32); THRB = sb("THRB", [128, TP], F32); DECB = sb("DECB", [128, 4, NCH], F32)
        SMS = sb("SMS", [4, NS], F32); SMm = sb("SMm", [4, NS], F32); SLI = sb("SLI", [4, NS], F32); SW = sb("SW", [4, 3, NS], F32); SWB = sb("SWB", [128, 4, 3, NS], F32)
        NT32 = sb("NT32", [128, 8, NS], F32)
        XIN = U[:, 0:D]; XS = U[:, D:2 * D].rearrange("p (k t) -> p k t", t=128)
        XR = [sb(f"XR{i}", [128, NW], F32) for i in range(2)]
        MS32 = Ta; MA32 = Tb; MB32 = Tc
        ST1 = sb("ST1", [128, NW], F32); ST2 = sb("ST2", [128, NW], F32); SQH = sb("SQH", [128, NW], F32)
        MSET(dve, V16[:, :, 256:257], 1.0, ["V16"])

        def xfres(par, kc):
            return f"xf{par}_{kc}"

        tiles = [(xp[i * 128:(i + 1) * 128, :], 128, i * 128) for i in range(NTOK // 128)] + [(xs, NS, NTOK)]
        for (src, nr, c0) in tiles:
            dma(sp, XIN[0:nr, :], src, [], ["U"], osem("xin"))
            for g in range(4):
                for j in range(4):
                    kc = g * 4 + j
                    TR(PB[g % 2][:, j * 128:j * 128 + nr], XIN[0:nr, kc * 128:(kc + 1) * 128], ident32[0:nr, 0:nr], ["U", "ident32"], [PBn[g % 2]])
                CP(act if g % 2 else dve, XS[:, g * 4:(g + 1) * 4, 0:nr], PB[g % 2][:].rearrange("p (j t) -> p j t", t=128)[:, :, 0:nr], [PBn[g % 2]], ["U"])
            dma(sp, XF[0, :, :, c0:c0 + nr].rearrange("k p t -> p k t"), XS[:, :, 0:nr], ["U"], [xfres(0, k) for k in range(16)], osem("xs"))

        def load_w(segs):
            i = wcnt[0] % 3
            wcnt[0] += 1
            off = 0
            for s in segs:
                ncol = s.shape[1]
                dma(pool, WT[i][:, :, off:off + ncol], s.rearrange("(k p) c -> p k c", p=128), [], [f"WT{i}"], s_w[i])
                off += ncol
            return WT[i], f"WT{i}"

        def proj_fm(bank, wt, wn, c0, ncol, blk):
            fw.tag = fw.tag.split(".")[0] + ".proj"
            lc0, n = blk
            for kc in range(16):
                MM(PB[bank][0:ncol, 0:n], wt[:, kc, c0:c0 + ncol], xT16[:, kc, lc0:lc0 + n], kc == 0, kc == 15, [wn, "xT16"], [PBn[bank]])
            return PB[bank][0:ncol, 0:n]

        def proj_tm(bank, wt, wn, c0, ncol, lc0, nr):
            for kc in range(16):
                MM(PB[bank][0:nr, 0:ncol], xT16[:, kc, lc0:lc0 + nr], wt[:, kc, c0:c0 + ncol], kc == 0, kc == 15, [wn, "xT16"], [PBn[bank]])

        bankctr = [0]

        def nb():
            bankctr[0] ^= 1
            return bankctr[0]

        def chunk_scan(c, nkc, nvc, Sst, sname, h, mask_fn, a_fn, den):
            fw.tag = fw.tag.split(".")[0] + ".scan"
            dv = nvc * 128
            ncol = dv + (1 if den else 0)
            for j in range(TP // c):
                i = (j * c) // 128
                base = (j * c) % 128
                rows = slice(base, base + c)
                cs = slice(j * c, (j + 1) * c)
                am = AM16[j % 2]; amn = f"AM16_{j % 2}"
                for kc in range(nkc):
                    MM(PB[4][rows, 0:c], K16[:, kc, cs], Q16[:, kc, cs], kc == 0, kc == nkc - 1, ["K16", "Q16"], [PBn[4]])
                mask_fn(j, rows, am, amn, c)
                for kc in range(nkc):
                    MM(PB[5 + kc][:, 0:ncol], KDT16[rows, i, kc * 128:(kc + 1) * 128], V16[rows, i, 0:ncol], True, True, ["KDT16", "V16"], [PBn[5 + kc]])
                for vc in range(nvc):
                    MM(PB[vc][:, cs], V16[rows, i, vc * 128:(vc + 1) * 128], am[rows, 0:c], True, False, ["V16", amn], [PBn[vc]])
                    for kc in range(nkc):
                        MM(PB[vc][:, cs], Sst[:, h * nkc + kc, vc * 128:(vc + 1) * 128], Q32[:, kc, cs], False, kc == nkc - 1, [sname, "Q32"], [PBn[vc]])
                if den:
                    for kc in range(nkc):
                        CP(dve, NREP[:, kc, :], Sst[:, h * nkc + kc, 256:257].to_broadcast([128, 128]), [sname], ["NREP"])
                    CP(act, Tg[:, 0:c], am[:, 0:c], [amn], ["Tg"])
                    MM(PB[2][:, cs], ones32[:, :], Tg[:, 0:c], True, False, ["ones32", "Tg"], [PBn[2]])
                    for kc in range(nkc):
                        MM(PB[2][:, cs], NREP[:, kc, :], Q32[:, kc, cs], False, kc == nkc - 1, ["NREP", "Q32"], [PBn[2]])
                for kc in range(nkc):
                    STT(Sst[:, h * nkc + kc, 0:ncol], Sst[:, h * nkc + kc, 0:ncol], a_fn(j), PB[5 + kc][:, 0:ncol], ALU.mult, ALU.add,
                        [sname, PBn[5 + kc], "DECB", "Te"], [sname])

        def kd_transposes(nkc, srcs, scale_fn):
            fw.tag = fw.tag.split(".")[0] + ".kdT"
            for i in range(NCH):
                for kc in range(nkc):
                    ap, rn = srcs[kc]
                    pt = PB[4][:].bitcast(BF16)[:, 512 + kc * 128:512 + (kc + 1) * 128]
                    TR(pt, ap[:, i * 128:(i + 1) * 128], ident16[:], [rn, "ident16"], [PBn[4]])
                    sc = scale_fn(i)
                    if sc is None:
                        CP(act, KDT16[:, i, kc * 128:(kc + 1) * 128], pt, [PBn[4]], ["KDT16"])
                    else:
                        fw.op(act, lambda: nc.scalar.activation(out=KDT16[:, i, kc * 128:(kc + 1) * 128], in_=pt, func=AF.Copy, scale=sc),
                              reads=[PBn[4], "ET", "rkd"], writes=["KDT16"])

        def v_tokmajor(wt, wn, c0, dv, blocks):
            fw.tag = fw.tag.split(".")[0] + ".vtok"
            for (lc0, n), bt in blocks:
                if bt == "P":
                    for i in range(NCH):
                        b = nb()
                        proj_tm(b, wt, wn, c0, dv, lc0 + i * 128, 128)
                        CP(act, V16[:, i, 0:dv], PB[b][:, 0:dv], [PBn[b]], ["V16"])
                else:
                    b = nb()
                    proj_tm(b, wt, wn, c0, dv, lc0, NS)
                    CP(act, Vs16[:, 0:dv], PB[b][0:NS, 0:dv], [PBn[b]], ["Vs16"])
                    TT(dve, BD[:, :, 0:dv], Vs16[:, 0:dv].unsqueeze(1).to_broadcast([NS, NS, dv]), i16[:].unsqueeze(2).to_broadcast([NS, NS, dv]),
                       ALU.mult, ["Vs16", "i16"], ["BD"])

        def sample_update(l, dv, nkc, h, st_in, st_out, a_fn, kp_fn, nvc):
            per = 512 // dv
            fw.tag = fw.tag.split(".")[0] + ".sample"
            for kc in range(nkc):
                dma(sp, Ssmp[:, :, 0:dv], st_in[l, :, h, kc * 128:(kc + 1) * 128, :].rearrange("s k v -> k s v"), [], ["Ssmp"], s_st)
                a_ap = a_fn(kc); kp_ap = kp_fn(kc)
                for g in range(NS // per):
                    ss = slice(g * per, (g + 1) * per)
                    MM(PB[3][:, 0:512], ones16[0:NS, :], BD[:, ss, 0:dv], True, True, ["ones16", "BD"], [PBn[3]])
                    TT(dve, T1[:].rearrange("p (s v) -> p s v", v=dv), PB[3][:, 0:512].rearrange("p (s v) -> p s v", v=dv),
                       kp_ap[:, ss].unsqueeze(2).to_broadcast([128, per, dv]), ALU.mult, [PBn[3], "KP32", "Td"], ["T1"])
                    TT(dve, Ssmp[:, ss, 0:dv], Ssmp[:, ss, 0:dv], a_ap[:, ss].unsqueeze(2).to_broadcast([128, per, dv]), ALU.mult,
                       ["Ssmp", "A32", "Tf", "SWB"], ["Ssmp"])
                    TT(dve, Ssmp[:, ss, 0:dv], Ssmp[:, ss, 0:dv], T1[:].rearrange("p (s v) -> p s v", v=dv), ALU.add, ["Ssmp", "T1"], ["Ssmp"])
                dma(sp, st_out[l, :, h, kc * 128:(kc + 1) * 128, :].rearrange("s k v -> k s v"), Ssmp[:, :, 0:dv], ["Ssmp"], [], osem("ssmp"))
                for s in range(NS):
                    for vc in range(nvc):
                        MM(PB[7][:, vc * 16 + s:vc * 16 + s + 1], Ssmp[:, s, vc * 128:(vc + 1) * 128], Q32[:, kc, TP + s:TP + s + 1],
                           kc == 0 and s == 0 and vc == 0, kc == nkc - 1, ["Ssmp", "Q32"], [PBn[7]], skip=True)

        def head_norm(kind, nvc, gain_chunk0, l, ycol0, n):
            fw.tag = fw.tag.split(".")[0] + ".norm"
            for vc in range(nvc):
                ACT(SQ32[:, vc, 0:n], H32[:, vc, 0:n], AF.Square, ["H32"], ["SQ32"])
            pieces = [(0, 512, 3)] + ([(512, n - 512, 2)] if n > 512 else [])
            for (c0, nn, bk) in pieces:
                for vc in range(nvc):
                    MM(PB[bk][:, 0:nn], ones32[:], SQ32[:, vc, c0:c0 + nn], vc == 0, vc == nvc - 1, ["ones32", "SQ32"], [PBn[bk]])
                CP(act, ST2[:, c0:c0 + nn], PB[bk][:, 0:nn], [PBn[bk]], ["ST2"])
                if kind == "ln":
                    for vc in range(nvc):
                        MM(PB[bk][:, 0:nn], ones32[:], H32[:, vc, c0:c0 + nn], vc == 0, vc == nvc - 1, ["ones32", "H32"], [PBn[bk]])
                    CP(act, ST1[:, c0:c0 + nn], PB[bk][:, 0:nn], [PBn[bk]], ["ST1"])
            dvv = float(nvc * 128)
            if kind == "ln":
                TS(dve, ST1[:, 0:n], ST1[:, 0:n], 1.0 / dvv, None, ALU.mult, None, ["ST1"], ["ST1"])
                TT(dve, SQH[:, 0:n], ST1[:, 0:n], ST1[:, 0:n], ALU.mult, ["ST1"], ["SQH"])
                STT(ST2[:, 0:n], ST2[:, 0:n], 1.0 / dvv, SQH[:, 0:n], ALU.mult, ALU.subtract, ["ST2", "SQH"], ["ST2"])
                ACT(ST2[:, 0:n], ST2[:, 0:n], AF.Ln, ["ST2", "epsn"], ["ST2"], scale=1.0, bias=epsn[:])
            else:
                ACT(ST2[:, 0:n], ST2[:, 0:n], AF.Ln, ["ST2", "epsn"], ["ST2"], scale=1.0 / dvv, bias=epsn[:])
            ACT(ST2[:, 0:n], ST2[:, 0:n], AF.Exp, ["ST2"], ["ST2"], scale=-0.5)
            for vc in range(nvc):
                g_col = PAR[:, gain_chunk0 + vc, l:l + 1]
                if kind == "ln":
                    TT(dve, SQ32[:, vc, 0:n], H32[:, vc, 0:n], ST1[:, 0:n], ALU.subtract, ["H32", "ST1"], ["SQ32"])
                    STT(SQ32[:, vc, 0:n], SQ32[:, vc, 0:n], g_col, ST2[:, 0:n], ALU.mult, ALU.mult, ["SQ32", "ST2", "PAR"], ["SQ32"])
                else:
                    STT(SQ32[:, vc, 0:n], H32[:, vc, 0:n], g_col, ST2[:, 0:n], ALU.mult, ALU.mult, ["H32", "ST2", "PAR"], ["SQ32"])
                TT(dve, yT[:, ycol0 + vc, 0:n], SQ32[:, vc, 0:n], Tz[:, vc, 0:n], ALU.mult, ["SQ32", "Tz"], ["U"])

        def rotary(dst32, dname, kc, n):
            for (c0, nn) in [(0, 512)] + ([(512, n - 512)] if n > 512 else []):
                b = nb()
                MM(PB[b][:, 0:nn], perm32[:], Ta[:, c0:c0 + nn], True, True, ["perm32", "Ta"], [PBn[b]])
                TT(dve, Tc[:, c0:c0 + nn], PB[b][:, 0:nn], sinT[:, kc, c0:c0 + nn], ALU.mult, [PBn[b], "sinT"], ["Tc"])
            TT(dve, Tb[:, 0:n], Ta[:, 0:n], cosT[:, kc, 0:n], ALU.mult, ["Ta", "cosT"], ["Tb"])
            TT(dve, dst32, Tb[:, 0:n], Tc[:, 0:n], ALU.add, ["Tb", "Tc"], [dname])

        for l in range(L):
            par = l % 2
            for p in range(NP):
                first = (p == 0)
                last = (p == NP - 1)
                blocks = [((0, TP), "P")] + ([((TP, NS), "S")] if first else [])
                n_all = TP + (NS if first else 0)
                xres = [xfres(par, k) for k in range(16)]
                dma(pool, xT16[:, :, 0:TP], XF[par, :, :, p * TP:(p + 1) * TP].rearrange("k p t -> p k t"), xres, ["xT16"], s_x)
                dma(sp, cosT[:, :, 0:TP], c_cos[:, :, p * TP:(p + 1) * TP], [], ["cosT"], s_tab)
                dma(sp, sinT[:, :, 0:TP], c_sin[:, :, p * TP:(p + 1) * TP], [], ["sinT"], s_tab)
                if first:
                    dma(pool, xT16[:, :, TP:NW], XF[par, :, :, NTOK:NT].rearrange("k p t -> p k t"), xres, ["xT16"], s_x)
                    dma(sp, cosT[:, :, TP:NW], c_cos[:, :, NTOK:NT], [], ["cosT"], s_tab)
                    dma(sp, sinT[:, :, TP:NW], c_sin[:, :, NTOK:NT], [], ["sinT"], s_tab)
                    MSET(dve, S_h[:], 0.0, ["S_h"]); MSET(dve, S_r[:], 0.0, ["S_r"]); MSET(dve, S_c[:], 0.0, ["S_c"])
                    MSET(dve, Bc[:], 0.0, ["Bc"]); MSET(dve, Mc[:], 0.0, ["Mc"])
                fw.fence(s_tab, ["cosT", "sinT"])
                fw.fence(s_x, ["xT16"])

                fw.tag = "hgrn.x"
                for h in range(8):
                    wtA, wnA = load_w([w_in[l][:, sg * BW + h * 128: sg * BW + (h + 1) * 128] for sg in (0, 1)])
                    wtB, wnB = load_w([w_in[l][:, sg * BW + h * 128: sg * BW + (h + 1) * 128] for sg in (2, 3)])
                    for (blk, bt) in blocks:
                        lc0, n = blk
                        b = nb(); ps = proj_fm(b, wtA, wnA, 0, 128, blk)
                        ACT(Q32[:, 0, lc0:lc0 + n], ps, AF.Silu, [PBn[b]], ["Q32"])
                        b = nb(); ps = proj_fm(b, wtA, wnA, 128, 128, blk)
                        ACT(Tf[:, lc0:lc0 + n], ps, AF.Sigmoid, [PBn[b]], ["Tf"])
                        b = nb(); ps = proj_fm(b, wtB, wnB, 128, 128, blk)
                        ACT(Tz[:, 0, lc0:lc0 + n], ps, AF.Silu, [PBn[b]], ["Tz"])
                    v_tokmajor(wtB, wnB, 0, 128, blocks)
                    TS(dve, Tf[:, 0:n_all], Tf[:, 0:n_all], oml[:, h, l:l + 1], lbv[:, h, l:l + 1], ALU.mult, ALU.add, ["Tf", "oml", "lbv"], ["Tf"])
                    TS(dve, Td[:, 0:n_all], Tf[:, 0:n_all], -1.0, 1.0, ALU.mult, ALU.add, ["Tf"], ["Td"])
                    ACT(Tg[:, 0:TP], Tf[:, 0:TP], AF.Ln, ["Tf"], ["Tg"])
                    SCAN(Tb[:, 0:TP], reset[:], Tg[:, 0:TP], 0.0, ALU.mult, ALU.add, ["reset", "Tg"], ["Tb"])
                    ACT(Te[:, 0:TP], Tb[:, 0:TP], AF.Exp, ["Tb"], ["Te"])
                    ACT(Tc[:, 0:TP], Tb[:, 0:TP], AF.Exp, ["Tb"], ["Tc"], scale=-1.0)
                    TT(dve, Q32[:, 0, 0:TP], Q32[:, 0, 0:TP], Te[:, 0:TP], ALU.mult, ["Q32", "Te"], ["Q32"])
                    CP(act, Q16[:, 0, :], Q32[:, 0, 0:TP], ["Q32"], ["Q16"])
                    TT(dve, K16[:, 0, :], Td[:, 0:TP], Tc[:, 0:TP], ALU.mult, ["Td", "Tc"], ["K16"])
                    TT(dve, KD16[:].rearrange("p (j t) -> p j t", t=64), K16[:, 0, :].rearrange("p (j t) -> p j t", t=64),
                       Te[:, 0:TP].rearrange("p (j t) -> p j t", t=64)[:, :, 63:64].to_broadcast([128, TP // 64, 64]), ALU.mult, ["K16", "Te"], ["KD16"])
                    kd_transposes(1, [(KD16, "KD16")], lambda i: None)

                    def hmask(j, rows, am, amn, c):
                        TT(dve, am[rows, 0:c], PB[4][rows, 0:c], tri64[rows, :], ALU.mult, [PBn[4], "tri64"], [amn])
                    chunk_scan(64, 1, 1, S_h, "S_h", h, hmask, lambda j: Te[:, 64 * j + 63:64 * j + 64], False)
                    CP(act, H32[:, 0, 0:TP], PB[0][:, 0:TP], [PBn[0]], ["H32"])
                    if first:
                        sample_update(l, 128, 1, h, st_h, o_hs, lambda kc: Tf[:, TP:NW], lambda kc: Td[:, TP:NW], 1)
                        CP(act, H32[:, 0, TP:NW], PB[7][:, 0:NS], [PBn[7]], ["H32"])
                    head_norm("rms", 1, 8 + h, l, h, n_all)
                if last:
                    dma(sp, o_hp[l].rearrange("h k v -> k h v"), S_h[:], ["S_h"], [], osem("S_h"))

                fw.tag = "ret.x"
                for h in range(4):
                    for sgi, sgn in ((4, "q"), (5, "k")):
                        wt, wn = load_w([w_in[l][:, sgi * BW + h * 256: sgi * BW + (h + 1) * 256]])
                        for kc in range(2):
                            for (blk, bt) in blocks:
                                lc0, n = blk
                                b = nb(); ps = proj_fm(b, wt, wn, kc * 128, 128, blk)
                                CP(act, Ta[:, lc0:lc0 + n], ps, [PBn[b]], ["Ta"])
                            if sgn == "q":
                                rotary(Q32[:, kc, 0:n_all], "Q32", kc, n_all)
                                CP(act, Q16[:, kc, :], Q32[:, kc, 0:TP], ["Q32"], ["Q16"])
                            else:
                                rotary(Td[:, 0:n_all], "Td", kc, n_all)
                                CP(act, K16[:, kc, :], Td[:, 0:TP], ["Td"], ["K16"])
                                if first:
                                    TS(dve, KP32[:, kc, :], Td[:, TP:NW], 1.0 / 16.0, None, ALU.mult, None, ["Td"], ["KP32"])
                    wt, wn = load_w([w_in[l][:, 6 * BW + h * 256: 6 * BW + (h + 1) * 256]])
                    v_tokmajor(wt, wn, 0, 256, blocks)
                    wt, wn = load_w([w_in[l][:, 7 * BW + h * 256: 7 * BW + (h + 1) * 256]])
                    for vc in range(2):
                        for (blk, bt) in blocks:
                            lc0, n = blk
                            b = nb(); ps = proj_fm(b, wt, wn, vc * 128, 128, blk)
                            ACT(Tz[:, vc, lc0:lc0 + n], ps, AF.Silu, [PBn[b]], ["Tz"])
                    kd_transposes(2, [(K16[:, 0, :], "K16"), (K16[:, 1, :], "K16")], lambda i: rkd[:, h:h + 1])

                    def rmaskf(j, rows, am, amn, c):
                        TT(dve, am[rows, 0:c], PB[4][rows, 0:c], rmask[:, h, :], ALU.mult, [PBn[4], "rmask"], [amn])
                    ga = float(GAM[h] ** 128)
                    chunk_scan(128, 2, 2, S_r, "S_r", h, rmaskf, lambda j: ga, False)
                    for vc in range(2):
                        TT(dve, H32[:, vc, 0:TP].rearrange("p (j t) -> p j t", t=128), PB[vc][:, 0:TP].rearrange("p (j t) -> p j t", t=128),
                           rG[:, h, :].unsqueeze(1).to_broadcast([128, NCH, 128]), ALU.mult, [PBn[vc], "rG"], ["H32"])
                    if first:
                        MSET(dve, A32[:], float(GAM[h]), ["A32"])
                        sample_update(l, 256, 2, h, st_r, o_rs, lambda kc: A32, lambda kc: KP32[:, kc, :], 2)
                        for vc in range(2):
                            CP(act, H32[:, vc, TP:NW], PB[7][:, vc * 16:vc * 16 + NS], [PBn[7]], ["H32"])
                    head_norm("ln", 2, 16 + 2 * h, l, 8 + 2 * h, n_all)
                if last:
                    dma(sp, o_rp[l].rearrange("h (kc k) v -> k (h kc) v", k=128), S_r[:], ["S_r"], [], osem("S_r"))

                fw.tag = "mlstm.x"
                wt, wn = load_w([w_in[l][:, 13312:13320]])
                for (blk, bt) in blocks:
                    lc0, n = blk
                    b = nb(); ps = proj_fm(b, wt, wn, 0, 4, blk)
                    ACT(GI[:, lc0:lc0 + n], ps, AF.Identity, [PBn[b], "BIF"], ["Ta"], bias=BIF[:, 0, l:l + 1])
                    b = nb(); ps = proj_fm(b, wt, wn, 4, 4, blk)
                    ACT(GLF[:, lc0:lc0 + n], ps, AF.Sigmoid, [PBn[b], "BIF"], ["Tb"], bias=BIF[:, 1, l:l + 1])
                ACT(GLF[:, 0:n_all], GLF[:, 0:n_all], AF.Ln, ["Tb"], ["Tb"])
                SCAN(GB[:], onesrow[:], GLF[:, 0:TP], Bc[:, 0:1], ALU.mult, ALU.add, ["onesrow", "Tb", "Bc"], ["Tc"])
                TT(dve, GU[:], GI[:, 0:TP], GB[:], ALU.subtract, ["Ta", "Tc"], ["Td"])
                SCAN(GM[:], onesrow[:], GU[:], Mc[:, 0:1], ALU.mult, ALU.max, ["onesrow", "Td", "Mc"], ["Te"])
                GMv = GM[:].rearrange("p (j t) -> p j t", t=128)
                CP(dve, GMS[:, 0:1], Mc[:], ["Mc"], ["GMS"])
                if NCH > 1:
                    CP(dve, GMS[:, 1:NCH].unsqueeze(2), GMv[:, 0:NCH - 1, 127:128], ["Te"], ["GMS"])
                TT(dve, GE1[:].rearrange("p (j t) -> p j t", t=128), GU[:].rearrange("p (j t) -> p j t", t=128),
                   GMS[:].unsqueeze(2).to_broadcast([4, NCH, 128]), ALU.subtract, ["Td", "GMS"], ["Tf"])
                ACT(GE1[:], GE1[:], AF.Exp, ["Tf"], ["Tf"])
                TT(dve, GE2[:].rearrange("p (j t) -> p j t", t=128), GU[:].rearrange("p (j t) -> p j t", t=128),
                   GMv[:, :, 127:128].to_broadcast([4, NCH, 128]), ALU.subtract, ["Td", "Te"], ["Tg"])
                ACT(GE2[:], GE2[:], AF.Exp, ["Tg"], ["Tg"])
                TS(dve, GE2[:], GE2[:], 1.0 / 16.0, None, ALU.mult, None, ["Tg"], ["Tg"])
                TT(dve, GT[:].rearrange("p (j t) -> p j t", t=128), GB[:].rearrange("p (j t) -> p j t", t=128),
                   GMS[:].unsqueeze(2).to_broadcast([4, NCH, 128]), ALU.add, ["Tc", "GMS"], ["Tz"])
                ACT(GT[:], GT[:], AF.Exp, ["Tz"], ["Tz"], scale=-1.0)
                TT(dve, GDEC[:].unsqueeze(2), GMS[:].unsqueeze(2), GMv[:, :, 127:128], ALU.subtract, ["GMS", "Te"], ["GDEC"])
                ACT(GDEC[:], GDEC[:], AF.Exp, ["GDEC"], ["GDEC"])
                CP(dve, Bc[:], GB[:, TP - 1:TP], ["Tc"], ["Bc"])
                CP(dve, Mc[:], GM[:, TP - 1:TP], ["Te"], ["Mc"])
                TT(dve, Mfin[:], GB[:, TP - 1:TP], GM[:, TP - 1:TP], ALU.add, ["Tc", "Te"], ["Mfin"])
                for j in range(NCH):
                    TR(PB[4][:, 0:4], GE1[:, j * 128:(j + 1) * 128], ident32[0:4, 0:4], ["Tf", "ident32"], [PBn[4]])
                    TR(PB[4][:, 4:8], GE2[:, j * 128:(j + 1) * 128], ident32[0:4, 0:4], ["Tg", "ident32"], [PBn[4]])
                    CP(act, ET[:, j, :], PB[4][:, 0:8], [PBn[4]], ["ET"])
                GT2 = sb_gt2
                CP(dve, GT2[:], GT[:], ["Tz"], ["GT2"])
                for h in range(4):
                    MM(PB[2][:, h * NCH:(h + 1) * NCH], sel[:, h, :], GDEC[:], True, True, ["sel", "GDEC"], [PBn[2]])
                CP(act, DECB[:].rearrange("p h j -> p (h j)"), PB[2][:, 0:4 * NCH], [PBn[2]], ["DECB"])
                if first:
                    dma(sp, SMS[:], st_m[l].rearrange("s h -> h s"), [], ["SMS"], osem("sms_in"), slow=True)
                    dma(sp, NS32[:], st_n[l], [], ["Ssmp"], osem("ns_in"))
                    TT(dve, SLI[:], SMS[:], GLF[:, TP:NW], ALU.add, ["SMS", "Tb"], ["SLI"])
                    TT(dve, SMm[:], SLI[:], GI[:, TP:NW], ALU.max, ["SLI", "Ta"], ["SMm"])
                    TT(dve, SW[:, 0, :], GI[:, TP:NW], SMm[:], ALU.subtract, ["Ta", "SMm"], ["SW"])
                    TT(dve, SW[:, 1, :], SLI[:], SMm[:], ALU.subtract, ["SLI", "SMm"], ["SW"])
                    TS(dve, SW[:, 2, :], SMm[:], -1.0, None, ALU.mult, None, ["SMm"], ["SW"])
                    ACT(SW[:], SW[:], AF.Exp, ["SW"], ["SW"])
                    TS(dve, SW[:, 0, :], SW[:, 0, :], 1.0 / 16.0, None, ALU.mult, None, ["SW"], ["SW"])
                    dma(sp, o_ms[l].rearrange("s h -> h s"), SMm[:], ["SMm"], [], osem("smm"), slow=True)
                    for h in range(4):
                        MM(PB[3][:, h * 48:(h + 1) * 48], sel[:, h, :], SW[:].rearrange("p a s -> p (a s)"), True, True, ["sel", "SW"], [PBn[3]])
                    CP(act, SWB[:].rearrange("p h a s -> p (h a s)"), PB[3][:, 0:192], [PBn[3]], ["SWB"])
                    for c in range(8):
                        TR(PB[4][:, 128 + c * NS:128 + (c + 1) * NS], NS32[0:NS, c * 128:(c + 1) * 128], ident32[0:NS, 0:NS], ["Ssmp", "ident32"], [PBn[4]])
                    CP(act, NT32[:].rearrange("p c s -> p (c s)"), PB[4][:, 128:128 + 8 * NS], [PBn[4]], ["NT32"])
                for h in range(4):
                    wt, wn = load_w([w_in[l][:, 8 * BW + h * 256: 8 * BW + (h + 1) * 256]])
                    for kc in range(2):
                        for (blk, bt) in blocks:
                            lc0, n = blk
                            b = nb(); ps = proj_fm(b, wt, wn, kc * 128, 128, blk)
                            CP(act, Q32[:, kc, lc0:lc0 + n], ps, [PBn[b]], ["Q32"])
                        CP(act, Q16[:, kc, :], Q32[:, kc, 0:TP], ["Q32"], ["Q16"])
                    wt, wn = load_w([w_in[l][:, 9 * BW + h * 256: 9 * BW + (h + 1) * 256]])
                    for kc in range(2):
                        for (blk, bt) in blocks:
                            lc0, n = blk
                            b = nb(); ps = proj_fm(b, wt, wn, kc * 128, 128, blk)
                            if bt == "P":
                                CP(act, K16[:, kc, :], ps, [PBn[b]], ["K16"])
                            else:
                                TT(dve, KP32[:, kc, :], ps, SWB[:, h, 0, :], ALU.mult, [PBn[b], "SWB"], ["KP32"])
                    wt, wn = load_w([w_in[l][:, 10 * BW + h * 256: 10 * BW + (h + 1) * 256]])
                    v_tokmajor(wt, wn, 0, 256, blocks)
                    for sgi, dst, fn_, dn in ((11, Tz, AF.Silu, "Tz"), (12, To, AF.Sigmoid, "To")):
                        wt, wn = load_w([w_in[l][:, sgi * BW + h * 256: sgi * BW + (h + 1) * 256]])
                        for vc in range(2):
                            for (blk, bt) in blocks:
                                lc0, n = blk
                                b = nb(); ps = proj_fm(b, wt, wn, vc * 128, 128, blk)
                                ACT(dst[:, vc, lc0:lc0 + n], ps, fn_, [PBn[b]], [dn])
                    kd_transposes(2, [(K16[:, 0, :], "K16"), (K16[:, 1, :], "K16")], lambda i: ET[:, i, 4 + h:5 + h])

                    def mmask(j, rows, am, amn, c):
                        STT(am[rows, 0:c], PB[4][rows, 0:c], ET[:, j, h:h + 1], tri128[:], ALU.mult, ALU.mult, [PBn[4], "ET", "tri128"], [amn])
                    chunk_scan(128, 2, 2, S_c, "S_c", h, mmask, lambda j: DECB[:, h, j:j + 1], True)
                    MM(PB[3][:, 0:TP], sel[:, h, :], GT2[:], True, True, ["sel", "GT2"], [PBn[3]])
                    CP(act, THRB[:], PB[3][:, 0:TP], [PBn[3]], ["THRB"])
                    ACT(Te[:, 0:TP], PB[2][:, 0:TP], AF.Abs, [PBn[2]], ["Te"])
                    TT(dve, Te[:, 0:TP], Te[:, 0:TP], THRB[:], ALU.max, ["Te", "THRB"], ["Te"])
                    RECIP(Te[:, 0:TP], Te[:, 0:TP], ["Te"], ["Te"])
                    for vc in range(2):
                        TT(dve, H32[:, vc, 0:TP], PB[vc][:, 0:TP], Te[:, 0:TP], ALU.mult, [PBn[vc], "Te"], ["H32"])
                    if first:
                        for kc in range(2):
                            c8 = h * 2 + kc
                            TT(dve, NT32[:, c8, :], NT32[:, c8, :], SWB[:, h, 1, :], ALU.mult, ["NT32", "SWB"], ["NT32"])
                            TT(dve, NT32[:, c8, :], NT32[:, c8, :], KP32[:, kc, :], ALU.add, ["NT32", "KP32"], ["NT32"])
                            TT(dve, Ta[:, kc * NS:(kc + 1) * NS], Q32[:, kc, TP:NW], NT32[:, c8, :], ALU.mult, ["Q32", "NT32"], ["Ta"])
                        MM(PB[7][:, 32:48], ones32[:], Ta[:, 0:NS], True, False, ["ones32", "Ta"], [PBn[7]])
                        MM(PB[7][:, 32:48], ones32[:], Ta[:, NS:2 * NS], False, True, ["ones32", "Ta"], [PBn[7]])
                        sample_update(l, 256, 2, h, st_c, o_cs, lambda kc: SWB[:, h, 1, :], lambda kc: KP32[:, kc, :], 2)
                        ACT(Tc[:, 0:NS], PB[7][:, 32:48], AF.Abs, [PBn[7]], ["Tc"])
                        TT(dve, Tc[:, 0:NS], Tc[:, 0:NS], SWB[:, h, 2, :], ALU.max, ["Tc", "SWB"], ["Tc"])
                        RECIP(Tc[:, 0:NS], Tc[:, 0:NS], ["Tc"], ["Tc"])
                        for vc in range(2):
                            TT(dve, H32[:, vc, TP:NW], PB[7][:, vc * 16:vc * 16 + NS], Tc[:, 0:NS], ALU.mult, [PBn[7], "Tc"], ["H32"])
                    for vc in range(2):
                        TT(dve, H32[:, vc, 0:n_all], H32[:, vc, 0:n_all], To[:, vc, 0:n_all], ALU.mult, ["H32", "To"], ["H32"])
                    head_norm("ln", 2, 24 + 2 * h, l, 16 + 2 * h, n_all)
                if first:
                    for c in range(8):
                        TR(PB[5 + c // 4][0:NS, (c % 4) * 128:(c % 4 + 1) * 128], NT32[:, c, :], ident32[:], ["NT32", "ident32"], [PBn[5 + c // 4]])
                    CP(act, NO32[:, 0:512], PB[5][0:NS, :], [PBn[5]], ["Ssmp"])
                    CP(act, NO32[:, 512:1024], PB[6][0:NS, :], [PBn[6]], ["Ssmp"])
                    dma(sp, o_ns[l], NO32[:], ["Ssmp"], [], osem("no32"))
                if last:
                    dma(sp, o_cp[l].rearrange("h (kc k) v -> k (h kc) v", k=128), S_c[:, :, 0:256], ["S_c"], [], osem("S_c"))
                    dma(sp, o_np[l].rearrange("h (kc k) -> k (h kc)", k=128), S_c[:, :, 256], ["S_c"], [], osem("S_c"), slow=True)
                    dma(sp, o_mp[l].rearrange("(h o) -> h o", o=1), Mfin[:], ["Mfin"], [], osem("mfin"), slow=True)

                fw.tag = "merge"
                pieces = [(0, 512)] + ([(512, NS)] if first else [])
                S16f = Sflat.bitcast(BF16)
                WGu = [S16f[:, 0:2048], S16f[:, 3072:5120], BDs[:, 0:2048]]
                WBu = [S16f[:, 2048:3072], S16f[:, 5120:6144], BDs[:, 2048:3072]]
                WGu = [a.rearrange("p (k d) -> p k d", d=128) for a in WGu]
                WBu = [a.rearrange("p (k d) -> p k d", d=128) for a in WBu]
                units = [(dc, nbr) for dc in range(16) for nbr in range(3)]

                def load_unit(i):
                    dc, nbr = units[i]
                    u = i % 3
                    extra = (["Ssmp"] if u < 2 else ["BD"]) if i < 3 else []
                    d0 = dc * 128
                    dma(pool, WGu[u], w_in[l][:, GATE0 + nbr * D + d0: GATE0 + nbr * D + d0 + 128].rearrange("(k p) c -> p k c", p=128), [], [f"MU{u}"] + extra, s_mu[u])
                    dma(pool, WBu[u], w_br[l, nbr][:, d0:d0 + 128].rearrange("(k p) c -> p k c", p=128), [], [f"MU{u}"] + extra, s_mu[u])
                load_unit(0); load_unit(1)
                for i, (dc, nbr) in enumerate(units):
                    u = i % 3
                    if i + 2 < len(units):
                        load_unit(i + 2)
                    for (c0, nn) in pieces:
                        b = nb()
                        for kc in range(16):
                            MM(PB[b][:, 0:nn], WGu[u][:, kc, :], xT16[:, kc, c0:c0 + nn], kc == 0, kc == 15, [f"MU{u}", "xT16"], [PBn[b]])
                        ACT(MS32[:, c0:c0 + nn], PB[b][:, 0:nn], AF.Sigmoid, [PBn[b]], ["Ta"])
                        b = nb()
                        for cc in range(8):
                            MM(PB[b][:, 0:nn], WBu[u][:, cc, :], yT[:, nbr * 8 + cc, c0:c0 + nn], cc == 0, cc == 7, [f"MU{u}", "U"], [PBn[b]])
                        if nbr == 0:
                            TT(dve, MA32[:, c0:c0 + nn], MS32[:, c0:c0 + nn], PB[b][:, 0:nn], ALU.mult, ["Ta", PBn[b]], ["Tb"])
                        else:
                            TT(dve, MB32[:, c0:c0 + nn], MS32[:, c0:c0 + nn], PB[b][:, 0:nn], ALU.mult, ["Ta", PBn[b]], ["Tc"])
                            if nbr == 1:
                                TT(dve, MA32[:, c0:c0 + nn], MA32[:, c0:c0 + nn], MB32[:, c0:c0 + nn], ALU.add, ["Tb", "Tc"], ["Tb"])
                            else:
                                TT(dve, mg16[:, dc, c0:c0 + nn], MA32[:, c0:c0 + nn], MB32[:, c0:c0 + nn], ALU.add, ["Tb", "Tc"], MGALL)
                for u, base in ((0, "Ssmp"), (1, "Ssmp"), (2, "BD")):
                    Rb = fw.R(base); Rm = fw.R(f"MU{u}")
                    for s_, v_ in Rm.r.items():
                        Rb.r[s_] = max(Rb.r.get(s_, 0), v_)
                    if Rm.w is not None:
                        Rb.r[Rm.w[0]] = max(Rb.r.get(Rm.w[0], 0), Rm.w[1])
                fw.tag = "out"
                for dc in range(16):
                    d0 = dc * 128
                    wt, wn = load_w([w_o[l][:, d0:d0 + 128]])
                    xi = dc % 2
                    dma(sp, XR[xi][:, 0:TP], XF[par, dc, :, p * TP:(p + 1) * TP], [xfres(par, dc)], [f"XR{xi}"], s_xr[xi])
                    if first:
                        dma(sp, XR[xi][:, TP:NW], XF[par, dc, :, NTOK:NT], [xfres(par, dc)], [f"XR{xi}"], s_xr[xi])
                    for (c0, nn) in pieces:
                        b = nb()
                        for mc in range(16):
                            MM(PB[b][:, 0:nn], wt[:, mc, 0:128], mg16[:, mc, c0:c0 + nn], mc == 0, mc == 15, [wn] + MGALL, [PBn[b]])
                        STT(Hbuf[:, dc, c0:c0 + nn], XR[xi][:, c0:c0 + nn], ALPHA, PB[b][:, 0:nn], ALU.mult, ALU.add, [f"XR{xi}", PBn[b]], ["U"])
                    ACT(SQH[:, 0:n_all], Hbuf[:, dc, 0:n_all], AF.Square, ["U"], ["SQH"])
                    MM(PB[2][:, 0:512], ones32[:], Hbuf[:, dc, 0:512], dc == 0, dc == 15, ["ones32", "U"], [PBn[2]])
                    MM(PB[3][:, 0:512], ones32[:], SQH[:, 0:512], dc == 0, dc == 15, ["ones32", "SQH"], [PBn[3]])
                    if first:
                        MM(PB[6][:, 0:NS], ones32[:], Hbuf[:, dc, TP:NW], dc == 0, dc == 15, ["ones32", "U"], [PBn[6]])
                        MM(PB[7][:, 0:NS], ones32[:], SQH[:, TP:NW], dc == 0, dc == 15, ["ones32", "SQH"], [PBn[7]])
                CP(act, ST1[:, 0:512], PB[2][:, 0:512], [PBn[2]], ["ST1"])
                CP(act, ST2[:, 0:512], PB[3][:, 0:512], [PBn[3]], ["ST2"])
                if first:
                    CP(act, ST1[:, TP:NW], PB[6][:, 0:NS], [PBn[6]], ["ST1"])
                    CP(act, ST2[:, TP:NW], PB[7][:, 0:NS], [PBn[7]], ["ST2"])
                n = n_all
                TS(dve, ST1[:, 0:n], ST1[:, 0:n], 1.0 / D, None, ALU.mult, None, ["ST1"], ["ST1"])
                TT(dve, SQH[:, 0:n], ST1[:, 0:n], ST1[:, 0:n], ALU.mult, ["ST1"], ["SQH"])
                STT(ST2[:, 0:n], ST2[:, 0:n], 1.0 / D, SQH[:, 0:n], ALU.mult, ALU.subtract, ["ST2", "SQH"], ["ST2"])
                ACT(ST2[:, 0:n], ST2[:, 0:n], AF.Ln, ["ST2", "epsl"], ["ST2"], scale=1.0, bias=epsl[:])
                ACT(ST2[:, 0:n], ST2[:, 0:n], AF.Exp, ["ST2"], ["ST2"], scale=-0.5)
                for dc in range(16):
                    xi = dc % 2
                    TT(dve, MA32[:, 0:n], Hbuf[:, dc, 0:n], ST1[:, 0:n], ALU.subtract, ["U", "ST1"], ["Tb"])
                    TT(dve, MB32[:, 0:n], MA32[:, 0:n], ST2[:, 0:n], ALU.mult, ["Tb", "ST2"], ["Tc"])
                    TS(dve, XR[xi][:, 0:n], MB32[:, 0:n], PAR[:, 32 + dc, l:l + 1], PAR[:, 48 + dc, l:l + 1], ALU.mult, ALU.add, ["Tc", "PAR"], [f"XR{xi}"])
                    dma(sp, XF[1 - par, dc, :, p * TP:(p + 1) * TP], XR[xi][:, 0:TP], [f"XR{xi}"], [xfres(1 - par, dc)], s_xw[xi])
                    if first:
                        dma(sp, XF[1 - par, dc, :, NTOK:NT], XR[xi][:, TP:NW], [f"XR{xi}"], [xfres(1 - par, dc)], s_xw[xi])

        fw.tag = "final"
        fp_ = L % 2
        otiles = [(yp[i * 128:(i + 1) * 128, :], 128, i * 128) for i in range(NTOK // 128)] + [(ys, NS, NTOK)]
        for (dst, nr, c0) in otiles:
            dma(sp, XS[:, :, 0:nr], XF[fp_, :, :, c0:c0 + nr].rearrange("k p t -> p k t"), [xfres(fp_, k) for k in range(16)], ["U"], osem("xs_in"))
            for g in range(4):
                for j in range(4):
                    kc = g * 4 + j
                    TR(PB[g % 2][0:nr, j * 128:(j + 1) * 128], XS[:, kc, 0:nr], ident32[:], ["U", "ident32"], [PBn[g % 2]])
                CP(act if g % 2 else dve, XIN[0:nr, g * 512:(g + 1) * 512], PB[g % 2][0:nr, :], [PBn[g % 2]], ["U"])
            dma(sp, dst, XIN[0:nr, :], ["U"], [], osem("xin_out"))
        for s in osems.values():
            nc.sync.wait_ge(s.h, s.v)
        for s in s_xw:
            nc.sync.wait_ge(s.h, s.v)
        print("ops", fw.nops)
        nc._petags = fw.petags
    return nc


def make_consts(NP, seq_b_pos0=0):
    NTOK = NP * TP
    NT = NTOK + NS
    c = {}
    c["c_id"] = np.eye(128, dtype=np.float32)
    perm = np.zeros((128, 128), np.float32)
    for k in range(128):
        perm[k, k ^ 1] = 1.0
    c["c_perm"] = perm
    s = np.arange(128)
    c["c_tri64"] = ((s[:, None] % 64) <= np.arange(64)[None, :]).astype(np.float32)
    tri = (s[:, None] <= s[None, :]).astype(np.float32)
    c["c_tri128"] = (tri / 16.0).astype(np.float32)
    gam = np.array([1.0 - 2.0 ** (-5.0 - h) for h in range(4)], np.float64)
    rmask = np.zeros((128, 4, 128), np.float64); rG = np.zeros((128, 4, 128), np.float64); rkd = np.zeros((128, 4), np.float64)
    for h in range(4):
        rmask[:, h, :] = tri * (gam[h] ** (-(s[:, None] + 1.0))) / 16.0
        rG[:, h, :] = (gam[h] ** (s[None, :] + 1.0))
        rkd[:, h] = gam[h] ** (127.0 - s) / 16.0
    c["c_rmask"] = rmask.astype(np.float32); c["c_rG"] = rG.astype(np.float32); c["c_rkd"] = rkd.astype(np.float32)
    rs = np.ones((128, TP), np.float32); rs[:, 0::64] = 0.0
    c["c_reset"] = rs
    pos = np.concatenate([np.arange(NTOK, dtype=np.float32), np.full((NS,), float(PAST), np.float32)])
    theta = (1.0 / (10000.0 ** np.linspace(0.0, 1.0, 128, dtype=np.float32))).astype(np.float32)
    ang = (pos[:, None] * theta[None, :]).astype(np.float32)
    cosv = np.cos(ang).astype(np.float32); sinv = np.sin(ang).astype(np.float32)
    cos_t = np.zeros((128, 2, NT), np.float32); sin_t = np.zeros((128, 2, NT), np.float32)
    for kc in range(2):
        for f in range(128):
            jf = (kc * 128 + f) // 2
            cos_t[f, kc, :] = cosv[:, jf]
            sin_t[f, kc, :] = sinv[:, jf] * (-1.0 if f % 2 == 0 else 1.0)
    c["c_cos"] = cos_t; c["c_sin"] = sin_t
    sel = np.zeros((4, 4, 128), np.float32)
    for h in range(4):
        sel[h, h, :] = 1.0
    c["c_sel"] = sel
    c["c_i16"] = np.eye(16, dtype=np.float32)
    return c


_cache = {}


def prep(L, NP, NB, ncores, x_prompt, x_sample, state_hgrn, state_ret, state_mlstm_c, state_mlstm_n, state_mlstm_m,
         w_in, hgrn_lb_logits, hgrn_norm, ret_norm, mlstm_norm, mlstm_b_i, mlstm_b_f, w_branch, w_out, ln_g, ln_b):
    f = lambda a: np.ascontiguousarray(np.asarray(a, dtype=np.float32))
    consts = make_consts(NP)
    pvec = f(np.concatenate([hgrn_lb_logits, hgrn_norm, ret_norm, mlstm_norm, ln_g, ln_b], axis=1))
    bif = f(np.stack([np.asarray(mlstm_b_i).T, np.asarray(mlstm_b_f).T], axis=1))
    w_in = f(w_in); w_branch = f(w_branch); w_out = f(w_out)
    in_maps = []
    nsb = np.asarray(x_sample).shape[0]
    for c in range(ncores):
        b = c % NB
        s0 = (c * NS) % nsb
        sl = slice(s0, s0 + NS)
        m = dict(consts)
        m.update({
            "xp": f(x_prompt[b]), "xs": f(np.asarray(x_sample)[sl, 0, :]),
            "st_h": f(np.asarray(state_hgrn)[:, sl]), "st_r": f(np.asarray(state_ret)[:, sl]), "st_c": f(np.asarray(state_mlstm_c)[:, sl]),
            "st_n": f(np.asarray(state_mlstm_n)[:, sl].reshape(L, NS, 1024)), "st_m": f(np.asarray(state_mlstm_m)[:, sl]),
            "w_in": w_in, "w_br": w_branch, "w_o": w_out, "pvec": pvec, "bif": bif,
        })
        in_maps.append(m)
    return in_maps


def gather(R, L, NB, nsb):
    nsc = nsb // NS
    y_prompt = np.stack([R[b]["yp"] for b in range(NB)], axis=0)
    y_sample = np.concatenate([R[c]["ys"] for c in range(nsc)], axis=0)[:, None, :]
    hg_p = np.stack([R[b]["o_hp"] for b in range(NB)], axis=1)
    ret_p = np.stack([R[b]["o_rp"] for b in range(NB)], axis=1)
    mc_p = np.stack([R[b]["o_cp"] for b in range(NB)], axis=1)
    mn_p = np.stack([R[b]["o_np"] for b in range(NB)], axis=1)
    mm_p = np.stack([R[b]["o_mp"] for b in range(NB)], axis=1)
    hg_s = np.concatenate([R[c]["o_hs"] for c in range(nsc)], axis=1)
    ret_s = np.concatenate([R[c]["o_rs"] for c in range(nsc)], axis=1)
    mc_s = np.concatenate([R[c]["o_cs"] for c in range(nsc)], axis=1)
    mn_s = np.concatenate([R[c]["o_ns"] for c in range(nsc)], axis=1).reshape(L, nsb, 4, 256)
    mm_s = np.concatenate([R[c]["o_ms"] for c in range(nsc)], axis=1)
    return (y_prompt, y_sample, hg_p, ret_p, mc_p, mn_p, mm_p, hg_s, ret_s, mc_s, mn_s, mm_s)


def kernel(**inputs):
    L, NP, NB = 4, 4, 4
    key = (L, NP)
    if key not in _cache:
        _cache[key] = build(L, NP)
    in_maps = prep(L, NP, NB, 8, **inputs)
    res = run_bass_kernel_spmd(_cache[key], in_maps, core_ids=list(range(8)))
    return gather(res.results, L, NB, 128)
```

```python
import math
import numpy as np
from contextlib import ExitStack
import concourse.bass as bass
import concourse.mybir as mybir
from concourse.bass_utils import run_bass_kernel_spmd

F32 = mybir.dt.float32
BF16 = mybir.dt.bfloat16
ALU = mybir.AluOpType
AF = mybir.ActivationFunctionType
AX = mybir.AxisListType

D = 2048
DIN = 19464
BW = 1024
TP = 512
NS = 16
NW = TP + NS
PAST = 16384
GATE0 = 13320


class Sem:
    def __init__(self, h):
        self.h = h
        self.v = 0


class Res:
    __slots__ = ("w", "r")

    def __init__(self):
        self.w = None
        self.r = {}


class Eng:
    def __init__(self, name, eng, sem, inorder=False):
        self.name = name
        self.eng = eng
        self.sem = sem
        self.waited = {}
        self.inorder = inorder


class FW:
    def __init__(self, nc, ctx):
        self.nc = nc
        self.ctx = ctx
        self.allsems = []
        self.pe = Eng("pe", nc.tensor, self.sem("pe"), inorder=True)
        self.act = Eng("act", nc.scalar, self.sem("act"))
        self.dve = Eng("dve", nc.vector, self.sem("dve"))
        self.pool = Eng("pool", nc.gpsimd, self.sem("pool"))
        self.sp = Eng("sp", nc.sync, self.sem("sp"))
        self.res = {}
        self.nops = 0
        self.tag = ""
        self.petags = []

    def sem(self, name):
        s = Sem(self.ctx.enter_context(self.nc.semaphore(name)))
        self.allsems.append(s)
        return s

    def sb(self, name, shape, dt):
        t = self.ctx.enter_context(self.nc.sbuf_tensor(name, list(shape), dt))
        self.res[name] = Res()
        return t

    def R(self, name):
        if name not in self.res:
            self.res[name] = Res()
        return self.res[name]

    def op(self, E, fn, reads=(), writes=(), sem=None):
        reads = [self.R(r) if isinstance(r, str) else r for r in reads]
        writes = [self.R(w) if isinstance(w, str) else w for w in writes]
        deps = []
        for r in reads:
            if r.w is not None:
                deps.append(r.w)
        for w in writes:
            if w.w is not None:
                deps.append(w.w)
            deps.extend(w.r.items())
        for (s, v) in deps:
            if s is E.sem and E.inorder:
                continue
            if E.waited.get(s, 0) >= v:
                continue
            E.eng.wait_ge(s.h, v)
            E.waited[s] = v
        ins = fn()
        self.nops += 1
        if E.inorder:
            self.petags.append(self.tag)
        if sem is not None:
            S = sem
            S.v += 16
            ins.then_inc(S.h, 16)
        else:
            S = E.sem
            S.v += 1
            ins.then_inc(S.h, 1)
        for r in reads:
            if r.r.get(S, 0) < S.v:
                r.r[S] = S.v
        for w in writes:
            w.w = (S, S.v)
            w.r = {}
        return ins

    def fence(self, sem, names):
        for n in names:
            self.R(n).w = (sem, sem.v)


def build(L, NP):
    nc = bass.Bass("TRN2", target_bir_lowering=False)
    NTOK = NP * TP
    NT = NTOK + NS
    NCH = TP // 128
    ALPHA = float((2 * L) ** 0.25)

    def din(name, shape):
        return nc.dram_tensor(name, list(shape), F32, kind="ExternalInput").ap()

    def dout(name, shape):
        return nc.dram_tensor(name, list(shape), F32, kind="ExternalOutput").ap()

    xp = din("xp", [NTOK, D]); xs = din("xs", [NS, D])
    st_h = din("st_h", [L, NS, 8, 128, 128]); st_r = din("st_r", [L, NS, 4, 256, 256]); st_c = din("st_c", [L, NS, 4, 256, 256])
    st_n = din("st_n", [L, NS, 1024]); st_m = din("st_m", [L, NS, 4])
    w_in = din("w_in", [L, D, DIN]); w_br = din("w_br", [L, 3, BW, D]); w_o = din("w_o", [L, D, D])
    pvec = din("pvec", [L, 8192]); bif = din("bif", [4, 2, L])
    c_id = din("c_id", [128, 128]); c_perm = din("c_perm", [128, 128]); c_tri64 = din("c_tri64", [128, 64]); c_tri128 = din("c_tri128", [128, 128])
    c_rmask = din("c_rmask", [128, 4, 128]); c_rG = din("c_rG", [128, 4, 128]); c_rkd = din("c_rkd", [128, 4])
    c_reset = din("c_reset", [128, TP]); c_cos = din("c_cos", [128, 2, NT]); c_sin = din("c_sin", [128, 2, NT]); c_sel = din("c_sel", [4, 4, 128])
    c_i16 = din("c_i16", [16, 16])
    yp = dout("yp", [NTOK, D]); ys = dout("ys", [NS, D])
    o_hp = dout("o_hp", [L, 8, 128, 128]); o_rp = dout("o_rp", [L, 4, 256, 256]); o_cp = dout("o_cp", [L, 4, 256, 256])
    o_np = dout("o_np", [L, 4, 256]); o_mp = dout("o_mp", [L, 4])
    o_hs = dout("o_hs", [L, NS, 8, 128, 128]); o_rs = dout("o_rs", [L, NS, 4, 256, 256]); o_cs = dout("o_cs", [L, NS, 4, 256, 256])
    o_ns = dout("o_ns", [L, NS, 1024]); o_ms = dout("o_ms", [L, NS, 4])
    XF = nc.dram_tensor("xf", [2, 16, 128, NT], F32, kind="Internal").ap()
    GAM = [1.0 - 2.0 ** (-5.0 - h) for h in range(4)]

    with ExitStack() as ctx:
        fw = FW(nc, ctx)
        pe, act, dve, pool, sp = fw.pe, fw.act, fw.dve, fw.pool, fw.sp
        sb = fw.sb
        s_misc = fw.sem("misc")
        s_w = [fw.sem(f"w{i}") for i in range(3)]
        s_mu = [fw.sem(f"mu{i}") for i in range(3)]
        s_x = fw.sem("x"); s_xr = [fw.sem(f"xr{i}") for i in range(2)]; s_xw = [fw.sem(f"xw{i}") for i in range(2)]
        s_st = fw.sem("st"); s_tab = fw.sem("tab"); s_wg = fw.sem("wg"); s_wb = fw.sem("wb")
        osems = {}

        def osem(name):
            if name not in osems:
                osems[name] = fw.sem("o_" + name)
            return osems[name]

        pool_hist = []

        def dma(E, out, in_, reads, writes, sem, slow=False):
            q = {"sp": nc.sync, "pool": nc.gpsimd, "act": nc.scalar}[E.name]
            if E is pool:
                if len(pool_hist) >= 4:
                    s_, v_ = pool_hist[-4]
                    if E.waited.get(s_, 0) < v_:
                        E.eng.wait_ge(s_.h, v_)
                        E.waited[s_] = v_
                pool_hist.append((sem, sem.v + 16))
            if slow:
                return fw.op(E, lambda: q.dma_start(out=out, in_=in_, allow_slow_non_contiguous=True), reads=reads, writes=writes, sem=sem)
            return fw.op(E, lambda: q.dma_start(out=out, in_=in_), reads=reads, writes=writes, sem=sem)

        def MM(out, lhsT, rhs, start, stop, reads, writes, skip=False):
            if skip:
                return fw.op(pe, lambda: nc.tensor.matmul(out, lhsT=lhsT, rhs=rhs, start=start, stop=stop, skip_group_check=True), reads=reads, writes=writes)
            return fw.op(pe, lambda: nc.tensor.matmul(out, lhsT=lhsT, rhs=rhs, start=start, stop=stop), reads=reads, writes=writes)

        def TR(out, in_, ident, reads, writes):
            return fw.op(pe, lambda: nc.tensor.transpose(out, in_, ident), reads=reads, writes=writes)

        def ACT(out, in_, func, reads, writes, scale=1.0, bias=None):
            if bias is None:
                return fw.op(act, lambda: nc.scalar.activation(out=out, in_=in_, func=func, scale=scale), reads=reads, writes=writes)
            return fw.op(act, lambda: nc.scalar.activation(out=out, in_=in_, func=func, scale=scale, bias=bias), reads=reads, writes=writes)

        def TT(E, out, in0, in1, op, reads, writes):
            e = nc.vector if E is dve else nc.gpsimd
            return fw.op(E, lambda: e.tensor_tensor(out=out, in0=in0, in1=in1, op=op), reads=reads, writes=writes)

        def TS(E, out, in0, s1, s2, op0, op1, reads, writes):
            e = nc.vector if E is dve else nc.gpsimd
            if op1 is None:
                return fw.op(E, lambda: e.tensor_scalar(out=out, in0=in0, scalar1=s1, scalar2=None, op0=op0), reads=reads, writes=writes)
            return fw.op(E, lambda: e.tensor_scalar(out=out, in0=in0, scalar1=s1, scalar2=s2, op0=op0, op1=op1), reads=reads, writes=writes)

        def STT(out, in0, scalar, in1, op0, op1, reads, writes):
            return fw.op(dve, lambda: nc.vector.scalar_tensor_tensor(out=out, in0=in0, scalar=scalar, in1=in1, op0=op0, op1=op1), reads=reads, writes=writes)

        def CP(E, out, in_, reads, writes):
            if E is act:
                return fw.op(act, lambda: nc.scalar.copy(out=out, in_=in_), reads=reads, writes=writes)
            e = nc.vector if E is dve else nc.gpsimd
            return fw.op(E, lambda: e.tensor_copy(out=out, in_=in_), reads=reads, writes=writes)

        def MSET(E, ap, val, writes):
            e = nc.vector if E is dve else nc.gpsimd
            return fw.op(E, lambda: e.memset(ap, val), writes=writes)

        def SCAN(out, d0, d1, init, op0, op1, reads, writes):
            return fw.op(dve, lambda: nc.vector.tensor_tensor_scan(out=out, data0=d0, data1=d1, initial=init, op0=op0, op1=op1), reads=reads, writes=writes)

        def RECIP(out, in_, reads, writes):
            return fw.op(dve, lambda: nc.vector.reciprocal(out=out, in_=in_), reads=reads, writes=writes)

        ident32 = sb("ident32", [128, 128], F32); ident16 = sb("ident16", [128, 128], BF16); perm32 = sb("perm32", [128, 128], F32)
        tri64 = sb("tri64", [128, 64], F32); tri128 = sb("tri128", [128, 128], F32)
        rmask = sb("rmask", [128, 4, 128], F32); rG = sb("rG", [128, 4, 128], F32); rkd = sb("rkd", [128, 4], F32)
        reset = sb("reset", [128, TP], BF16); sel = sb("sel", [4, 4, 128], F32); i16 = sb("i16", [16, 16], F32)
        ones16 = sb("ones16", [128, 128], BF16); ones32 = sb("ones32", [128, 128], F32); onesrow = sb("onesrow", [4, TP], BF16)
        epsn = sb("epsn", [128, 1], F32); epsl = sb("epsl", [128, 1], F32)
        BIF = sb("BIF", [4, 2, L], F32)
        PT32 = sb("PT32", [L, 1024], F32); PAR = sb("PAR", [128, 64, L], F32)
        lbv = sb("lbv", [128, 8, L], F32); oml = sb("oml", [128, 8, L], F32); lbe = sb("lbe", [128, 8, L], F32); lbs = sb("lbs", [128, 8], F32)
        cl = [(ident32, c_id), (perm32, c_perm), (tri64, c_tri64), (tri128, c_tri128), (rmask, c_rmask), (rG, c_rG), (rkd, c_rkd),
              (sel, c_sel), (i16, c_i16), (BIF, bif)]
        for t, src in cl:
            dma(sp, t[:], src, [], [t.name], s_misc)
        dma(pool, ident16[:], c_id, [], ["ident16"], s_misc)
        dma(pool, reset[:], c_reset, [], ["reset"], s_misc)
        fw.fence(s_misc, [t.name for t, _ in cl] + ["ident16", "reset"])
        MSET(dve, ones16[:], 1.0, ["ones16"]); MSET(dve, ones32[:], 1.0, ["ones32"]); MSET(dve, onesrow[:], 1.0, ["onesrow"])
        MSET(dve, epsn[:], 1e-6, ["epsn"]); MSET(dve, epsl[:], 1e-5, ["epsl"])

        PB = [ctx.enter_context(nc.psum_tensor(f"pb{i}", [128, 512], F32)) for i in range(8)]
        PBn = [f"pb{i}" for i in range(8)]

        for pc in range(8):
            dma(sp, PT32[:], pvec[:, pc * 1024:(pc + 1) * 1024], [], ["PT32"], osem("pt32"))
            for c8 in range(8):
                c = pc * 8 + c8
                TR(PB[0][:, c * L:(c + 1) * L], PT32[0:L, c8 * 128:(c8 + 1) * 128], ident32[0:L, 0:L], ["PT32", "ident32"], [PBn[0]])
        CP(dve, PAR[:].rearrange("p c l -> p (c l)"), PB[0][:, 0:64 * L], [PBn[0]], ["PAR"])
        ACT(lbe[:], PAR[:, 0:8, :], AF.Exp, ["PAR"], ["lbe"])
        fw.op(dve, lambda: nc.vector.tensor_reduce(out=lbs[:], in_=lbe[:], axis=AX.X, op=ALU.add), reads=["lbe"], writes=["lbs"])
        RECIP(lbs[:], lbs[:], ["lbs"], ["lbs"])
        TT(dve, lbe[:], lbe[:], lbs[:].unsqueeze(2).to_broadcast([128, 8, L]), ALU.mult, ["lbe", "lbs"], ["lbe"])
        MSET(dve, lbv[:], 0.0, ["lbv"])
        for l in range(1, L):
            TT(dve, lbv[:, :, l:l + 1], lbv[:, :, l - 1:l], lbe[:, :, l:l + 1], ALU.add, ["lbv", "lbe"], ["lbv"])
        TS(dve, oml[:], lbv[:], -1.0, 1.0, ALU.mult, ALU.add, ["lbv"], ["oml"])

        xT16 = sb("xT16", [128, 16, NW], BF16)
        U = sb("U", [128, 16 * NW], F32)
        yT = U[:].bitcast(BF16)[:, 0:24 * NW].rearrange("p (c t) -> p c t", t=NW)
        Hbuf = U[:].rearrange("p (c t) -> p c t", t=NW)
        S_h = sb("S_h", [128, 8, 128], F32); S_r = sb("S_r", [128, 8, 256], F32); S_c = sb("S_c", [128, 8, 257], F32)
        cosT = sb("cosT", [128, 2, NW], F32); sinT = sb("sinT", [128, 2, NW], F32)
        WT = [sb(f"WT{i}", [128, 16, 256], BF16) for i in range(3)]
        wcnt = [0]
        Ta = sb("Ta", [128, NW], F32); Tb = sb("Tb", [128, NW], F32); Tc = sb("Tc", [128, NW], F32); Td = sb("Td", [128, NW], F32)
        Te = sb("Te", [128, NW], F32); Tf = sb("Tf", [128, NW], F32); Tg = sb("Tg", [128, NW], F32)
        PH1 = sb("PH1", [128, 8 * NW], F32)
        Tz = PH1[:, 0:2 * NW].rearrange("p (c t) -> p c t", t=NW); To = PH1[:, 2 * NW:4 * NW].rearrange("p (c t) -> p c t", t=NW)
        H32 = PH1[:, 4 * NW:6 * NW].rearrange("p (c t) -> p c t", t=NW); SQ32 = PH1[:, 6 * NW:8 * NW].rearrange("p (c t) -> p c t", t=NW)
        mg16 = PH1[:].bitcast(BF16).rearrange("p (c t) -> p c t", t=NW)
        MGALL = ["mg16", "Tz", "To", "H32", "SQ32"]
        Q16 = sb("Q16", [128, 2, TP], BF16); K16 = sb("K16", [128, 2, TP], BF16); Q32 = sb("Q32", [128, 2, NW], F32); KP32 = sb("KP32", [128, 2, NS], F32)
        A32 = sb("A32", [128, NS], F32)
        KD16 = sb("KD16", [128, TP], BF16); KDT16 = sb("KDT16", [128, 4, 256], BF16); V16 = sb("V16", [128, 4, 257], BF16)
        AM16 = [sb(f"AM16_{i}", [128, 128], BF16) for i in range(2)]
        NREP = sb("NREP", [128, 2, 128], F32)
        Vs16 = sb("Vs16", [16, 256], BF16); BDs = sb("BD", [128, 16 * 256], BF16)
        BD = BDs[0:16, :].rearrange("p (s v) -> p s v", v=256)
        WBt = BDs[:, 0:8 * 3 * 128].rearrange("p (c n d) -> p c n d", n=3, d=128)
        Ssmp = sb("Ssmp", [128, 16, 256], F32); T1 = sb("T1", [128, 512], F32)
        Sflat = Ssmp[:].rearrange("p s v -> p (s v)")
        NS32 = Sflat[0:16, 0:1024]; NO32 = Sflat[0:16, 1024:2048]
        WG = Sflat.bitcast(BF16)[:, 0:16 * 3 * 128].rearrange("p (k n d) -> p k n d", n=3, d=128)
        GI = Ta[0:4, :]; GLF = Tb[0:4, :]; GB = Tc[0:4, 0:TP]; GU = Td[0:4, 0:TP]; GM = Te[0:4, 0:TP]
        GE1 = Tf[0:4, 0:TP]; GE2 = Tg[0:4, 0:TP]; GT = PH1[0:4, 0:TP]; GMS = sb("GMS", [4, NCH], F32); GDEC = sb("GDEC", [4, NCH], F32)
        sb_gt2 = sb("GT2", [4, TP], F32)
        Bc = sb("Bc", [4, 1], F32); Mc = sb("Mc", [4, 1], F32); Mfin = sb("Mfin", [4, 1], F32)
        ET = sb("ET", [128, NCH, 8], F32); THRB = sb("THRB", [128, TP], F32); DECB = sb("DECB", [128, 4, NCH], F32)
        SMS = sb("SMS", [4, NS], F32); SMm = sb("SMm", [4, NS], F32); SLI = sb("SLI", [4, NS], F32); SW = sb("SW", [4, 3, NS], F32); SWB = sb("SWB", [128, 4, 3, NS], F32)
        NT32 = sb("NT32", [128, 8, NS], F32)
        XIN = U[:, 0:D]; XS = U[:, D:2 * D].rearrange("p (k t) -> p k t", t=128)
        XR = [sb(f"XR{i}", [128, NW], F32) for i in range(2)]
        MS32 = Ta; MA32 = Tb; MB32 = Tc
        ST1 = sb("ST1", [128, NW], F32); ST2 = sb("ST2", [128, NW], F32); SQH = sb("SQH", [128, NW], F32)
        MSET(dve, V16[:, :, 256:257], 1.0, ["V16"])

        def xfres(par, kc):
            return f"xf{par}_{kc}"

        tiles = [(xp[i * 128:(i + 1) * 128, :], 128, i * 128) for i in range(NTOK // 128)] + [(xs, NS, NTOK)]
        for (src, nr, c0) in tiles:
            dma(sp, XIN[0:nr, :], src, [], ["U"], osem("xin"))
            for g in range(4):
                for j in range(4):
                    kc = g * 4 + j
                    TR(PB[g % 2][:, j * 128:j * 128 + nr], XIN[0:nr, kc * 128:(kc + 1) * 128], ident32[0:nr, 0:nr], ["U", "ident32"], [PBn[g % 2]])
                CP(act if g % 2 else dve, XS[:, g * 4:(g + 1) * 4, 0:nr], PB[g % 2][:].rearrange("p (j t) -> p j t", t=128)[:, :, 0:nr], [PBn[g % 2]], ["U"])
            dma(sp, XF[0, :, :, c0:c0 + nr].rearrange("k p t -> p k t"), XS[:, :, 0:nr], ["U"], [xfres(0, k) for k in range(16)], osem("xs"))

        def load_w(segs):
            i = wcnt[0] % 3
            wcnt[0] += 1
            off = 0
            for s in segs:
                ncol = s.shape[1]
                dma(pool, WT[i][:, :, off:off + ncol], s.rearrange("(k p) c -> p k c", p=128), [], [f"WT{i}"], s_w[i])
                off += ncol
            return WT[i], f"WT{i}"

        def proj_fm(bank, wt, wn, c0, ncol, blk):
            fw.tag = fw.tag.split(".")[0] + ".proj"
            lc0, n = blk
            for kc in range(16):
                MM(PB[bank][0:ncol, 0:n], wt[:, kc, c0:c0 + ncol], xT16[:, kc, lc0:lc0 + n], kc == 0, kc == 15, [wn, "xT16"], [PBn[bank]])
            return PB[bank][0:ncol, 0:n]

        def proj_tm(bank, wt, wn, c0, ncol, lc0, nr):
            for kc in range(16):
                MM(PB[bank][0:nr, 0:ncol], xT16[:, kc, lc0:lc0 + nr], wt[:, kc, c0:c0 + ncol], kc == 0, kc == 15, [wn, "xT16"], [PBn[bank]])

        bankctr = [0]

        def nb():
            bankctr[0] ^= 1
            return bankctr[0]

        def chunk_scan(c, nkc, nvc, Sst, sname, h, mask_fn, a_fn, den):
            fw.tag = fw.tag.split(".")[0] + ".scan"
            dv = nvc * 128
            ncol = dv + (1 if den else 0)
            nchunk = TP // c

            def geom(j):
                base = (j * c) % 128
                return (j * c) // 128, slice(base, base + c), slice(j * c, (j + 1) * c)

            def emit_at(j):
                i, rows, cs = geom(j)
                pa = (PB[4] if j % 2 == 0 else PB[3])[:, 0:128]
                pan = PBn[4] if j % 2 == 0 else PBn[3]
                for kc in range(nkc):
                    MM(pa[rows, 0:c], K16[:, kc, cs], Q16[:, kc, cs], kc == 0, kc == nkc - 1, ["K16", "Q16"], [pan])
                mask_fn(j, rows, AM16[j % 2], f"AM16_{j % 2}", c, pa, pan)

            emit_at(0)
            for j in range(nchunk):
                i, rows, cs = geom(j)
                am = AM16[j % 2]; amn = f"AM16_{j % 2}"
                if j + 1 < nchunk:
                    emit_at(j + 1)
                for vc in range(nvc):
                    MM(PB[vc][:, cs], V16[rows, i, vc * 128:(vc + 1) * 128], am[rows, 0:c], True, False, ["V16", amn], [PBn[vc]])
                if den:
                    CP(act, Tg[:, 0:c], am[:, 0:c], [amn], ["Tg"])
                    MM(PB[2][:, cs], ones32[:, :], Tg[:, 0:c], True, False, ["ones32", "Tg"], [PBn[2]])
                    for kc in range(nkc):
                        CP(dve, NREP[:, kc, :], Sst[:, h * nkc + kc, 256:257].to_broadcast([128, 128]), [sname], ["NREP"])
                for kc in range(nkc):
                    MM(PB[5 + kc][:, 0:ncol], KDT16[rows, i, kc * 128:(kc + 1) * 128], V16[rows, i, 0:ncol], True, True, ["KDT16", "V16"], [PBn[5 + kc]])
                for vc in range(nvc):
                    for kc in range(nkc):
                        MM(PB[vc][:, cs], Sst[:, h * nkc + kc, vc * 128:(vc + 1) * 128], Q32[:, kc, cs], False, kc == nkc - 1, [sname, "Q32"], [PBn[vc]])
                if den:
                    for kc in range(nkc):
                        MM(PB[2][:, cs], NREP[:, kc, :], Q32[:, kc, cs], False, kc == nkc - 1, ["NREP", "Q32"], [PBn[2]])
                for kc in range(nkc):
                    STT(Sst[:, h * nkc + kc, 0:ncol], Sst[:, h * nkc + kc, 0:ncol], a_fn(j), PB[5 + kc][:, 0:ncol], ALU.mult, ALU.add,
                        [sname, PBn[5 + kc], "DECB", "Te"], [sname])

        def kd_transposes(nkc, srcs, scale_fn):
            fw.tag = fw.tag.split(".")[0] + ".kdT"
            for i in range(NCH):
                for kc in range(nkc):
                    ap, rn = srcs[kc]
                    pt = PB[4][:].bitcast(BF16)[:, 512 + kc * 128:512 + (kc + 1) * 128]
                    TR(pt, ap[:, i * 128:(i + 1) * 128], ident16[:], [rn, "ident16"], [PBn[4]])
                    sc = scale_fn(i)
                    if sc is None:
                        CP(act, KDT16[:, i, kc * 128:(kc + 1) * 128], pt, [PBn[4]], ["KDT16"])
                    else:
                        fw.op(act, lambda: nc.scalar.activation(out=KDT16[:, i, kc * 128:(kc + 1) * 128], in_=pt, func=AF.Copy, scale=sc),
                              reads=[PBn[4], "ET", "rkd"], writes=["KDT16"])

        def v_tokmajor(wt, wn, c0, dv, blocks):
            fw.tag = fw.tag.split(".")[0] + ".vtok"
            for (lc0, n), bt in blocks:
                if bt == "P":
                    for i in range(NCH):
                        b = nb()
                        proj_tm(b, wt, wn, c0, dv, lc0 + i * 128, 128)
                        CP(act, V16[:, i, 0:dv], PB[b][:, 0:dv], [PBn[b]], ["V16"])
                else:
                    b = nb()
                    proj_tm(b, wt, wn, c0, dv, lc0, NS)
                    CP(act, Vs16[:, 0:dv], PB[b][0:NS, 0:dv], [PBn[b]], ["Vs16"])
                    TT(dve, BD[:, :, 0:dv], Vs16[:, 0:dv].unsqueeze(1).to_broadcast([NS, NS, dv]), i16[:].unsqueeze(2).to_broadcast([NS, NS, dv]),
                       ALU.mult, ["Vs16", "i16"], ["BD"])

        def sample_update(l, dv, nkc, h, st_in, st_out, a_fn, kp_fn, nvc):
            per = 512 // dv
            fw.tag = fw.tag.split(".")[0] + ".sample"
            for kc in range(nkc):
                dma(sp, Ssmp[:, :, 0:dv], st_in[l, :, h, kc * 128:(kc + 1) * 128, :].rearrange("s k v -> k s v"), [], ["Ssmp"], s_st)
                a_ap = a_fn(kc); kp_ap = kp_fn(kc)
                for g in range(NS // per):
                    ss = slice(g * per, (g + 1) * per)
                    MM(PB[3][:, 0:512], ones16[0:NS, :], BD[:, ss, 0:dv], True, True, ["ones16", "BD"], [PBn[3]])
                    TT(dve, T1[:].rearrange("p (s v) -> p s v", v=dv), PB[3][:, 0:512].rearrange("p (s v) -> p s v", v=dv),
                       kp_ap[:, ss].unsqueeze(2).to_broadcast([128, per, dv]), ALU.mult, [PBn[3], "KP32", "Td"], ["T1"])
                    TT(dve, Ssmp[:, ss, 0:dv], Ssmp[:, ss, 0:dv], a_ap[:, ss].unsqueeze(2).to_broadcast([128, per, dv]), ALU.mult,
                       ["Ssmp", "A32", "Tf", "SWB"], ["Ssmp"])
                    TT(dve, Ssmp[:, ss, 0:dv], Ssmp[:, ss, 0:dv], T1[:].rearrange("p (s v) -> p s v", v=dv), ALU.add, ["Ssmp", "T1"], ["Ssmp"])
                dma(sp, st_out[l, :, h, kc * 128:(kc + 1) * 128, :].rearrange("s k v -> k s v"), Ssmp[:, :, 0:dv], ["Ssmp"], [], osem("ssmp"))
                for s in range(NS):
                    for vc in range(nvc):
                        MM(PB[7][:, vc * 16 + s:vc * 16 + s + 1], Ssmp[:, s, vc * 128:(vc + 1) * 128], Q32[:, kc, TP + s:TP + s + 1],
                           kc == 0 and s == 0 and vc == 0, kc == nkc - 1, ["Ssmp", "Q32"], [PBn[7]], skip=True)

        def head_norm(kind, nvc, gain_chunk0, l, ycol0, n):
            fw.tag = fw.tag.split(".")[0] + ".norm"
            for vc in range(nvc):
                ACT(SQ32[:, vc, 0:n], H32[:, vc, 0:n], AF.Square, ["H32"], ["SQ32"])
            pieces = [(0, 512, 3)] + ([(512, n - 512, 2)] if n > 512 else [])
            for (c0, nn, bk) in pieces:
                for vc in range(nvc):
                    MM(PB[bk][:, 0:nn], ones32[:], SQ32[:, vc, c0:c0 + nn], vc == 0, vc == nvc - 1, ["ones32", "SQ32"], [PBn[bk]])
                CP(act, ST2[:, c0:c0 + nn], PB[bk][:, 0:nn], [PBn[bk]], ["ST2"])
                if kind == "ln":
                    for vc in range(nvc):
                        MM(PB[bk][:, 0:nn], ones32[:], H32[:, vc, c0:c0 + nn], vc == 0, vc == nvc - 1, ["ones32", "H32"], [PBn[bk]])
                    CP(act, ST1[:, c0:c0 + nn], PB[bk][:, 0:nn], [PBn[bk]], ["ST1"])
            dvv = float(nvc * 128)
            if kind == "ln":
                TS(dve, ST1[:, 0:n], ST1[:, 0:n], 1.0 / dvv, None, ALU.mult, None, ["ST1"], ["ST1"])
                TT(dve, SQH[:, 0:n], ST1[:, 0:n], ST1[:, 0:n], ALU.mult, ["ST1"], ["SQH"])
                STT(ST2[:, 0:n], ST2[:, 0:n], 1.0 / dvv, SQH[:, 0:n], ALU.mult, ALU.subtract, ["ST2", "SQH"], ["ST2"])
                ACT(ST2[:, 0:n], ST2[:, 0:n], AF.Ln, ["ST2", "epsn"], ["ST2"], scale=1.0, bias=epsn[:])
            else:
                ACT(ST2[:, 0:n], ST2[:, 0:n], AF.Ln, ["ST2", "epsn"], ["ST2"], scale=1.0 / dvv, bias=epsn[:])
            ACT(ST2[:, 0:n], ST2[:, 0:n], AF.Exp, ["ST2"], ["ST2"], scale=-0.5)
            for vc in range(nvc):
                g_col = PAR[:, gain_chunk0 + vc, l:l + 1]
                if kind == "ln":
                    TT(dve, SQ32[:, vc, 0:n], H32[:, vc, 0:n], ST1[:, 0:n], ALU.subtract, ["H32", "ST1"], ["SQ32"])
                    STT(SQ32[:, vc, 0:n], SQ32[:, vc, 0:n], g_col, ST2[:, 0:n], ALU.mult, ALU.mult, ["SQ32", "ST2", "PAR"], ["SQ32"])
                else:
                    STT(SQ32[:, vc, 0:n], H32[:, vc, 0:n], g_col, ST2[:, 0:n], ALU.mult, ALU.mult, ["H32", "ST2", "PAR"], ["SQ32"])
                TT(dve, yT[:, ycol0 + vc, 0:n], SQ32[:, vc, 0:n], Tz[:, vc, 0:n], ALU.mult, ["SQ32", "Tz"], ["U"])

        def rotary(dst32, dname, kc, n, src, sname_):
            for (c0, nn) in [(0, 512)] + ([(512, n - 512)] if n > 512 else []):
                b = nb()
                MM(PB[b][:, 0:nn], perm32[:], src[:, c0:c0 + nn], True, True, ["perm32", sname_], [PBn[b]])
                TT(dve, Tc[:, c0:c0 + nn], PB[b][:, 0:nn], sinT[:, kc, c0:c0 + nn], ALU.mult, [PBn[b], "sinT"], ["Tc"])
            TT(dve, Tb[:, 0:n], src[:, 0:n], cosT[:, kc, 0:n], ALU.mult, [sname_, "cosT"], ["Tb"])
            TT(dve, dst32, Tb[:, 0:n], Tc[:, 0:n], ALU.add, ["Tb", "Tc"], [dname])

        for l in range(L):
            par = l % 2
            for p in range(NP):
                first = (p == 0)
                last = (p == NP - 1)
                blocks = [((0, TP), "P")] + ([((TP, NS), "S")] if first else [])
                n_all = TP + (NS if first else 0)
                xres = [xfres(par, k) for k in range(16)]
                dma(pool, xT16[:, :, 0:TP], XF[par, :, :, p * TP:(p + 1) * TP].rearrange("k p t -> p k t"), xres, ["xT16"], s_x)
                dma(sp, cosT[:, :, 0:TP], c_cos[:, :, p * TP:(p + 1) * TP], [], ["cosT"], s_tab)
                dma(sp, sinT[:, :, 0:TP], c_sin[:, :, p * TP:(p + 1) * TP], [], ["sinT"], s_tab)
                if first:
                    dma(pool, xT16[:, :, TP:NW], XF[par, :, :, NTOK:NT].rearrange("k p t -> p k t"), xres, ["xT16"], s_x)
                    dma(sp, cosT[:, :, TP:NW], c_cos[:, :, NTOK:NT], [], ["cosT"], s_tab)
                    dma(sp, sinT[:, :, TP:NW], c_sin[:, :, NTOK:NT], [], ["sinT"], s_tab)
                    MSET(dve, S_h[:], 0.0, ["S_h"]); MSET(dve, S_r[:], 0.0, ["S_r"]); MSET(dve, S_c[:], 0.0, ["S_c"])
                    MSET(dve, Bc[:], 0.0, ["Bc"]); MSET(dve, Mc[:], 0.0, ["Mc"])
                fw.fence(s_tab, ["cosT", "sinT"])
                fw.fence(s_x, ["xT16"])

                fw.tag = "hgrn.x"
                for h in range(8):
                    wtA, wnA = load_w([w_in[l][:, sg * BW + h * 128: sg * BW + (h + 1) * 128] for sg in (0, 1)])
                    wtB, wnB = load_w([w_in[l][:, sg * BW + h * 128: sg * BW + (h + 1) * 128] for sg in (2, 3)])
                    for (blk, bt) in blocks:
                        lc0, n = blk
                        b = nb(); ps = proj_fm(b, wtA, wnA, 128, 128, blk)
                        ACT(Tf[:, lc0:lc0 + n], ps, AF.Sigmoid, [PBn[b]], ["Tf"])
                    TS(dve, Tf[:, 0:n_all], Tf[:, 0:n_all], oml[:, h, l:l + 1], lbv[:, h, l:l + 1], ALU.mult, ALU.add, ["Tf", "oml", "lbv"], ["Tf"])
                    ACT(Tg[:, 0:TP], Tf[:, 0:TP], AF.Ln, ["Tf"], ["Tg"])
                    for (blk, bt) in blocks:
                        lc0, n = blk
                        b = nb(); ps = proj_fm(b, wtA, wnA, 0, 128, blk)
                        ACT(Q32[:, 0, lc0:lc0 + n], ps, AF.Silu, [PBn[b]], ["Q32"])
                    SCAN(Tb[:, 0:TP], reset[:], Tg[:, 0:TP], 0.0, ALU.mult, ALU.add, ["reset", "Tg"], ["Tb"])
                    TS(dve, Td[:, 0:n_all], Tf[:, 0:n_all], -1.0, 1.0, ALU.mult, ALU.add, ["Tf"], ["Td"])
                    ACT(Te[:, 0:TP], Tb[:, 0:TP], AF.Exp, ["Tb"], ["Te"])
                    ACT(Tc[:, 0:TP], Tb[:, 0:TP], AF.Exp, ["Tb"], ["Tc"], scale=-1.0)
                    TT(dve, Q32[:, 0, 0:TP], Q32[:, 0, 0:TP], Te[:, 0:TP], ALU.mult, ["Q32", "Te"], ["Q32"])
                    TT(dve, K16[:, 0, :], Td[:, 0:TP], Tc[:, 0:TP], ALU.mult, ["Td", "Tc"], ["K16"])
                    TT(dve, KD16[:].rearrange("p (j t) -> p j t", t=64), K16[:, 0, :].rearrange("p (j t) -> p j t", t=64),
                       Te[:, 0:TP].rearrange("p (j t) -> p j t", t=64)[:, :, 63:64].to_broadcast([128, TP // 64, 64]), ALU.mult, ["K16", "Te"], ["KD16"])
                    CP(dve, Q16[:, 0, :], Q32[:, 0, 0:TP], ["Q32"], ["Q16"])
                    for (blk, bt) in blocks:
                        lc0, n = blk
                        b = nb(); ps = proj_fm(b, wtB, wnB, 128, 128, blk)
                        ACT(Tz[:, 0, lc0:lc0 + n], ps, AF.Silu, [PBn[b]], ["Tz"])
                    v_tokmajor(wtB, wnB, 0, 128, blocks)
                    kd_transposes(1, [(KD16, "KD16")], lambda i: None)

                    def hmask(j, rows, am, amn, c, pa, pan):
                        TT(dve, am[rows, 0:c], pa[rows, 0:c], tri64[rows, :], ALU.mult, [pan, "tri64"], [amn])
                    chunk_scan(64, 1, 1, S_h, "S_h", h, hmask, lambda j: Te[:, 64 * j + 63:64 * j + 64], False)
                    CP(act, H32[:, 0, 0:TP], PB[0][:, 0:TP], [PBn[0]], ["H32"])
                    if first:
                        sample_update(l, 128, 1, h, st_h, o_hs, lambda kc: Tf[:, TP:NW], lambda kc: Td[:, TP:NW], 1)
                        CP(act, H32[:, 0, TP:NW], PB[7][:, 0:NS], [PBn[7]], ["H32"])
                    head_norm("rms", 1, 8 + h, l, h, n_all)
                if last:
                    dma(sp, o_hp[l].rearrange("h k v -> k h v"), S_h[:], ["S_h"], [], osem("S_h"))

                fw.tag = "ret.x"
                for h in range(4):
                    for sgi, sgn in ((4, "q"), (5, "k")):
                        wt, wn = load_w([w_in[l][:, sgi * BW + h * 256: sgi * BW + (h + 1) * 256]])
                        stg = [(Ta, "Ta"), (Te, "Te")]
                        for kc in range(2):
                            for (blk, bt) in blocks:
                                lc0, n = blk
                                b = nb(); ps = proj_fm(b, wt, wn, kc * 128, 128, blk)
                                CP(act, stg[kc][0][:, lc0:lc0 + n], ps, [PBn[b]], [stg[kc][1]])
                        for kc in range(2):
                            if sgn == "q":
                                rotary(Q32[:, kc, 0:n_all], "Q32", kc, n_all, stg[kc][0], stg[kc][1])
                                CP(act, Q16[:, kc, :], Q32[:, kc, 0:TP], ["Q32"], ["Q16"])
                            else:
                                rotary(Td[:, 0:n_all], "Td", kc, n_all, stg[kc][0], stg[kc][1])
                                CP(act, K16[:, kc, :], Td[:, 0:TP], ["Td"], ["K16"])
                                if first:
                                    TS(dve, KP32[:, kc, :], Td[:, TP:NW], 1.0 / 16.0, None, ALU.mult, None, ["Td"], ["KP32"])
                    wt, wn = load_w([w_in[l][:, 6 * BW + h * 256: 6 * BW + (h + 1) * 256]])
                    v_tokmajor(wt, wn, 0, 256, blocks)
                    wt, wn = load_w([w_in[l][:, 7 * BW + h * 256: 7 * BW + (h + 1) * 256]])
                    for vc in range(2):
                        for (blk, bt) in blocks:
                            lc0, n = blk
                            b = nb(); ps = proj_fm(b, wt, wn, vc * 128, 128, blk)
                            ACT(Tz[:, vc, lc0:lc0 + n], ps, AF.Silu, [PBn[b]], ["Tz"])
                    kd_transposes(2, [(K16[:, 0, :], "K16"), (K16[:, 1, :], "K16")], lambda i: rkd[:, h:h + 1])

                    def rmaskf(j, rows, am, amn, c, pa, pan):
                        TT(dve, am[rows, 0:c], pa[rows, 0:c], rmask[:, h, :], ALU.mult, [pan, "rmask"], [amn])
                    ga = float(GAM[h] ** 128)
                    chunk_scan(128, 2, 2, S_r, "S_r", h, rmaskf, lambda j: ga, False)
                    for vc in range(2):
                        TT(dve, H32[:, vc, 0:TP].rearrange("p (j t) -> p j t", t=128), PB[vc][:, 0:TP].rearrange("p (j t) -> p j t", t=128),
                           rG[:, h, :].unsqueeze(1).to_broadcast([128, NCH, 128]), ALU.mult, [PBn[vc], "rG"], ["H32"])
                    if first:
                        MSET(dve, A32[:], float(GAM[h]), ["A32"])
                        sample_update(l, 256, 2, h, st_r, o_rs, lambda kc: A32, lambda kc: KP32[:, kc, :], 2)
                        for vc in range(2):
                            CP(act, H32[:, vc, TP:NW], PB[7][:, vc * 16:vc * 16 + NS], [PBn[7]], ["H32"])
                    head_norm("ln", 2, 16 + 2 * h, l, 8 + 2 * h, n_all)
                if last:
                    dma(sp, o_rp[l].rearrange("h (kc k) v -> k (h kc) v", k=128), S_r[:], ["S_r"], [], osem("S_r"))

                fw.tag = "mlstm.x"
                wt, wn = load_w([w_in[l][:, 13312:13320]])
                for (blk, bt) in blocks:
                    lc0, n = blk
                    b = nb(); ps = proj_fm(b, wt, wn, 0, 4, blk)
                    ACT(GI[:, lc0:lc0 + n], ps, AF.Identity, [PBn[b], "BIF"], ["Ta"], bias=BIF[:, 0, l:l + 1])
                    b = nb(); ps = proj_fm(b, wt, wn, 4, 4, blk)
                    ACT(GLF[:, lc0:lc0 + n], ps, AF.Sigmoid, [PBn[b], "BIF"], ["Tb"], bias=BIF[:, 1, l:l + 1])
                ACT(GLF[:, 0:n_all], GLF[:, 0:n_all], AF.Ln, ["Tb"], ["Tb"])
                SCAN(GB[:], onesrow[:], GLF[:, 0:TP], Bc[:, 0:1], ALU.mult, ALU.add, ["onesrow", "Tb", "Bc"], ["Tc"])
                TT(dve, GU[:], GI[:, 0:TP], GB[:], ALU.subtract, ["Ta", "Tc"], ["Td"])
                SCAN(GM[:], onesrow[:], GU[:], Mc[:, 0:1], ALU.mult, ALU.max, ["onesrow", "Td", "Mc"], ["Te"])
                GMv = GM[:].rearrange("p (j t) -> p j t", t=128)
                CP(dve, GMS[:, 0:1], Mc[:], ["Mc"], ["GMS"])
                if NCH > 1:
                    CP(dve, GMS[:, 1:NCH].unsqueeze(2), GMv[:, 0:NCH - 1, 127:128], ["Te"], ["GMS"])
                TT(dve, GE1[:].rearrange("p (j t) -> p j t", t=128), GU[:].rearrange("p (j t) -> p j t", t=128),
                   GMS[:].unsqueeze(2).to_broadcast([4, NCH, 128]), ALU.subtract, ["Td", "GMS"], ["Tf"])
                ACT(GE1[:], GE1[:], AF.Exp, ["Tf"], ["Tf"])
                TT(dve, GE2[:].rearrange("p (j t) -> p j t", t=128), GU[:].rearrange("p (j t) -> p j t", t=128),
                   GMv[:, :, 127:128].to_broadcast([4, NCH, 128]), ALU.subtract, ["Td", "Te"], ["Tg"])
                ACT(GE2[:], GE2[:], AF.Exp, ["Tg"], ["Tg"])
                TS(dve, GE2[:], GE2[:], 1.0 / 16.0, None, ALU.mult, None, ["Tg"], ["Tg"])
                TT(dve, GT[:].rearrange("p (j t) -> p j t", t=128), GB[:].rearrange("p (j t) -> p j t", t=128),
                   GMS[:].unsqueeze(2).to_broadcast([4, NCH, 128]), ALU.add, ["Tc", "GMS"], ["Tz"])
                ACT(GT[:], GT[:], AF.Exp, ["Tz"], ["Tz"], scale=-1.0)
                TT(dve, GDEC[:].unsqueeze(2), GMS[:].unsqueeze(2), GMv[:, :, 127:128], ALU.subtract, ["GMS", "Te"], ["GDEC"])
                ACT(GDEC[:], GDEC[:], AF.Exp, ["GDEC"], ["GDEC"])
                CP(dve, Bc[:], GB[:, TP - 1:TP], ["Tc"], ["Bc"])
                CP(dve, Mc[:], GM[:, TP - 1:TP], ["Te"], ["Mc"])
                TT(dve, Mfin[:], GB[:, TP - 1:TP], GM[:, TP - 1:TP], ALU.add, ["Tc", "Te"], ["Mfin"])
                for j in range(NCH):
                    TR(PB[4][:, 0:4], GE1[:, j * 128:(j + 1) * 128], ident32[0:4, 0:4], ["Tf", "ident32"], [PBn[4]])
                    TR(PB[4][:, 4:8], GE2[:, j * 128:(j + 1) * 128], ident32[0:4, 0:4], ["Tg", "ident32"], [PBn[4]])
                    CP(act, ET[:, j, :], PB[4][:, 0:8], [PBn[4]], ["ET"])
                GT2 = sb_gt2
                CP(dve, GT2[:], GT[:], ["Tz"], ["GT2"])
                for h in range(4):
                    MM(PB[2][:, h * NCH:(h + 1) * NCH], sel[:, h, :], GDEC[:], True, True, ["sel", "GDEC"], [PBn[2]])
                CP(act, DECB[:].rearrange("p h j -> p (h j)"), PB[2][:, 0:4 * NCH], [PBn[2]], ["DECB"])
                if first:
                    dma(sp, SMS[:], st_m[l].rearrange("s h -> h s"), [], ["SMS"], osem("sms_in"), slow=True)
                    dma(sp, NS32[:], st_n[l], [], ["Ssmp"], osem("ns_in"))
                    TT(dve, SLI[:], SMS[:], GLF[:, TP:NW], ALU.add, ["SMS", "Tb"], ["SLI"])
                    TT(dve, SMm[:], SLI[:], GI[:, TP:NW], ALU.max, ["SLI", "Ta"], ["SMm"])
                    TT(dve, SW[:, 0, :], GI[:, TP:NW], SMm[:], ALU.subtract, ["Ta", "SMm"], ["SW"])
                    TT(dve, SW[:, 1, :], SLI[:], SMm[:], ALU.subtract, ["SLI", "SMm"], ["SW"])
                    TS(dve, SW[:, 2, :], SMm[:], -1.0, None, ALU.mult, None, ["SMm"], ["SW"])
                    ACT(SW[:], SW[:], AF.Exp, ["SW"], ["SW"])
                    TS(dve, SW[:, 0, :], SW[:, 0, :], 1.0 / 16.0, None, ALU.mult, None, ["SW"], ["SW"])
                    dma(sp, o_ms[l].rearrange("s h -> h s"), SMm[:], ["SMm"], [], osem("smm"), slow=True)
                    for h in range(4):
                        MM(PB[3][:, h * 48:(h + 1) * 48], sel[:, h, :], SW[:].rearrange("p a s -> p (a s)"), True, True, ["sel", "SW"], [PBn[3]])
                    CP(act, SWB[:].rearrange("p h a s -> p (h a s)"), PB[3][:, 0:192], [PBn[3]], ["SWB"])
                    for c in range(8):
                        TR(PB[4][:, 128 + c * NS:128 + (c + 1) * NS], NS32[0:NS, c * 128:(c + 1) * 128], ident32[0:NS, 0:NS], ["Ssmp", "ident32"], [PBn[4]])
                    CP(act, NT32[:].rearrange("p c s -> p (c s)"), PB[4][:, 128:128 + 8 * NS], [PBn[4]], ["NT32"])
                for h in range(4):
                    wt, wn = load_w([w_in[l][:, 8 * BW + h * 256: 8 * BW + (h + 1) * 256]])
                    for kc in range(2):
                        for (blk, bt) in blocks:
                            lc0, n = blk
                            b = nb(); ps = proj_fm(b, wt, wn, kc * 128, 128, blk)
                            CP(act, Q32[:, kc, lc0:lc0 + n], ps, [PBn[b]], ["Q32"])
                        CP(act, Q16[:, kc, :], Q32[:, kc, 0:TP], ["Q32"], ["Q16"])
                    wt, wn = load_w([w_in[l][:, 9 * BW + h * 256: 9 * BW + (h + 1) * 256]])
                    for kc in range(2):
                        for (blk, bt) in blocks:
                            lc0, n = blk
                            b = nb(); ps = proj_fm(b, wt, wn, kc * 128, 128, blk)
                            if bt == "P":
                                CP(act, K16[:, kc, :], ps, [PBn[b]], ["K16"])
                            else:
                                TT(dve, KP32[:, kc, :], ps, SWB[:, h, 0, :], ALU.mult, [PBn[b], "SWB"], ["KP32"])
                    wt, wn = load_w([w_in[l][:, 10 * BW + h * 256: 10 * BW + (h + 1) * 256]])
                    v_tokmajor(wt, wn, 0, 256, blocks)
                    for sgi, dst, fn_, dn in ((11, Tz, AF.Silu, "Tz"), (12, To, AF.Sigmoid, "To")):
                        wt, wn = load_w([w_in[l][:, sgi * BW + h * 256: sgi * BW + (h + 1) * 256]])
                        for vc in range(2):
                            for (blk, bt) in blocks:
                                lc0, n = blk
                                b = nb(); ps = proj_fm(b, wt, wn, vc * 128, 128, blk)
                                ACT(dst[:, vc, lc0:lc0 + n], ps, fn_, [PBn[b]], [dn])
                    kd_transposes(2, [(K16[:, 0, :], "K16"), (K16[:, 1, :], "K16")], lambda i: ET[:, i, 4 + h:5 + h])

                    def mmask(j, rows, am, amn, c, pa, pan):
                        STT(am[rows, 0:c], pa[rows, 0:c], ET[:, j, h:h + 1], tri128[:], ALU.mult, ALU.mult, [pan, "ET", "tri128"], [amn])
                    chunk_scan(128, 2, 2, S_c, "S_c", h, mmask, lambda j: DECB[:, h, j:j + 1], True)
                    MM(PB[3][:, 0:TP], sel[:, h, :], GT2[:], True, True, ["sel", "GT2"], [PBn[3]])
                    CP(act, THRB[:], PB[3][:, 0:TP], [PBn[3]], ["THRB"])
                    ACT(Te[:, 0:TP], PB[2][:, 0:TP], AF.Abs, [PBn[2]], ["Te"])
                    TT(dve, Te[:, 0:TP], Te[:, 0:TP], THRB[:], ALU.max, ["Te", "THRB"], ["Te"])
                    RECIP(Te[:, 0:TP], Te[:, 0:TP], ["Te"], ["Te"])
                    for vc in range(2):
                        TT(dve, H32[:, vc, 0:TP], PB[vc][:, 0:TP], Te[:, 0:TP], ALU.mult, [PBn[vc], "Te"], ["H32"])
                    if first:
                        for kc in range(2):
                            c8 = h * 2 + kc
                            TT(dve, NT32[:, c8, :], NT32[:, c8, :], SWB[:, h, 1, :], ALU.mult, ["NT32", "SWB"], ["NT32"])
                            TT(dve, NT32[:, c8, :], NT32[:, c8, :], KP32[:, kc, :], ALU.add, ["NT32", "KP32"], ["NT32"])
                            TT(dve, Ta[:, kc * NS:(kc + 1) * NS], Q32[:, kc, TP:NW], NT32[:, c8, :], ALU.mult, ["Q32", "NT32"], ["Ta"])
                        MM(PB[7][:, 32:48], ones32[:], Ta[:, 0:NS], True, False, ["ones32", "Ta"], [PBn[7]])
                        MM(PB[7][:, 32:48], ones32[:], Ta[:, NS:2 * NS], False, True, ["ones32", "Ta"], [PBn[7]])
                        sample_update(l, 256, 2, h, st_c, o_cs, lambda kc: SWB[:, h, 1, :], lambda kc: KP32[:, kc, :], 2)
                        ACT(Tc[:, 0:NS], PB[7][:, 32:48], AF.Abs, [PBn[7]], ["Tc"])
                        TT(dve, Tc[:, 0:NS], Tc[:, 0:NS], SWB[:, h, 2, :], ALU.max, ["Tc", "SWB"], ["Tc"])
                        RECIP(Tc[:, 0:NS], Tc[:, 0:NS], ["Tc"], ["Tc"])
                        for vc in range(2):
                            TT(dve, H32[:, vc, TP:NW], PB[7][:, vc * 16:vc * 16 + NS], Tc[:, 0:NS], ALU.mult, [PBn[7], "Tc"], ["H32"])
                    for vc in range(2):
                        TT(dve, H32[:, vc, 0:n_all], H32[:, vc, 0:n_all], To[:, vc, 0:n_all], ALU.mult, ["H32", "To"], ["H32"])
                    head_norm("ln", 2, 24 + 2 * h, l, 16 + 2 * h, n_all)
                if first:
                    for c in range(8):
                        TR(PB[5 + c // 4][0:NS, (c % 4) * 128:(c % 4 + 1) * 128], NT32[:, c, :], ident32[:], ["NT32", "ident32"], [PBn[5 + c // 4]])
                    CP(act, NO32[:, 0:512], PB[5][0:NS, :], [PBn[5]], ["Ssmp"])
                    CP(act, NO32[:, 512:1024], PB[6][0:NS, :], [PBn[6]], ["Ssmp"])
                    dma(sp, o_ns[l], NO32[:], ["Ssmp"], [], osem("no32"))
                if last:
                    dma(sp, o_cp[l].rearrange("h (kc k) v -> k (h kc) v", k=128), S_c[:, :, 0:256], ["S_c"], [], osem("S_c"))
                    dma(sp, o_np[l].rearrange("h (kc k) -> k (h kc)", k=128), S_c[:, :, 256], ["S_c"], [], osem("S_c"), slow=True)
                    dma(sp, o_mp[l].rearrange("(h o) -> h o", o=1), Mfin[:], ["Mfin"], [], osem("mfin"), slow=True)

                fw.tag = "merge"
                pieces = [(0, 512)] + ([(512, NS)] if first else [])
                S16f = Sflat.bitcast(BF16)
                WGu = [S16f[:, 0:2048], S16f[:, 3072:5120], BDs[:, 0:2048]]
                WBu = [S16f[:, 2048:3072], S16f[:, 5120:6144], BDs[:, 2048:3072]]
                WGu = [a.rearrange("p (k d) -> p k d", d=128) for a in WGu]
                WBu = [a.rearrange("p (k d) -> p k d", d=128) for a in WBu]
                units = [(dc, nbr) for dc in range(16) for nbr in range(3)]

                def load_unit(i):
                    dc, nbr = units[i]
                    u = i % 3
                    extra = (["Ssmp"] if u < 2 else ["BD"]) if i < 3 else []
                    d0 = dc * 128
                    dma(pool, WGu[u], w_in[l][:, GATE0 + nbr * D + d0: GATE0 + nbr * D + d0 + 128].rearrange("(k p) c -> p k c", p=128), [], [f"MU{u}"] + extra, s_mu[u])
                    dma(pool, WBu[u], w_br[l, nbr][:, d0:d0 + 128].rearrange("(k p) c -> p k c", p=128), [], [f"MU{u}"] + extra, s_mu[u])
                load_unit(0); load_unit(1)
                for i, (dc, nbr) in enumerate(units):
                    u = i % 3
                    if i + 2 < len(units):
                        load_unit(i + 2)
                    for (c0, nn) in pieces:
                        b = nb()
                        for kc in range(16):
                            MM(PB[b][:, 0:nn], WGu[u][:, kc, :], xT16[:, kc, c0:c0 + nn], kc == 0, kc == 15, [f"MU{u}", "xT16"], [PBn[b]])
                        ACT(MS32[:, c0:c0 + nn], PB[b][:, 0:nn], AF.Sigmoid, [PBn[b]], ["Ta"])
                        b = nb()
                        for cc in range(8):
                            MM(PB[b][:, 0:nn], WBu[u][:, cc, :], yT[:, nbr * 8 + cc, c0:c0 + nn], cc == 0, cc == 7, [f"MU{u}", "U"], [PBn[b]])
                        if nbr == 0:
                            TT(dve, MA32[:, c0:c0 + nn], MS32[:, c0:c0 + nn], PB[b][:, 0:nn], ALU.mult, ["Ta", PBn[b]], ["Tb"])
                        else:
                            TT(dve, MB32[:, c0:c0 + nn], MS32[:, c0:c0 + nn], PB[b][:, 0:nn], ALU.mult, ["Ta", PBn[b]], ["Tc"])
                            if nbr == 1:
                                TT(dve, MA32[:, c0:c0 + nn], MA32[:, c0:c0 + nn], MB32[:, c0:c0 + nn], ALU.add, ["Tb", "Tc"], ["Tb"])
                            else:
                                TT(dve, mg16[:, dc, c0:c0 + nn], MA32[:, c0:c0 + nn], MB32[:, c0:c0 + nn], ALU.add, ["Tb", "Tc"], MGALL)
                for u, base in ((0, "Ssmp"), (1, "Ssmp"), (2, "BD")):
                    Rb = fw.R(base); Rm = fw.R(f"MU{u}")
                    for s_, v_ in Rm.r.items():
                        Rb.r[s_] = max(Rb.r.get(s_, 0), v_)
                    if Rm.w is not None:
                        Rb.r[Rm.w[0]] = max(Rb.r.get(Rm.w[0], 0), Rm.w[1])
                fw.tag = "out"
                for dc in range(16):
                    d0 = dc * 128
                    wt, wn = load_w([w_o[l][:, d0:d0 + 128]])
                    xi = dc % 2
                    dma(sp, XR[xi][:, 0:TP], XF[par, dc, :, p * TP:(p + 1) * TP], [xfres(par, dc)], [f"XR{xi}"], s_xr[xi])
                    if first:
                        dma(sp, XR[xi][:, TP:NW], XF[par, dc, :, NTOK:NT], [xfres(par, dc)], [f"XR{xi}"], s_xr[xi])
                    for (c0, nn) in pieces:
                        b = nb()
                        for mc in range(16):
                            MM(PB[b][:, 0:nn], wt[:, mc, 0:128], mg16[:, mc, c0:c0 + nn], mc == 0, mc == 15, [wn] + MGALL, [PBn[b]])
                        STT(Hbuf[:, dc, c0:c0 + nn], XR[xi][:, c0:c0 + nn], ALPHA, PB[b][:, 0:nn], ALU.mult, ALU.add, [f"XR{xi}", PBn[b]], ["U"])
                    ACT(SQH[:, 0:n_all], Hbuf[:, dc, 0:n_all], AF.Square, ["U"], ["SQH"])
                    MM(PB[2][:, 0:512], ones32[:], Hbuf[:, dc, 0:512], dc == 0, dc == 15, ["ones32", "U"], [PBn[2]])
                    MM(PB[3][:, 0:512], ones32[:], SQH[:, 0:512], dc == 0, dc == 15, ["ones32", "SQH"], [PBn[3]])
                    if first:
                        MM(PB[6][:, 0:NS], ones32[:], Hbuf[:, dc, TP:NW], dc == 0, dc == 15, ["ones32", "U"], [PBn[6]])
                        MM(PB[7][:, 0:NS], ones32[:], SQH[:, TP:NW], dc == 0, dc == 15, ["ones32", "SQH"], [PBn[7]])
                CP(act, ST1[:, 0:512], PB[2][:, 0:512], [PBn[2]], ["ST1"])
                CP(act, ST2[:, 0:512], PB[3][:, 0:512], [PBn[3]], ["ST2"])
                if first:
                    CP(act, ST1[:, TP:NW], PB[6][:, 0:NS], [PBn[6]], ["ST1"])
                    CP(act, ST2[:, TP:NW], PB[7][:, 0:NS], [PBn[7]], ["ST2"])
                n = n_all
                TS(dve, ST1[:, 0:n], ST1[:, 0:n], 1.0 / D, None, ALU.mult, None, ["ST1"], ["ST1"])
                TT(dve, SQH[:, 0:n], ST1[:, 0:n], ST1[:, 0:n], ALU.mult, ["ST1"], ["SQH"])
                STT(ST2[:, 0:n], ST2[:, 0:n], 1.0 / D, SQH[:, 0:n], ALU.mult, ALU.subtract, ["ST2", "SQH"], ["ST2"])
                ACT(ST2[:, 0:n], ST2[:, 0:n], AF.Ln, ["ST2", "epsl"], ["ST2"], scale=1.0, bias=epsl[:])
                ACT(ST2[:, 0:n], ST2[:, 0:n], AF.Exp, ["ST2"], ["ST2"], scale=-0.5)
                for dc in range(16):
                    xi = dc % 2
                    TT(dve, MA32[:, 0:n], Hbuf[:, dc, 0:n], ST1[:, 0:n], ALU.subtract, ["U", "ST1"], ["Tb"])
                    TT(dve, MB32[:, 0:n], MA32[:, 0:n], ST2[:, 0:n], ALU.mult, ["Tb", "ST2"], ["Tc"])
                    TS(dve, XR[xi][:, 0:n], MB32[:, 0:n], PAR[:, 32 + dc, l:l + 1], PAR[:, 48 + dc, l:l + 1], ALU.mult, ALU.add, ["Tc", "PAR"], [f"XR{xi}"])
                    dma(sp, XF[1 - par, dc, :, p * TP:(p + 1) * TP], XR[xi][:, 0:TP], [f"XR{xi}"], [xfres(1 - par, dc)], s_xw[xi])
                    if first:
                        dma(sp, XF[1 - par, dc, :, NTOK:NT], XR[xi][:, TP:NW], [f"XR{xi}"], [xfres(1 - par, dc)], s_xw[xi])

        fw.tag = "final"
        fp_ = L % 2
        otiles = [(yp[i * 128:(i + 1) * 128, :], 128, i * 128) for i in range(NTOK // 128)] + [(ys, NS, NTOK)]
        for (dst, nr, c0) in otiles:
            dma(sp, XS[:, :, 0:nr], XF[fp_, :, :, c0:c0 + nr].rearrange("k p t -> p k t"), [xfres(fp_, k) for k in range(16)], ["U"], osem("xs_in"))
            for g in range(4):
                for j in range(4):
                    kc = g * 4 + j
                    TR(PB[g % 2][0:nr, j * 128:(j + 1) * 128], XS[:, kc, 0:nr], ident32[:], ["U", "ident32"], [PBn[g % 2]])
                CP(act if g % 2 else dve, XIN[0:nr, g * 512:(g + 1) * 512], PB[g % 2][0:nr, :], [PBn[g % 2]], ["U"])
            dma(sp, dst, XIN[0:nr, :], ["U"], [], osem("xin_out"))
        for s in osems.values():
            nc.sync.wait_ge(s.h, s.v)
        for s in s_xw:
            nc.sync.wait_ge(s.h, s.v)
        print("ops", fw.nops)
        nc._petags = fw.petags
    return nc


def make_consts(NP, seq_b_pos0=0):
    NTOK = NP * TP
    NT = NTOK + NS
    c = {}
    c["c_id"] = np.eye(128, dtype=np.float32)
    perm = np.zeros((128, 128), np.float32)
    for k in range(128):
        perm[k, k ^ 1] = 1.0
    c["c_perm"] = perm
    s = np.arange(128)
    c["c_tri64"] = ((s[:, None] % 64) <= np.arange(64)[None, :]).astype(np.float32)
    tri = (s[:, None] <= s[None, :]).astype(np.float32)
    c["c_tri128"] = (tri / 16.0).astype(np.float32)
    gam = np.array([1.0 - 2.0 ** (-5.0 - h) for h in range(4)], np.float64)
    rmask = np.zeros((128, 4, 128), np.float64); rG = np.zeros((128, 4, 128), np.float64); rkd = np.zeros((128, 4), np.float64)
    for h in range(4):
        rmask[:, h, :] = tri * (gam[h] ** (-(s[:, None] + 1.0))) / 16.0
        rG[:, h, :] = (gam[h] ** (s[None, :] + 1.0))
        rkd[:, h] = gam[h] ** (127.0 - s) / 16.0
    c["c_rmask"] = rmask.astype(np.float32); c["c_rG"] = rG.astype(np.float32); c["c_rkd"] = rkd.astype(np.float32)
    rs = np.ones((128, TP), np.float32); rs[:, 0::64] = 0.0
    c["c_reset"] = rs
    pos = np.concatenate([np.arange(NTOK, dtype=np.float32), np.full((NS,), float(PAST), np.float32)])
    theta = (1.0 / (10000.0 ** np.linspace(0.0, 1.0, 128, dtype=np.float32))).astype(np.float32)
    ang = (pos[:, None] * theta[None, :]).astype(np.float32)
    cosv = np.cos(ang).astype(np.float32); sinv = np.sin(ang).astype(np.float32)
    cos_t = np.zeros((128, 2, NT), np.float32); sin_t = np.zeros((128, 2, NT), np.float32)
    for kc in range(2):
        for f in range(128):
            jf = (kc * 128 + f) // 2
            cos_t[f, kc, :] = cosv[:, jf]
            sin_t[f, kc, :] = sinv[:, jf] * (-1.0 if f % 2 == 0 else 1.0)
    c["c_cos"] = cos_t; c["c_sin"] = sin_t
    sel = np.zeros((4, 4, 128), np.float32)
    for h in range(4):
        sel[h, h, :] = 1.0
    c["c_sel"] = sel
    c["c_i16"] = np.eye(16, dtype=np.float32)
    return c


_cache = {}


def prep(L, NP, NB, ncores, x_prompt, x_sample, state_hgrn, state_ret, state_mlstm_c, state_mlstm_n, state_mlstm_m,
         w_in, hgrn_lb_logits, hgrn_norm, ret_norm, mlstm_norm, mlstm_b_i, mlstm_b_f, w_branch, w_out, ln_g, ln_b):
    f = lambda a: np.ascontiguousarray(np.asarray(a, dtype=np.float32))
    consts = make_consts(NP)
    pvec = f(np.concatenate([hgrn_lb_logits, hgrn_norm, ret_norm, mlstm_norm, ln_g, ln_b], axis=1))
    bif = f(np.stack([np.asarray(mlstm_b_i).T, np.asarray(mlstm_b_f).T], axis=1))
    w_in = f(w_in); w_branch = f(w_branch); w_out = f(w_out)
    in_maps = []
    nsb = np.asarray(x_sample).shape[0]
    for c in range(ncores):
        b = c % NB
        s0 = (c * NS) % nsb
        sl = slice(s0, s0 + NS)
        m = dict(consts)
        m.update({
            "xp": f(x_prompt[b]), "xs": f(np.asarray(x_sample)[sl, 0, :]),
            "st_h": f(np.asarray(state_hgrn)[:, sl]), "st_r": f(np.asarray(state_ret)[:, sl]), "st_c": f(np.asarray(state_mlstm_c)[:, sl]),
            "st_n": f(np.asarray(state_mlstm_n)[:, sl].reshape(L, NS, 1024)), "st_m": f(np.asarray(state_mlstm_m)[:, sl]),
            "w_in": w_in, "w_br": w_branch, "w_o": w_out, "pvec": pvec, "bif": bif,
        })
        in_maps.append(m)
    return in_maps


def gather(R, L, NB, nsb):
    nsc = nsb // NS
    y_prompt = np.stack([R[b]["yp"] for b in range(NB)], axis=0)
    y_sample = np.concatenate([R[c]["ys"] for c in range(nsc)], axis=0)[:, None, :]
    hg_p = np.stack([R[b]["o_hp"] for b in range(NB)], axis=1)
    ret_p = np.stack([R[b]["o_rp"] for b in range(NB)], axis=1)
    mc_p = np.stack([R[b]["o_cp"] for b in range(NB)], axis=1)
    mn_p = np.stack([R[b]["o_np"] for b in range(NB)], axis=1)
    mm_p = np.stack([R[b]["o_mp"] for b in range(NB)], axis=1)
    hg_s = np.concatenate([R[c]["o_hs"] for c in range(nsc)], axis=1)
    ret_s = np.concatenate([R[c]["o_rs"] for c in range(nsc)], axis=1)
    mc_s = np.concatenate([R[c]["o_cs"] for c in range(nsc)], axis=1)
    mn_s = np.concatenate([R[c]["o_ns"] for c in range(nsc)], axis=1).reshape(L, nsb, 4, 256)
    mm_s = np.concatenate([R[c]["o_ms"] for c in range(nsc)], axis=1)
    return (y_prompt, y_sample, hg_p, ret_p, mc_p, mn_p, mm_p, hg_s, ret_s, mc_s, mn_s, mm_s)


def kernel(**inputs):
    L, NP, NB = 4, 4, 4
    key = (L, NP)
    if key not in _cache:
        _cache[key] = build(L, NP)
    in_maps = prep(L, NP, NB, 8, **inputs)
    res = run_bass_kernel_spmd(_cache[key], in_maps, core_ids=list(range(8)))
    return gather(res.results, L, NB, 128)
```
